# Optimizing a Trainium2 kernel written in Bass

```python
import math
import jax, jax.numpy as jnp
from jax import lax
import numpy as np

D_MODEL = 1024
BATCH = 8
SEQ = 2048
DEPTH = 4

CHUNK = 64
N_MIXERS = 3
N_LAYERS_A = len(range(0, DEPTH, N_MIXERS))
N_LAYERS_B = len(range(1, DEPTH, N_MIXERS))
N_LAYERS_C = len(range(2, DEPTH, N_MIXERS))
RMS_EPS = 1e-6
D_FF = 4 * D_MODEL
CONV_W = 4

GDN_DK = 128
GDN_DV = 128
GDN_HEADS = D_MODEL // GDN_DK
GDN_QK = GDN_HEADS * GDN_DK
GDN_V = GDN_HEADS * GDN_DV
GDN_CONV_CH = 2 * GDN_QK + GDN_V
GDN_IN = GDN_CONV_CH + GDN_V + 2 * GDN_HEADS

S5_GROUP = 16
S5_GROUPS = D_MODEL // S5_GROUP
S5_STATE = 64

M2_INNER = 2 * D_MODEL
M2_HEAD_DIM = 64
M2_HEADS = M2_INNER // M2_HEAD_DIM
M2_GROUPS = 8
M2_HPG = M2_HEADS // M2_GROUPS
M2_STATE = 128
M2_BC = M2_GROUPS * M2_STATE
M2_CONV_CH = M2_INNER + 2 * M2_BC
M2_IN = M2_INNER + M2_CONV_CH + M2_HEADS

kernel_name = "hybrid_gdn_s5_ssd_trunk"


def rmsnorm(x, g):
    xf = x.astype(jnp.float32)
    y = xf * lax.rsqrt(jnp.mean(xf * xf, axis=-1, keepdims=True) + RMS_EPS)
    return (y * g.astype(jnp.float32)).astype(x.dtype)


def causal_dwconv(x, w):
    return lax.conv_general_dilated(
        x, w[:, None, :], window_strides=(1,), padding=[(w.shape[0] - 1, 0)],
        dimension_numbers=("NWC", "WIO", "NWC"), feature_group_count=x.shape[-1])


def l2norm(t):
    return t * lax.rsqrt(jnp.sum(t * t, axis=-1, keepdims=True) + 1e-6)


def gated_deltanet(h, w_in, conv_w, a_log, dt_bias, o_norm_g, w_out):
    bsz, L, _ = h.shape
    nc = L // CHUNK
    f32 = jnp.float32
    proj = h @ w_in
    qkv, gate, a_raw, b_raw = jnp.split(
        proj, [GDN_CONV_CH, GDN_CONV_CH + GDN_V, GDN_CONV_CH + GDN_V + GDN_HEADS], axis=-1)
    qkv = jax.nn.silu(causal_dwconv(qkv, conv_w)).astype(f32)
    q, k, v = jnp.split(qkv, [GDN_QK, 2 * GDN_QK], axis=-1)
    q = l2norm(q.reshape(bsz, L, GDN_HEADS, GDN_DK)) * (GDN_DK ** -0.5)
    k = l2norm(k.reshape(bsz, L, GDN_HEADS, GDN_DK))
    v = v.reshape(bsz, L, GDN_HEADS, GDN_DV)
    g = -jnp.exp(a_log.astype(f32)) * jax.nn.softplus(a_raw.astype(f32) + dt_bias.astype(f32))
    beta = jax.nn.sigmoid(b_raw.astype(f32))

    def to_chunks(t):
        return t.reshape(bsz, nc, CHUNK, GDN_HEADS, -1).transpose(0, 3, 1, 2, 4)

    qc, kc, vc = to_chunks(q), to_chunks(k), to_chunks(v)
    gc = g.reshape(bsz, nc, CHUNK, GDN_HEADS).transpose(0, 3, 1, 2)
    bc = beta.reshape(bsz, nc, CHUNK, GDN_HEADS).transpose(0, 3, 1, 2)
    G = jnp.cumsum(gc, axis=-1)
    idx = jnp.arange(CHUNK)
    causal = idx[:, None] >= idx[None, :]
    strict = idx[:, None] > idx[None, :]
    decay = jnp.exp(jnp.where(causal, G[..., :, None] - G[..., None, :], -jnp.inf))
    kk = jnp.einsum('bhcid,bhcjd->bhcij', kc, kc)
    tri = jnp.where(strict, bc[..., :, None] * kk * decay, 0.0) + jnp.eye(CHUNK, dtype=f32)
    rhs = jnp.concatenate([vc * bc[..., None], kc * (bc * jnp.exp(G))[..., None]], axis=-1)
    sol = lax.linalg.triangular_solve(tri, rhs, left_side=True, lower=True, unit_diagonal=True)
    u, w = sol[..., :GDN_DV], sol[..., GDN_DV:]
    qk = jnp.einsum('bhcid,bhcjd->bhcij', qc, kc) * decay
    q_dec = qc * jnp.exp(G)[..., None]
    k_dec = kc * jnp.exp(G[..., -1:] - G)[..., None]
    chunk_decay = jnp.exp(G[..., -1])
    xs = tuple(jnp.moveaxis(t, 2, 0) for t in (u, w, qk, q_dec, k_dec, chunk_decay))

    def step(S, inp):
        u_c, w_c, qk_c, qd_c, kd_c, cd_c = inp
        v_new = u_c - jnp.einsum('bhid,bhde->bhie', w_c, S)
        o = jnp.einsum('bhid,bhde->bhie', qd_c, S) + jnp.einsum('bhij,bhje->bhie', qk_c, v_new)
        S = cd_c[..., None, None] * S + jnp.einsum('bhid,bhie->bhde', kd_c, v_new)
        return S, o

    S0 = jnp.zeros((bsz, GDN_HEADS, GDN_DK, GDN_DV), f32)
    _, o = lax.scan(step, S0, xs)
    o = o.transpose(1, 0, 3, 2, 4).reshape(bsz, L, GDN_HEADS, GDN_DV)
    o = rmsnorm(o, o_norm_g) * jax.nn.silu(gate.astype(f32).reshape(bsz, L, GDN_HEADS, GDN_DV))
    return o.reshape(bsz, L, GDN_V).astype(h.dtype) @ w_out


def s5_mixer(h, w_in, lam_re, lam_im, log_dt, b_re, b_im, c_re, c_im, d_skip, w_out):
    bsz, L, _ = h.shape
    f32 = jnp.float32
    u = (h @ w_in).astype(f32)
    ug = u.reshape(bsz, L, S5_GROUPS, S5_GROUP).transpose(1, 0, 2, 3)
    lam = lax.complex(lam_re.astype(f32), lam_im.astype(f32))
    dt = jnp.exp(log_dt.astype(f32))[:, None]
    lam_bar = jnp.exp(lam * dt)
    b_bar = ((lam_bar - 1.0) / lam)[..., None] * lax.complex(b_re.astype(f32), b_im.astype(f32))
    bu = jnp.einsum('gpk,lbgk->lbgp', b_bar, ug.astype(jnp.complex64))
    a = jnp.broadcast_to(lam_bar[None, None], (L, 1, S5_GROUPS, S5_STATE))

    def combine(e1, e2):
        a1, b1 = e1
        a2, b2 = e2
        return a1 * a2, a2 * b1 + b2

    _, states = lax.associative_scan(combine, (a, bu), axis=0)
    c = lax.complex(c_re.astype(f32), c_im.astype(f32))
    y = jnp.real(jnp.einsum('gkp,lbgp->lbgk', c, states)) \
        + d_skip.astype(f32).reshape(S5_GROUPS, S5_GROUP) * ug
    y = jax.nn.gelu(y.transpose(1, 0, 2, 3).reshape(bsz, L, D_MODEL)).astype(h.dtype)
    ag = y @ w_out
    val, gt = jnp.split(ag, 2, axis=-1)
    return val * jax.nn.sigmoid(gt)


def mamba2_mixer(h, w_in, conv_w, conv_b, dt_bias, a_log, d_skip, norm_g, w_out):
    bsz, L, _ = h.shape
    nc = L // CHUNK
    f32 = jnp.float32
    proj = h @ w_in
    z, xbc, dt_raw = jnp.split(proj, [M2_INNER, M2_INNER + M2_CONV_CH], axis=-1)
    xbc = jax.nn.silu(causal_dwconv(xbc, conv_w) + conv_b).astype(f32)
    xs, Bm, Cm = jnp.split(xbc, [M2_INNER, M2_INNER + M2_BC], axis=-1)
    x = xs.reshape(bsz, L, M2_HEADS, M2_HEAD_DIM)
    dt = jax.nn.softplus(dt_raw.astype(f32) + dt_bias.astype(f32))
    dA = dt * (-jnp.exp(a_log.astype(f32)))
    xdt = (x * dt[..., None]).reshape(bsz, nc, CHUNK, M2_GROUPS, M2_HPG, M2_HEAD_DIM)
    Bc = Bm.reshape(bsz, nc, CHUNK, M2_GROUPS, M2_STATE)
    Cc = Cm.reshape(bsz, nc, CHUNK, M2_GROUPS, M2_STATE)
    cum = jnp.cumsum(dA.reshape(bsz, nc, CHUNK, M2_GROUPS, M2_HPG), axis=2)
    idx = jnp.arange(CHUNK)
    causal = (idx[:, None] >= idx[None, :])[:, :, None, None]
    seg = cum[:, :, :, None] - cum[:, :, None, :]
    Lmat = jnp.exp(jnp.where(causal, seg, -jnp.inf))
    cb = jnp.einsum('bclgn,bcsgn->bclsg', Cc, Bc)
    y_diag = jnp.einsum('bclsgr,bcsgrp->bclgrp', cb[..., None] * Lmat, xdt)
    decay_states = jnp.exp(cum[:, :, -1:] - cum)
    states = jnp.einsum('bclgn,bclgrp->bcgrpn', Bc, xdt * decay_states[..., None])
    chunk_decay = jnp.exp(cum[:, :, -1])

    def step(S, inp):
        cd, st = inp
        return cd[..., None, None] * S + st, S

    S0 = jnp.zeros((bsz, M2_GROUPS, M2_HPG, M2_HEAD_DIM, M2_STATE), f32)
    _, S_prev = lax.scan(step, S0, (jnp.moveaxis(chunk_decay, 1, 0), jnp.moveaxis(states, 1, 0)))
    S_prev = jnp.moveaxis(S_prev, 0, 1)
    y_off = jnp.einsum('bclgn,bcgrpn->bclgrp', Cc, S_prev) * jnp.exp(cum)[..., None]
    y = (y_diag + y_off).reshape(bsz, L, M2_HEADS, M2_HEAD_DIM) + d_skip.astype(f32)[:, None] * x
    y = y.reshape(bsz, L, M2_INNER) * jax.nn.silu(z.astype(f32))
    y = rmsnorm(y.reshape(bsz, L, M2_GROUPS, M2_INNER // M2_GROUPS),
                norm_g.reshape(M2_GROUPS, M2_INNER // M2_GROUPS))
    return y.reshape(bsz, L, M2_INNER).astype(h.dtype) @ w_out


def sq_relu_mlp(h, w1, w2):
    return jnp.square(jax.nn.relu(h @ w1)) @ w2


def _inv_softplus_dt(key, shape):
    dt = jnp.exp(jax.random.uniform(key, shape, minval=math.log(1e-3), maxval=math.log(1e-1)))
    return dt + jnp.log(-jnp.expm1(-dt))


def setup_inputs(seed: int = 0) -> dict:
    key = jax.random.key(seed)
    ks = jax.random.split(key, 32)
    nrm = jax.random.normal
    f32 = jnp.float32
    nA, nB, nC = N_LAYERS_A, N_LAYERS_B, N_LAYERS_C
    return {
        "x": nrm(ks[0], (BATCH, SEQ, D_MODEL), f32),
        "norm_mix_g": 1.0 + 0.02 * nrm(ks[1], (DEPTH, D_MODEL), f32),
        "norm_mlp_g": 1.0 + 0.02 * nrm(ks[2], (DEPTH, D_MODEL), f32),
        "mlp_w1": nrm(ks[3], (DEPTH, D_MODEL, D_FF), f32) * D_MODEL ** -0.5,
        "mlp_w2": nrm(ks[4], (DEPTH, D_FF, D_MODEL), f32) * D_FF ** -0.5,
        "gdn_w_in": nrm(ks[5], (nA, D_MODEL, GDN_IN), f32) * D_MODEL ** -0.5,
        "gdn_conv_w": nrm(ks[6], (nA, CONV_W, GDN_CONV_CH), f32) * CONV_W ** -0.5,
        "gdn_a_log": jnp.log(jax.random.uniform(ks[7], (nA, GDN_HEADS), minval=1.0, maxval=16.0)),
        "gdn_dt_bias": _inv_softplus_dt(ks[8], (nA, GDN_HEADS)),
        "gdn_o_norm_g": 1.0 + 0.02 * nrm(ks[9], (nA, GDN_DV), f32),
        "gdn_w_out": nrm(ks[10], (nA, GDN_V, D_MODEL), f32) * GDN_V ** -0.5,
        "s5_w_in": nrm(ks[11], (nB, D_MODEL, D_MODEL), f32) * D_MODEL ** -0.5,
        "s5_lam_re": -0.5 + 0.01 * nrm(ks[12], (nB, S5_GROUPS, S5_STATE), f32),
        "s5_lam_im": math.pi * jnp.arange(S5_STATE, dtype=f32) + 0.01 * nrm(ks[13], (nB, S5_GROUPS, S5_STATE), f32),
        "s5_log_dt": jax.random.uniform(ks[14], (nB, S5_GROUPS), minval=math.log(1e-3), maxval=math.log(1e-1)),
        "s5_b_re": nrm(ks[15], (nB, S5_GROUPS, S5_STATE, S5_GROUP), f32) * (2 * S5_GROUP) ** -0.5,
        "s5_b_im": nrm(ks[16], (nB, S5_GROUPS, S5_STATE, S5_GROUP), f32) * (2 * S5_GROUP) ** -0.5,
        "s5_c_re": nrm(ks[17], (nB, S5_GROUPS, S5_GROUP, S5_STATE), f32) * (2 * S5_STATE) ** -0.5 * 4.0,
        "s5_c_im": nrm(ks[18], (nB, S5_GROUPS, S5_GROUP, S5_STATE), f32) * (2 * S5_STATE) ** -0.5 * 4.0,
        "s5_d": nrm(ks[19], (nB, D_MODEL), f32),
        "s5_w_out": nrm(ks[20], (nB, D_MODEL, 2 * D_MODEL), f32) * D_MODEL ** -0.5,
        "m2_w_in": nrm(ks[21], (nC, D_MODEL, M2_IN), f32) * D_MODEL ** -0.5,
        "m2_conv_w": nrm(ks[22], (nC, CONV_W, M2_CONV_CH), f32) * CONV_W ** -0.5,
        "m2_conv_b": 0.02 * nrm(ks[23], (nC, M2_CONV_CH), f32),
        "m2_dt_bias": _inv_softplus_dt(ks[24], (nC, M2_HEADS)),
        "m2_a_log": jnp.log(jax.random.uniform(ks[25], (nC, M2_HEADS), minval=1.0, maxval=16.0)),
        "m2_d": 1.0 + 0.02 * nrm(ks[26], (nC, M2_HEADS), f32),
        "m2_norm_g": 1.0 + 0.02 * nrm(ks[27], (nC, M2_INNER), f32),
        "m2_w_out": nrm(ks[28], (nC, M2_INNER, D_MODEL), f32) * M2_INNER ** -0.5,
        "final_norm_g": 1.0 + 0.02 * nrm(ks[29], (D_MODEL,), f32),
    }


def reference(x, norm_mix_g, norm_mlp_g, mlp_w1, mlp_w2,
              gdn_w_in, gdn_conv_w, gdn_a_log, gdn_dt_bias, gdn_o_norm_g, gdn_w_out,
              s5_w_in, s5_lam_re, s5_lam_im, s5_log_dt, s5_b_re, s5_b_im, s5_c_re, s5_c_im, s5_d, s5_w_out,
              m2_w_in, m2_conv_w, m2_conv_b, m2_dt_bias, m2_a_log, m2_d, m2_norm_g, m2_w_out,
              final_norm_g):
    h = x
    for i in range(DEPTH):
        kind, j = i % N_MIXERS, i // N_MIXERS
        hn = rmsnorm(h, norm_mix_g[i])
        if kind == 0:
            m = gated_deltanet(hn, gdn_w_in[j], gdn_conv_w[j], gdn_a_log[j], gdn_dt_bias[j],
                               gdn_o_norm_g[j], gdn_w_out[j])
        elif kind == 1:
            m = s5_mixer(hn, s5_w_in[j], s5_lam_re[j], s5_lam_im[j], s5_log_dt[j], s5_b_re[j],
                         s5_b_im[j], s5_c_re[j], s5_c_im[j], s5_d[j], s5_w_out[j])
        else:
            m = mamba2_mixer(hn, m2_w_in[j], m2_conv_w[j], m2_conv_b[j], m2_dt_bias[j], m2_a_log[j],
                             m2_d[j], m2_norm_g[j], m2_w_out[j])
        h = h + m.astype(h.dtype)
        h = h + sq_relu_mlp(rmsnorm(h, norm_mlp_g[i]), mlp_w1[i], mlp_w2[i]).astype(h.dtype)
    return rmsnorm(h, final_norm_g)
```

```python
import json
import numpy as np
from contextlib import ExitStack
import concourse.bass as bass
import concourse.mybir as mybir
from concourse.bass_utils import run_bass_kernel_spmd

F32 = mybir.dt.float32
BF16 = mybir.dt.bfloat16
AF = mybir.ActivationFunctionType
ALU = mybir.AluOpType

L = 2048
D = 1024
KC = 8
DFF = 4096
EPS = 1e-6


class Sched:
    ENG = ("pe", "act", "dve", "pool", "sp")

    def __init__(self, nc, stack):
        self.nc = nc
        self.stack = stack
        self.items = {e: [] for e in self.ENG}
        self.sems = {}
        self.val = {}
        self.seen = {e: {} for e in self.ENG}
        self.last_w = {}
        self.readers = {}
        self.epoch = {e: 0 for e in self.ENG}
        self.nsem = 0

    def _sem(self, key):
        if key not in self.sems:
            self.nsem += 1
            self.sems[key] = self.stack.enter_context(self.nc.semaphore("s%d" % self.nsem))
            self.val[key] = 0
        return self.sems[key]

    def _engkey(self, e):
        k = (e, self.epoch[e])
        if self.val.get(k, 0) >= 30000:
            self.epoch[e] += 1
            k = (e, self.epoch[e])
        return k

    def op(self, eng, fn, reads=(), writes=(), dsem=None):
        fns = fn if isinstance(fn, (list, tuple)) else [fn]
        deps = {}

        def add(ev):
            k, v = ev
            if deps.get(k, 0) < v:
                deps[k] = v

        for t in reads:
            if t in self.last_w:
                add(self.last_w[t])
            if isinstance(t, tuple) and t[0] == "ps":
                for k, v in self.readers.get(t, {}).items():
                    add((k, v))
        for t in writes:
            if t in self.last_w:
                add(self.last_w[t])
            for k, v in self.readers.get(t, {}).items():
                add((k, v))
        if dsem is None:
            key = self._engkey(eng)
            inc = 1
        else:
            key = ("dma", dsem)
            inc = 16
        self._sem(key)
        self.val[key] += inc
        ev = (key, self.val[key])
        waits = []
        for k, v in deps.items():
            if self.seen[eng].get(k, 0) >= v:
                continue
            self.seen[eng][k] = v
            waits.append((self.sems[k], v, k))
        self.items[eng].append((waits, fns, self.sems[key], inc, key))
        for t in writes:
            self.last_w[t] = ev
            self.readers[t] = {}
        for t in reads:
            r = self.readers.setdefault(t, {})
            if r.get(ev[0], 0) < ev[1]:
                r[ev[0]] = ev[1]
        return ev

    def wait_tokens(self, eng, tokens):
        waits = []
        for t in tokens:
            if t in self.last_w:
                k, v = self.last_w[t]
                if self.seen[eng].get(k, 0) < v:
                    self.seen[eng][k] = v
                    waits.append((self.sems[k], v, k))
        self.items[eng].append((waits, [], None, 0, None))

    def barrier(self):
        snap = dict(self.val)
        for e in self.ENG:
            waits = []
            for k, v in snap.items():
                if v > 0 and self.seen[e].get(k, 0) < v:
                    self.seen[e][k] = v
                    waits.append((self.sems[k], v, k))
            self.items[e].append((waits, [], None, 0, None))
        self.last_w = {}
        self.readers = {}

    def flush(self):
        self.simulate()
        nc = self.nc
        with nc.Block() as block:
            @block.tensor
            def _(e):
                self.emit("pe", e)

            @block.scalar
            def _(e):
                self.emit("act", e)

            @block.vector
            def _(e):
                self.emit("dve", e)

            @block.gpsimd
            def _(e):
                self.emit("pool", e)

            @block.sync
            def _(e):
                self.emit("sp", e)
        self.simvals = getattr(self, "simvals", None)
        self.items = {e: [] for e in self.ENG}

    def simulate(self):
        val = dict(getattr(self, "_simval", {}))
        for k in self.sems:
            val.setdefault(k, 0)
        pc = {e: 0 for e in self.ENG}
        progress = True
        while progress:
            progress = False
            for e in self.ENG:
                while pc[e] < len(self.items[e]):
                    waits, fns, sem, inc, key = self.items[e][pc[e]]
                    if all(val[k] >= v for _, v, k in waits):
                        if key is not None:
                            val[key] += inc
                        pc[e] += 1
                        progress = True
                    else:
                        break
        stuck = {e: (pc[e], len(self.items[e])) for e in self.ENG if pc[e] < len(self.items[e])}
        if stuck:
            for e in stuck:
                waits = self.items[e][pc[e]][0]
                print("STUCK", e, pc[e], [(k, v, val[k]) for _, v, k in waits if val[k] < v])
            raise RuntimeError("schedule deadlock: %s" % stuck)
        self._simval = val

    def emit(self, name, e):
        for waits, fns, sem, inc, _k in self.items[name]:
            for s, v, _kk in waits:
                e.wait_ge(s, v)
            last = None
            for f in fns:
                last = f(e)
            if last is not None and sem is not None:
                last.then_inc(sem, inc)


class PsumPool:
    def __init__(self, nc, stack, n=8):
        self.tiles = [stack.enter_context(nc.psum_tensor("ps%d" % i, [128, 512], F32)) for i in range(n)]
        self.i = 0
        self.n = n
        self.lo, self.hi = 0, n

    def get(self):
        if not (self.lo <= self.i < self.hi):
            self.i = self.lo
        i = self.i
        self.i = self.i + 1
        if self.i >= self.hi:
            self.i = self.lo
        return self.tiles[i], ("ps", i)


class Prog:
    def __init__(self, stages):
        self.stages = stages
        nc = bass.Bass("TRN2", target_bir_lowering=False)
        self.nc = nc
        self.stack = ExitStack()
        self.S = Sched(nc, self.stack)
        self.dram = {}

    def din(self, name, shape, dt=F32):
        t = self.nc.dram_tensor(name, list(shape), dt, kind="ExternalInput")
        self.dram[name] = t
        return t.ap()

    def sb(self, name, shape, dt):
        return self.stack.enter_context(self.nc.sbuf_tensor("sb_" + name, list(shape), dt))

    def rmsnorm_T(self, gcol, out_bf, tag):
        S, nc = self.S, self.nc
        hT, sq, ones = self.hT, self.sq, self.ones_bf
        for nt in range(4):
            ts = slice(nt * 512, (nt + 1) * 512)
            S.op("act", lambda e, ts=ts: e.activation(out=sq[:, :, :], in_=hT[:, :, ts], func=AF.Square),
                 reads=[("hT", nt, kc) for kc in range(KC)], writes=["sq"])
            ps, pt = self.PS.get()
            fns = [lambda e, kc=kc, ps=ps: e.matmul(ps[:, :], lhsT=ones[:, :], rhs=sq[:, kc, :],
                                                       start=(kc == 0), stop=(kc == KC - 1)) for kc in range(KC)]
            S.op("pe", fns, reads=["sq", "ones"], writes=[pt])
            rs = self.rstd
            S.op("act", lambda e, ps=ps: e.activation(out=rs[:, :], in_=ps[:, :], func=AF.Sqrt,
                                                     scale=1.0 / D, bias=self.eps_col[:, 0:1]),
                 reads=[pt, "eps"], writes=["rstd"])
            S.op("dve", lambda e: e.reciprocal(out=rs[:, :], in_=rs[:, :]), reads=["rstd"], writes=["rstd"])
            for kc in range(KC):
                S.op("dve", lambda e, kc=kc, ts=ts: e.scalar_tensor_tensor(
                    out=out_bf[:, kc, ts], in0=hT[:, kc, ts], scalar=gcol(kc), in1=rs[:, :],
                    op0=ALU.mult, op1=ALU.mult),
                    reads=[("hT", nt, kc), "rstd", "vecs"], writes=[(tag, nt, kc)])

    def mlp(self, l, w1_d, w2_d):
        S, nc = self.S, self.nc
        hT, hn = self.hT, self.hn
        gcol = lambda kc: self.vecs[:, 32 + l * 8 + kc:32 + l * 8 + kc + 1]
        w1v = w1_d[l].rearrange("(kc p) f -> p kc f", p=128)
        w2v = w2_d[l].rearrange("(fc p) d -> p fc d", p=128)
        NG = 8

        def load_w1(g):
            b = g % 2
            S.op("pool", lambda e: e.dma_start(out=self.w1b[b][:, :, :], in_=w1v[:, :, g * 512:(g + 1) * 512]),
                 writes=[("w1b", b)], dsem="w1b%d" % b)

        def load_w2(g):
            b = g % 2
            S.op("pool", lambda e: e.dma_start(out=self.w2b[b][:, :, :], in_=w2v[:, g * 4:(g + 1) * 4, :]),
                 writes=[("w2b", b)], dsem="w2b%d" % b)

        def W1(g):
            b = g % 2
            w1 = self.w1b[b]
            aT = self.aT[b]
            for nt in range(4):
                ts = slice(nt * 512, (nt + 1) * 512)
                for mc in range(4):
                    ps, pt = self.PS.get()
                    fns = [lambda e, kc=kc, ps=ps, mc=mc, ts=ts: e.matmul(
                        ps[:, :], lhsT=w1[:, kc, mc * 128:(mc + 1) * 128], rhs=hn[:, kc, ts],
                        start=(kc == 0), stop=(kc == KC - 1)) for kc in range(KC)]
                    S.op("pe", fns, reads=[("w1b", b)] + [("hn", nt, kc) for kc in range(KC)], writes=[pt])
                    r = self.rbuf[self.ri % 2]
                    rt = ("rbuf", self.ri % 2)
                    self.ri += 1
                    S.op("act", lambda e, ps=ps, r=r: e.activation(out=r[:, :], in_=ps[:, :], func=AF.Relu),
                         reads=[pt], writes=[rt])
                    S.op("act", lambda e, r=r, mc=mc, ts=ts: e.activation(out=aT[:, mc, ts], in_=r[:, :], func=AF.Square),
                         reads=[rt], writes=[("aT", b, nt, mc)])

        def W2(g):
            b = g % 2
            w2 = self.w2b[b]
            aT = self.aT[b]
            for nt in range(4):
                ts = slice(nt * 512, (nt + 1) * 512)
                for dc in range(KC):
                    ps, pt = self.PS.get()
                    fns = [lambda e, mc=mc, ps=ps, dc=dc, ts=ts: e.matmul(
                        ps[:, :], lhsT=w2[:, mc, dc * 128:(dc + 1) * 128], rhs=aT[:, mc, ts],
                        start=(mc == 0), stop=(mc == 3)) for mc in range(4)]
                    S.op("pe", fns, reads=[("w2b", b)] + [("aT", b, nt, mc) for mc in range(4)], writes=[pt])
                    S.op("dve", lambda e, ps=ps, dc=dc, ts=ts: e.tensor_tensor(
                        out=hT[:, dc, ts], in0=ps[:, :], in1=hT[:, dc, ts], op=ALU.add),
                        reads=[pt, ("hT", nt, dc)], writes=[("hT", nt, dc)])

        import os
        dbg = int(os.environ.get("KDBG", "9"))
        load_w1(0); load_w2(0); load_w1(1); load_w2(1)
        if dbg <= 1:
            return
        self.rmsnorm_T(gcol, hn, "hn")
        W1(0)
        if dbg <= 2:
            return
        if dbg == 3:
            NG = 2
        if dbg == 4:
            NG = 3
        for g in range(NG):
            if g + 1 < NG:
                W1(g + 1)
            if g + 2 < NG:
                load_w1(g + 2)
            W2(g)
            if g + 2 < NG:
                load_w2(g + 2)

    def final(self):
        S = self.S
        hT = self.hT
        rs = self.rstd
        gcol = lambda kc: self.vecs[:, 64 + kc:64 + kc + 1]
        self.rmsnorm_T(gcol, hT, "hT")

    def phase_begin(self):
        self.ph = ExitStack()
        self.S.barrier()

    def phase_end(self):
        self.S.flush()
        self.ph.close()

    def A(self, name, shape, dt):
        self.nbuf = getattr(self, "nbuf", 0) + 1
        return self.ph.enter_context(self.nc.sbuf_tensor("p%d_%s" % (self.nbuf, name), list(shape), dt))

    def mlp_phase(self, l, d):
        self.phase_begin()
        self.w1b = [self.A("w1b%d" % i, [128, KC, 512], BF16) for i in range(2)]
        self.w2b = [self.A("w2b%d" % i, [128, 4, D], BF16) for i in range(2)]
        self.aT = [self.A("aT%d" % i, [128, 4, L], BF16) for i in range(2)]
        self.rbuf = [self.A("rbuf%d" % i, [128, 512], BF16) for i in range(2)]
        self.ri = 0
        self.mlp(l, d["mlp_w1"], d["mlp_w2"])
        self.phase_end()

    def final_phase(self):
        self.phase_begin()
        self.final()
        self.phase_end()

    def mamba2_phase(self, li, d):
        self.phase_begin()
        import os
        stop = float(os.environ.get("KDBG2", "99"))
        S, nc, hT, hn, vecs = self.S, self.nc, self.hT, self.hn, self.vecs
        A = self.A
        U, SL, ident, onesf = self.U, self.SL, self.ident_bf, self.ones_f
        w_in = d["m2_w_in"][0].rearrange("(kc p) f -> p kc f", p=128)
        w_out = d["m2_w_out"][0].rearrange("(c p) f -> p c f", p=128)
        gcol = lambda kc: vecs[:, li * 8 + kc:li * 8 + kc + 1]
        self.rmsnorm_T(gcol, hn, "hn")
        HN = [("hn", nt, kc) for nt in range(4) for kc in range(KC)]

        rows = A("rows", [128, 96], F32)
        ngd = d["m2_rows"][:, 96:96 + 2048]
        ng = [A("ng%d" % i, [128, 256], F32) for i in range(2)]
        S.op("sp", lambda e: e.dma_start(out=rows[:, :], in_=d["m2_rows"][:, 0:96]), writes=["rows"], dsem="rows")
        dtb, alog, Dh = rows[:, 0:32], rows[:, 32:64], rows[:, 64:96]
        wdt = A("wdt", [128, KC, 32], BF16)
        S.op("pool", lambda e: e.dma_start(out=wdt[:, :, :], in_=w_in[:, :, 6144:6176]), writes=["wdt"], dsem="wdt")
        if stop <= 1:
            self.phase_end()
            return
        wxbc = A("wxbc", [128, KC, 512], BF16)
        wz = [A("wz0", [128, KC, 256], BF16)] * 2
        wo = [A("wo%d" % i, [128, 2, D], BF16) for i in range(1)]

        def dma(eng, out, in_, wtok, sem, reads=()):
            S.op(eng, lambda e: e.dma_start(out=out, in_=in_), reads=list(reads), writes=[wtok], dsem=sem)

        def load_group(g):
            b = g % 2
            dma("pool", wxbc[:, :, 0:256], w_in[:, :, 2048 + 256 * g:2048 + 256 * (g + 1)], ("wxbc", 0), "wxbc0")
            dma("pool", wxbc[:, :, 256:384], w_in[:, :, 4096 + 128 * g:4096 + 128 * (g + 1)], ("wxbc", 1), "wxbc1")
            dma("pool", wxbc[:, :, 384:512], w_in[:, :, 5120 + 128 * g:5120 + 128 * (g + 1)], ("wxbc", 2), "wxbc2")
            dma("pool", wz[0][:, :, :], w_in[:, :, 256 * g:256 * (g + 1)], "wz", "wz0")
            dma("pool", wo[0][:, :, :], w_out[:, 2 * g:2 * g + 2, :], ("wo", 0), "wo0")
            dma("sp", ng[b][:, :], ngd[:, 256 * g:256 * (g + 1)], ("ng", b), "ng%d" % b)

        dt = A("dt", [128, 16, 32], F32)
        dA = A("dA", [128, 16, 32], F32)
        ecum = A("ecum", [128, 16, 32], F32)
        ed = A("ed", [128, 16, 32], F32)
        cdrep = A("cdrep", [128, 16, 32], F32)
        negA = A("negA", [128, 32], F32)
        HL = L // 2
        cbufs = [A("cbufb0", [128, 3 + L], BF16)] * 2
        dgs = [[A("dg0_%d" % j, [128, 128], BF16) for j in range(4)]] * 2
        accc = A("accc", [128, 512], F32)
        cum = accc[:, :].rearrange("p (c h) -> p c h", h=32)
        ps, pt = self.PS.get()
        for c in range(16):
            fns = [lambda e, kc=kc, c=c, ps=ps: e.matmul(ps[:, c * 32:(c + 1) * 32], lhsT=hn[:, kc, c * 128:(c + 1) * 128],
                                                        rhs=wdt[:, kc, :], start=(kc == 0), stop=(kc == KC - 1))
                   for kc in range(KC)]
            S.op("pe", fns, reads=HN + ["wdt"], writes=[pt])
        psv = lambda p: p[:, :].rearrange("p (c h) -> p c h", h=32)
        S.op("dve", lambda e, ps=ps: e.tensor_tensor(out=dt[:, :, :], in0=psv(ps), in1=dtb.unsqueeze(1).to_broadcast([128, 16, 32]),
                                                    op=ALU.add), reads=[pt, "rows"], writes=["dt"])
        if stop <= 1.1:
            self.phase_end()
            return
        S.op("act", lambda e: e.activation(out=dt[:, :, :], in_=dt[:, :, :], func=AF.Exp), reads=["dt"], writes=["dt"])
        S.op("act", lambda e: e.activation(out=dt[:, :, :], in_=dt[:, :, :], func=AF.Ln, bias=1.0), reads=["dt"], writes=["dt"])
        if stop <= 1.2:
            self.phase_end()
            return
        S.op("act", lambda e: e.activation(out=negA[:, :], in_=alog, func=AF.Exp), reads=["rows"], writes=["negA"])
        S.op("dve", lambda e: e.tensor_scalar(out=negA[:, :], in0=negA[:, :], scalar1=-1.0, scalar2=None, op0=ALU.mult),
             reads=["negA"], writes=["negA"])
        S.op("dve", lambda e: e.tensor_tensor(out=dA[:, :, :], in0=dt[:, :, :], in1=negA[:, :].unsqueeze(1).to_broadcast([128, 16, 32]),
                                              op=ALU.mult), reads=["dt", "negA"], writes=["dA"])
        if stop <= 1.3:
            self.phase_end()
            return
        ps1, pt1 = self.PS.get()
        ps2, pt2 = self.PS.get()
        S.op("pe", [lambda e, c=c: e.matmul(ps1[:, c * 32:(c + 1) * 32], lhsT=U[:, :], rhs=dA[:, c, :], start=True, stop=True)
                    for c in range(16)], reads=["dA", "U"], writes=[pt1])
        S.op("pe", [lambda e, c=c: e.matmul(ps2[:, c * 32:(c + 1) * 32], lhsT=onesf[:, :], rhs=dA[:, c, :], start=True, stop=True)
                    for c in range(16)], reads=["dA", "onesf"], writes=[pt2])
        if stop <= 1.4:
            self.phase_end()
            return
        S.op("act", lambda e: e.activation(out=cum, in_=psv(ps1), func=AF.Identity), reads=[pt1], writes=["cum"])
        if stop <= 1.5:
            self.phase_end()
            return
        S.op("act", lambda e: e.activation(out=ecum[:, :, :], in_=psv(ps1), func=AF.Exp), reads=[pt1], writes=["ecum"])
        if stop <= 1.6:
            self.phase_end()
            return

        S.op("act", lambda e: e.activation(out=cdrep[:, :, :], in_=psv(ps2), func=AF.Exp), reads=[pt2], writes=["cdrep"])
        if stop <= 1.7:
            self.phase_end()
            return
        S.op("dve", lambda e: e.tensor_tensor(out=ed[:, :, :], in0=psv(ps2), in1=cum, op=ALU.subtract),
             reads=[pt2, "cum"], writes=["ed"])
        if stop <= 1.8:
            self.phase_end()
            return
        S.op("act", lambda e: e.activation(out=ed[:, :, :], in_=ed[:, :, :], func=AF.Exp), reads=["ed"], writes=["ed"])
        DEC = ["dt", "dA", "cum", "ecum", "ed", "cdrep"]
        if stop <= 2:
            self.phase_end()
            return

        fm = [A("fm%d" % i, [128, L], BF16) for i in range(4)]
        xB = A("xB", [128, 16, 384], BF16)
        yTg = A("yTg", [128, 2, L], BF16)
        S32 = A("S32", [128, 256], F32)
        Sbf = A("Sbf", [128, 256], BF16)
        zsall = A("zsall", [128, 16, 256], BF16)
        cbU = [A("cbU0", [128, 128], F32)] * 2
        rhsD = [A("rhsD0", [128, 4, 128], F32)] * 2
        E = [A("E0", [128, 4, 128], F32)] * 2
        Mt = [A("Mt%d" % i, [128, 4, 128], BF16) for i in range(2)]
        xdt = [A("xdt%d" % i, [128, 256], BF16) for i in range(2)]
        xdtd = [A("xdtd%d" % i, [128, 256], BF16) for i in range(2)]
        t1 = [A("t1%d" % i, [128, 256], F32) for i in range(2)]
        t2 = [A("t2%d" % i, [128, 256], F32) for i in range(1)] * 2
        yn = [A("yn%d" % i, [128, 256], BF16) for i in range(2)]
        ss = [A("ss%d" % i, [128, 1], F32) for i in range(2)]
        S.op("dve", lambda e: e.memset(cbufs[0][:, 0:3], 0.0), writes=[("cbuf", 0, "pad")])

        load_group(0)
        def group(g):
            b = g % 2
            for ch in range(4):
                jj = [2 * g, 2 * g + 1, 16 + g, 24 + g][ch]
                cw = lambda tap, jj=jj: vecs[:, 80 + jj * 5 + tap:80 + jj * 5 + tap + 1]
                cbi = 0
                cbuf = cbufs[cbi]
                dg = dgs[cbi]
                for tap in range(4):
                    S.op("dve", lambda e, tap=tap, dg=dg, cw=cw: e.tensor_scalar(out=dg[tap][:, :], in0=ident[:, :], scalar1=cw(tap), scalar2=None, op0=ALU.mult),
                         reads=["ident", "vecs"], writes=[("dg", cbi, tap)])
                for nt in range(4):
                    ts = slice(nt * 512, (nt + 1) * 512)
                    ps, pt = self.PS.get()
                    S.op("pe", [lambda e, kc=kc, ps=ps, ch=ch, ts=ts: e.matmul(
                        ps[:, :], lhsT=wxbc[:, kc, ch * 128:(ch + 1) * 128], rhs=hn[:, kc, ts],
                        start=(kc == 0), stop=(kc == KC - 1)) for kc in range(KC)],
                        reads=HN + [("wxbc", 0), ("wxbc", 1), ("wxbc", 2)], writes=[pt])
                    S.op("act", lambda e, ps=ps, nt=nt, cbuf=cbuf: e.activation(out=cbuf[:, 3 + nt * 512:3 + (nt + 1) * 512], in_=ps[:, :], func=AF.Identity),
                         reads=[pt], writes=[("cbuf", cbi, nt)])
                for nt in range(4):
                    ps, pt = self.PS.get()
                    rd = [("cbuf", cbi, nt), ("cbuf", cbi, "pad")] + ([("cbuf", cbi, nt - 1)] if nt > 0 else []) + [("dg", cbi, t_) for t_ in range(4)]
                    S.op("pe", [lambda e, tap=tap, ps=ps, nt=nt, cbuf=cbuf, dg=dg: e.matmul(
                        ps[:, :], lhsT=dg[tap][:, :], rhs=cbuf[:, tap + nt * 512:tap + (nt + 1) * 512],
                        start=(tap == 0), stop=(tap == 3)) for tap in range(4)], reads=rd, writes=[pt])
                    S.op("act", lambda e, ps=ps, nt=nt, ch=ch, cw=cw: e.activation(out=fm[ch][:, nt * 512:(nt + 1) * 512], in_=ps[:, :], func=AF.Silu, bias=cw(4)),
                         reads=[pt, "vecs"], writes=[("fm", ch, nt // 2, nt % 2)])
            if g + 1 < 8:
                pass
            if stop <= 3:
                return
            for c in range(16):
                ct = slice(c * 128, (c + 1) * 128)
                ps, pt = self.PS.get()
                S.op("pe", [lambda e, ch=ch, ps=ps, ct=ct: e.matmul(ps[:, ch * 128:(ch + 1) * 128], lhsT=fm[ch][:, ct], rhs=ident[:, :],
                                                                   start=True, stop=True) for ch in range(3)],
                     reads=[("fm", ch, hf, q) for ch in range(3) for hf in range(2) for q in range(2)] + ["ident"], writes=[pt])
                S.op("act", lambda e, ps=ps, c=c: e.activation(out=xB[:, c, :], in_=ps[:, 0:384], func=AF.Identity),
                     reads=[pt], writes=[("xB", c)])
            S.op("dve", lambda e: e.memset(S32[:, :], 0.0), writes=["S32"])
            S.op("dve", lambda e: e.memset(Sbf[:, :], 0.0), writes=["Sbf"])
            if stop <= 4:
                return
            hs = slice(4 * g, 4 * g + 4)
            for cq in range(8):
                ps, pt = self.PS.get()
                for cc in range(2):
                    c_ = 2 * cq + cc
                    S.op("pe", [lambda e, kc=kc, ps=ps, cc=cc, c_=c_: e.matmul(ps[:, cc * 256:(cc + 1) * 256], lhsT=hn[:, kc, c_ * 128:(c_ + 1) * 128],
                                                                             rhs=wz[0][:, kc, :], start=(kc == 0), stop=(kc == KC - 1)) for kc in range(KC)],
                         reads=HN + ["wz"], writes=[pt])
                S.op("act", lambda e, ps=ps, cq=cq: e.activation(out=zsall[:, 2 * cq:2 * cq + 2, :], in_=ps[:, :].rearrange("p (c v) -> p c v", v=256), func=AF.Silu),
                     reads=[pt], writes=["zsall"])
            v3 = lambda ap: ap.rearrange("p (h q) -> p h q", q=64)
            cur = {"ops": None}

            def sop(eng, fn, reads=(), writes=()):
                cur["ops"].append((eng, fn, list(reads), list(writes)))

            def chunk(c):
                ct = slice(c * 128, (c + 1) * 128)
                i = c % 2
                prep, tail = [], []
                cur["ops"] = prep
                self.PS.lo, self.PS.hi = 0, 4
                ps, pt = self.PS.get()
                sop("pe", lambda e, ps=ps: e.matmul(ps[:, 0:128], lhsT=fm[2][:, ct], rhs=fm[3][:, ct], start=True, stop=True),
                    reads=[("fm", ch, hf, q) for ch in (2, 3) for hf in range(2) for q in range(2)], writes=[pt])
                sop("dve", lambda e, ps=ps: e.tensor_tensor(out=cbU[i][:, :], in0=ps[:, 0:128], in1=U[:, :], op=ALU.mult),
                    reads=[pt, "U"], writes=["cbU"])
                sop("dve", lambda e: e.tensor_tensor(
                    out=rhsD[i][:, :, :], in0=U[:, :].unsqueeze(1).to_broadcast([128, 4, 128]),
                    in1=dA[:, c, hs].unsqueeze(2).to_broadcast([128, 4, 128]), op=ALU.mult),
                    reads=["U", "dA"], writes=["rhsD"])
                psD, ptD = self.PS.get()
                sop("pe", lambda e: e.matmul(psD[:, :], lhsT=SL[:, :], rhs=rhsD[i][:, :, :].rearrange("p h l -> p (h l)"),
                                             start=True, stop=True), reads=["rhsD", "SL"], writes=[ptD])
                sop("act", lambda e: e.activation(out=E[i][:, :, :].rearrange("p h l -> p (h l)"), in_=psD[:, :], func=AF.Exp),
                    reads=[ptD], writes=["E"])
                sop("dve", lambda e: e.tensor_tensor(out=Mt[i][:, :, :], in0=E[i][:, :, :],
                                                     in1=cbU[i][:, :].unsqueeze(1).to_broadcast([128, 4, 128]), op=ALU.mult),
                    reads=["E", "cbU"], writes=[("Mt", i)])
                sop("dve", lambda e: e.tensor_tensor(
                    out=v3(xdt[i][:, :]), in0=v3(xB[:, c, 0:256]),
                    in1=dt[:, c, hs].unsqueeze(2).to_broadcast([128, 4, 64]), op=ALU.mult),
                    reads=[("xB", c), "dt"], writes=[("xdt", i)])
                sop("dve", lambda e: e.tensor_tensor(
                    out=v3(xdtd[i][:, :]), in0=v3(xdt[i][:, :]),
                    in1=ed[:, c, hs].unsqueeze(2).to_broadcast([128, 4, 64]), op=ALU.mult),
                    reads=[("xdt", i), "ed"], writes=[("xdtd", i)])
                psy, pty = self.PS.get()
                sop("pe", [lambda e, h=h: e.matmul(psy[:, h * 64:(h + 1) * 64], lhsT=Mt[i][:, h, :], rhs=xdt[i][:, h * 64:(h + 1) * 64],
                                                  start=True, stop=True) for h in range(4)],
                    reads=[("Mt", i), ("xdt", i)], writes=[pty])
                sop("dve", lambda e: e.tensor_tensor(
                    out=v3(t2[i][:, :]), in0=v3(xB[:, c, 0:256]), in1=Dh[:, hs].unsqueeze(2).to_broadcast([128, 4, 64]), op=ALU.mult),
                    reads=[("xB", c), "rows"], writes=[("t2", i)])
                sop("dve", lambda e: e.tensor_tensor(out=t2[i][:, :], in0=psy[:, 0:256], in1=t2[i][:, :], op=ALU.add),
                    reads=[pty, ("t2", i)], writes=[("t2", i)])
                cur["ops"] = tail
                self.PS.lo, self.PS.hi = 4, 8
                pso, pto = self.PS.get()
                sop("pe", lambda e: e.matmul(pso[:, 0:256], lhsT=fm[3][:, ct], rhs=Sbf[:, :], start=True, stop=True),
                    reads=[("fm", 3, hf, q) for hf in range(2) for q in range(2)] + ["Sbf"], writes=[pto])
                pss, pts = self.PS.get()
                sop("pe", lambda e: e.matmul(pss[:, 0:256], lhsT=xB[:, c, 256:384], rhs=xdtd[i][:, :], start=True, stop=True),
                    reads=[("xB", c), ("xdtd", i)], writes=[pts])
                sop("dve", lambda e: e.tensor_tensor(
                    out=v3(t1[i][:, :]), in0=v3(pso[:, 0:256]), in1=ecum[:, c, hs].unsqueeze(2).to_broadcast([128, 4, 64]), op=ALU.mult),
                    reads=[pto, "ecum"], writes=[("t1", i)])
                sop("dve", lambda e: e.tensor_tensor(out=v3(S32[:, :]), in0=v3(S32[:, :]),
                                                     in1=cdrep[:, c, hs].unsqueeze(2).to_broadcast([128, 4, 64]), op=ALU.mult),
                    reads=["S32", "cdrep"], writes=["S32"])
                sop("dve", lambda e: e.tensor_tensor(out=S32[:, :], in0=pss[:, 0:256], in1=S32[:, :], op=ALU.add),
                    reads=[pts, "S32"], writes=["S32"])
                sop("act", lambda e: e.activation(out=Sbf[:, :], in_=S32[:, :], func=AF.Identity), reads=["S32"], writes=["Sbf"])
                sop("dve", lambda e: e.tensor_tensor(out=t1[i][:, :], in0=t1[i][:, :], in1=t2[i][:, :], op=ALU.add),
                    reads=[("t1", i), ("t2", i)], writes=[("t1", i)])
                sop("dve", lambda e: e.tensor_tensor(out=t1[i][:, :], in0=t1[i][:, :], in1=zsall[:, c, :], op=ALU.mult),
                    reads=[("t1", i), "zsall"], writes=[("t1", i)])
                sop("act", lambda e: e.activation(out=yn[i][:, :], in_=t1[i][:, :], func=AF.Square, accum_out=ss[i][:, 0:1]),
                    reads=[("t1", i)], writes=[("yn", i), ("ss", i)])
                sop("act", lambda e: e.activation(out=ss[i][:, :], in_=ss[i][:, :], func=AF.Ln, scale=1.0 / 256, bias=self.eps_col[:, 0:1]),
                    reads=[("ss", i), "eps"], writes=[("ss", i)])
                sop("act", lambda e: e.activation(out=ss[i][:, :], in_=ss[i][:, :], func=AF.Exp, scale=-0.5), reads=[("ss", i)], writes=[("ss", i)])
                sop("dve", lambda e: e.scalar_tensor_tensor(
                    out=yn[i][:, :], in0=t1[i][:, :], scalar=ss[i][:, 0:1], in1=ng[b][:, :], op0=ALU.mult, op1=ALU.mult),
                    reads=[("t1", i), ("ss", i), ("ng", b)], writes=[("yn", i)])
                psT, ptT = self.PS.get()
                sop("pe", [lambda e, j=j: e.matmul(psT[:, j * 128:(j + 1) * 128], lhsT=yn[i][:, j * 128:(j + 1) * 128], rhs=ident[:, :],
                                                  start=True, stop=True) for j in range(2)],
                    reads=[("yn", i), "ident"], writes=[ptT])
                sop("act", lambda e: e.activation(out=yTg[:, :, ct], in_=psT[:, 0:256].rearrange("p (j l) -> p j l", l=128), func=AF.Identity),
                    reads=[ptT], writes=[("yTg", c)])
                return prep, tail

            def zipl(x, y):
                out = []
                for k in range(max(len(x), len(y))):
                    if k < len(x):
                        out.append(x[k])
                    if k < len(y):
                        out.append(y[k])
                return out

            pts_ = [chunk(c) for c in range(16)]
            self.PS.lo, self.PS.hi = 0, 8
            stream = list(pts_[0][0])
            for c in range(16):
                stream += zipl(pts_[c][1], pts_[c + 1][0] if c + 1 < 16 else [])
            for eng, fn, r, w in stream:
                S.op(eng, fn, reads=r, writes=w)
            YT = [("yTg", c) for c in range(16)]
            for nt in range(4):
                ts = slice(nt * 512, (nt + 1) * 512)
                for dc in range(KC):
                    ps, pt = self.PS.get()
                    S.op("pe", [lambda e, j=j, ps=ps, dc=dc, ts=ts: e.matmul(ps[:, :], lhsT=wo[0][:, j, dc * 128:(dc + 1) * 128], rhs=yTg[:, j, ts],
                                                                            start=(j == 0), stop=(j == 1)) for j in range(2)],
                         reads=YT + [("wo", 0)], writes=[pt])
                    S.op("dve", lambda e, ps=ps, dc=dc, ts=ts: e.tensor_tensor(out=hT[:, dc, ts], in0=ps[:, :], in1=hT[:, dc, ts], op=ALU.add),
                         reads=[pt, ("hT", nt, dc)], writes=[("hT", nt, dc)])
        for g in range(8 if stop > 7 else 1):
            group(g)
            if g + 1 < 8 and stop > 7:
                load_group(g + 1)
        self.phase_end()

    def gdn_phase(self, li, d):
        self.phase_begin()
        S, nc, hT, hn, vecs = self.S, self.nc, self.hT, self.hn, self.vecs
        A = self.A
        U, SL, ident, identf, onesf, ones_bf = self.U, self.SL, self.ident_bf, self.ident_f, self.ones_f, self.ones_bf
        jl = li // 3
        w_in = d["gdn_w_in"][jl].rearrange("(kc p) f -> p kc f", p=128)
        w_out = d["gdn_w_out"][jl]
        gcol = lambda kc: vecs[:, li * 8 + kc:li * 8 + kc + 1]
        self.rmsnorm_T(gcol, hn, "hn")
        HN = [("hn", nt, kc) for nt in range(4) for kc in range(KC)]
        CW0 = 240 + jl * 96

        def dma(eng, out, in_, wtok, sem, reads=()):
            S.op(eng, lambda e: e.dma_start(out=out, in_=in_), reads=list(reads), writes=[wtok], dsem=sem)

        rows = A("rows", [128, 144], F32)
        dma("sp", rows[:, :], d["gdn_rows"][jl], "rows", "rows")
        dtb, alog, ong = rows[:, 0:8], rows[:, 8:16], rows[:, 16:144]
        wab = A("wab", [128, KC, 16], BF16)
        dma("pool", wab[:, :, :], w_in[:, :, 4096:4112], "wab", "wab")
        wqk = [A("wqkv%d" % i, [128, KC, 128], BF16) for i in range(2)]
        wgate = [A("wgate0", [128, KC, 128], BF16)] * 2
        wos = [A("wo%d" % i, [128, D], BF16) for i in range(2)]

        def load_head(h, slot):
            b = slot
            wo = wos[slot]
            for q in range(2):
                dma("pool", wqk[q % 2][:, :, :], w_in[:, :, q * 1024 + 128 * h:q * 1024 + 128 * (h + 1)], ("wqkv", q % 2), "wqkv%d" % (q % 2))
            dma("pool", wgate[0][:, :, :], w_in[:, :, 3072 + 128 * h:3072 + 128 * (h + 1)], "wgate", "wgate0")
            dma("pool", wo[:, :], w_out[128 * h:128 * (h + 1), :], (slot, "wo"), "wo%d" % slot)

        F3 = lambda nm: A(nm, [128, 16, 8], F32)
        gg, beta, nbeg, eG, ed, cdrep, G = F3("gg"), F3("beta"), F3("nbeg"), F3("eG"), F3("ed"), F3("cdrep"), F3("G")
        negA = A("negA", [128, 8], F32)
        ps, pt = self.PS.get()
        for c in range(16):
            S.op("pe", [lambda e, kc=kc, c=c, ps=ps: e.matmul(ps[:, c * 16:(c + 1) * 16], lhsT=hn[:, kc, c * 128:(c + 1) * 128],
                                                            rhs=wab[:, kc, :], start=(kc == 0), stop=(kc == KC - 1)) for kc in range(KC)],
                 reads=HN + ["wab"], writes=[pt])
        pab = ps[:, 0:256].rearrange("p (c t) -> p c t", t=16)
        S.op("dve", lambda e: e.tensor_tensor(out=gg[:, :, :], in0=pab[:, :, 0:8], in1=dtb.unsqueeze(1).to_broadcast([128, 16, 8]), op=ALU.add),
             reads=[pt, "rows"], writes=["gg"])
        S.op("act", lambda e: e.activation(out=beta[:, :, :], in_=pab[:, :, 8:16], func=AF.Sigmoid), reads=[pt], writes=["beta"])
        S.op("act", lambda e: e.activation(out=gg[:, :, :], in_=gg[:, :, :], func=AF.Exp), reads=["gg"], writes=["gg"])
        S.op("act", lambda e: e.activation(out=gg[:, :, :], in_=gg[:, :, :], func=AF.Ln, bias=1.0), reads=["gg"], writes=["gg"])
        S.op("act", lambda e: e.activation(out=negA[:, :], in_=alog, func=AF.Exp), reads=["rows"], writes=["negA"])
        S.op("dve", lambda e: e.tensor_scalar(out=negA[:, :], in0=negA[:, :], scalar1=-1.0, scalar2=None, op0=ALU.mult),
             reads=["negA"], writes=["negA"])
        S.op("dve", lambda e: e.tensor_tensor(out=gg[:, :, :], in0=gg[:, :, :], in1=negA[:, :].unsqueeze(1).to_broadcast([128, 16, 8]), op=ALU.mult),
             reads=["gg", "negA"], writes=["gg"])
        ps1, pt1 = self.PS.get()
        ps2, pt2 = self.PS.get()
        S.op("pe", [lambda e, c=c: e.matmul(ps1[:, c * 8:(c + 1) * 8], lhsT=U[:, :], rhs=gg[:, c, :], start=True, stop=True)
                    for c in range(16)], reads=["gg", "U"], writes=[pt1])
        S.op("pe", [lambda e, c=c: e.matmul(ps2[:, c * 8:(c + 1) * 8], lhsT=onesf[:, :], rhs=gg[:, c, :], start=True, stop=True)
                    for c in range(16)], reads=["gg", "onesf"], writes=[pt2])
        pv = lambda p: p[:, 0:128].rearrange("p (c h) -> p c h", h=8)
        S.op("act", lambda e: e.activation(out=G[:, :, :], in_=pv(ps1), func=AF.Identity), reads=[pt1], writes=["G"])
        S.op("act", lambda e: e.activation(out=eG[:, :, :], in_=pv(ps1), func=AF.Exp), reads=[pt1], writes=["eG"])
        S.op("act", lambda e: e.activation(out=cdrep[:, :, :], in_=pv(ps2), func=AF.Exp), reads=[pt2], writes=["cdrep"])
        S.op("dve", lambda e: e.tensor_tensor(out=ed[:, :, :], in0=pv(ps2), in1=G[:, :, :], op=ALU.subtract), reads=[pt2, "G"], writes=["ed"])
        S.op("act", lambda e: e.activation(out=ed[:, :, :], in_=ed[:, :, :], func=AF.Exp), reads=["ed"], writes=["ed"])
        S.op("dve", lambda e: e.tensor_tensor(out=nbeg[:, :, :], in0=beta[:, :, :], in1=eG[:, :, :], op=ALU.mult), reads=["beta", "eG"], writes=["nbeg"])
        S.op("dve", lambda e: e.tensor_scalar(out=nbeg[:, :, :], in0=nbeg[:, :, :], scalar1=-1.0, scalar2=None, op0=ALU.mult),
             reads=["nbeg"], writes=["nbeg"])
        nbeta = F3("nbeta")
        S.op("dve", lambda e: e.tensor_scalar(out=nbeta[:, :, :], in0=beta[:, :, :], scalar1=-1.0, scalar2=None, op0=ALU.mult),
             reads=["beta"], writes=["nbeta"])
        DECR = ["gg", "beta", "nbeg", "eG", "ed", "cdrep", "nbeta"]

        HL = 512
        NP, QN = L // HL, HL // 512
        cbuf = A("cbufb", [128, 3 + L], BF16)
        dg = [A("dg%d" % j, [128, 128], BF16) for j in range(4)]
        accb = A("accb", [128, L], BF16)
        sqb = self.sq[:, 0, :]
        rsn = self.rstd
        f128 = lambda nm: A(nm, [128, 128], F32)
        b128 = lambda nm: A(nm, [128, 128], BF16)
        slots = []
        for si in range(2):
            sl = {}
            sl["fm"] = [A("fmq%d" % si, [128, L], BF16), A("fmk%d" % si, [128, L], BF16), A("fmv%d" % si, [128, L], BF16)]
            sl["kvc"] = [A("kvc%d_%d" % (si, j), [128, 256], BF16) for j in range(2)]
            sl["oT"] = A("oT%d" % si, [128, L], BF16)
            sl["S32"] = A("S32_%d" % si, [128, 128], F32)
            sl["Sbf"] = A("Sbf_%d" % si, [128, 128], BF16)
            for par in range(2):
                for nm in ("rhsD", "E", "ET", "Pm", "X0", "X1"):
                    sl[(nm, par)] = f128("%s_%d_%d" % (nm, si, par))
                for nm in ("YR0", "YR1"):
                    sl[(nm, par)] = A("%s_%d_%d" % (nm, si, par), [128, 256], F32)
                for nm in ("Rtb", "bv", "kbg", "kd", "wTn", "qkT"):
                    sl[(nm, par)] = b128("%s_%d_%d" % (nm, si, par))
            sl["o1"] = f128("o1_%d" % si)
            sl["gsall"] = A("gsall%d" % si, [128, 16, 128], BF16)
            for nm in ("vnew", "onb"):
                sl[nm] = b128("%s_%d" % (nm, si))
            sl["ss"] = A("ss_%d" % si, [128, 1], F32)
            slots.append(sl)
        S.op("dve", lambda e: e.memset(cbuf[:, 0:3], 0.0), writes=[("cbuf", "pad")])
        self.evi = 0

        PER = {"vnew", "onb", "o1", "ss", "S32", "Sbf", "wo", "gsall"}

        def head(h, slot):
            b = slot
            sl = slots[slot]
            fm, oT, S32, Sbf, ss = sl["fm"], sl["oT"], sl["S32"], sl["Sbf"], sl["ss"]
            o1, vnew, onb, gsall = sl["o1"], sl["vnew"], sl["onb"], sl["gsall"]
            wo = wos[slot]
            stage2 = {"ops": None}
            PARTOK = {"rhsD", "E", "ET", "Pm", "Rt", "gs", "Rtb", "bv", "kbg", "kd", "wTn", "qkT"}

            def ns(t):
                if isinstance(t, str):
                    return (slot, t) if t in PER else t
                if t[0] in ("X", "Y", "kv", "oT", "par"):
                    return (slot,) + tuple(t)
                if t[0] == "fm":
                    return (slot,) + tuple(t)
                return t

            def sop(eng, fn, reads=(), writes=()):
                r, w = [ns(t) for t in reads], [ns(t) for t in writes]
                if stage2["ops"] is None:
                    S.op(eng, fn, reads=r, writes=w)
                else:
                    stage2["ops"].append((eng, fn, r, w))

            def evac(ps_ap, out_ap, rd, wr):
                self.evi += 1
                if self.evi % 2:
                    sop("act", lambda e: e.activation(out=out_ap, in_=ps_ap, func=AF.Identity), reads=rd, writes=wr)
                else:
                    sop("dve", lambda e: e.tensor_copy(out=out_ap, in_=ps_ap), reads=rd, writes=wr)

            for ch in range(3):
                jj = ch * 8 + h
                cw = lambda tap, jj=jj: vecs[:, CW0 + jj * 4 + tap:CW0 + jj * 4 + tap + 1]
                if ch == 2:
                    dma("pool", wqk[0][:, :, :], w_in[:, :, 2048 + 128 * h:2048 + 128 * (h + 1)], ("wqkv", 0), "wqkv0")
                for tap in range(4):
                    sop("dve", lambda e, tap=tap, cw=cw: e.tensor_scalar(out=dg[tap][:, :], in0=ident[:, :], scalar1=cw(tap), scalar2=None, op0=ALU.mult),
                        reads=["ident", "vecs"], writes=[("dg", tap)])
                for hf in range(NP):
                    ts = slice(hf * 512, (hf + 1) * 512)
                    ps, pt = self.PS.get()
                    sop("pe", [lambda e, kc=kc, ps=ps, ch=ch, ts=ts: e.matmul(
                        ps[:, :], lhsT=wqk[ch % 2][:, kc, :], rhs=hn[:, kc, ts],
                        start=(kc == 0), stop=(kc == KC - 1)) for kc in range(KC)],
                        reads=HN + [("wqkv", ch % 2)], writes=[pt])
                    sop("act", lambda e, ps=ps, hf=hf: e.activation(out=cbuf[:, 3 + hf * 512:3 + (hf + 1) * 512], in_=ps[:, :], func=AF.Identity),
                        reads=[pt], writes=[("cbuf", hf)])
                pss_ = []
                for hf in range(NP):
                    hsl = slice(hf * HL, (hf + 1) * HL)
                    psc, ptc = self.PS.get()
                    rd = [("cbuf", hf), ("cbuf", "pad")] + ([("cbuf", hf - 1)] if hf > 0 else []) + [("dg", t_) for t_ in range(4)]
                    sop("pe", [lambda e, tap=tap, psc=psc, hf=hf: e.matmul(psc[:, :], lhsT=dg[tap][:, :], rhs=cbuf[:, tap + hf * 512:tap + (hf + 1) * 512],
                                                                        start=(tap == 0), stop=(tap == 3)) for tap in range(4)], reads=rd, writes=[ptc])
                    if ch == 2:
                        sop("act", lambda e, hsl=hsl, psc=psc: e.activation(out=fm[2][:, hsl], in_=psc[:, :], func=AF.Silu),
                            reads=[ptc], writes=[("fm", 2, hf)])
                    else:
                        sop("act", lambda e, psc=psc, hsl=hsl: e.activation(out=accb[:, hsl], in_=psc[:, :], func=AF.Silu), reads=[ptc], writes=[("accb", hf)])
                        sop("act", lambda e, hf=hf, hsl=hsl: e.activation(out=self.sq[:, hf, :], in_=accb[:, hsl], func=AF.Square), reads=[("accb", hf)], writes=[("sqp", hf)])
                        ps, pt = self.PS.get()
                        sop("pe", lambda e, ps=ps, hf=hf: e.matmul(ps[:, :], lhsT=ones_bf[:, :], rhs=self.sq[:, hf, :], start=True, stop=True),
                            reads=[("sqp", hf), "ones"], writes=[pt])
                        pss_.append((ps, pt))
                if ch != 2:
                    sc = (128.0 ** -0.5) if ch == 0 else 1.0
                    for hf in range(NP):
                        hsl = slice(hf * HL, (hf + 1) * HL)
                        ps, pt = pss_[hf]
                        sop("act", lambda e, ps=ps: e.activation(out=rsn[:, :], in_=ps[:, :], func=AF.Ln, bias=self.eps_col[:, 0:1]), reads=[pt, "eps"], writes=["rstd"])
                        sop("act", lambda e: e.activation(out=rsn[:, :], in_=rsn[:, :], func=AF.Exp, scale=-0.5), reads=["rstd"], writes=["rstd"])
                        sop("dve", lambda e, ch=ch, hsl=hsl, sc=sc: e.scalar_tensor_tensor(
                            out=fm[ch][:, hsl], in0=accb[:, hsl], scalar=sc, in1=rsn[:, :], op0=ALU.mult, op1=ALU.mult),
                            reads=[("accb", hf), "rstd"], writes=[("fm", ch, hf)])
            FM = lambda ch: [("fm", ch, hf) for hf in range(NP)]
            sop("dve", lambda e: e.memset(S32[:, :], 0.0), writes=["S32"])
            sop("dve", lambda e: e.memset(Sbf[:, :], 0.0), writes=["Sbf"])
            for cq in range(4):
                psg, ptg = self.PS.get()
                for cc in range(4):
                    c_ = 4 * cq + cc
                    sop("pe", [lambda e, kc=kc, psg=psg, cc=cc, c_=c_: e.matmul(psg[:, cc * 128:(cc + 1) * 128], lhsT=hn[:, kc, c_ * 128:(c_ + 1) * 128],
                                                                              rhs=wgate[b][:, kc, :], start=(kc == 0), stop=(kc == KC - 1)) for kc in range(KC)],
                        reads=HN + ["wgate"], writes=[ptg])
                sop("act", lambda e, psg=psg, cq=cq: e.activation(out=gsall[:, 4 * cq:4 * cq + 4, :], in_=psg[:, :].rearrange("p (c v) -> p c v", v=128), func=AF.Silu),
                    reads=[ptg], writes=["gsall"])
            stage2["ops"] = []
            self.PS.lo, self.PS.hi = 4 * slot, 4 * slot + 4

            def chunk(c):
                ct = slice(c * 128, (c + 1) * 128)
                par = c % 2
                col = lambda t: t[:, c, h:h + 1]
                rhsD, E, ET, Pm = (sl[(n, par)] for n in ("rhsD", "E", "ET", "Pm"))
                YR = [sl[("YR0", par)], sl[("YR1", par)]]
                X = [sl[("X0", par)], sl[("X1", par)]]
                Rtb, bv, kbg, kd, wTn, qkT = (sl[(n, par)] for n in ("Rtb", "bv", "kbg", "kd", "wTn", "qkT"))
                T = lambda n: ("par", n, par)
                prep, tail = [], []
                stage2["ops"] = prep
                self.PS.lo, self.PS.hi = 4 * slot, 4 * slot + 2
                sop("pool", lambda e: e.tensor_scalar(out=rhsD[:, :], in0=SL[:, :], scalar1=col(gg), scalar2=1.0, op0=ALU.mult, op1=ALU.mult),
                     reads=["SL", "gg"], writes=[T("rhsD")])
                psd, ptd = self.PS.get()
                sop("pe", [lambda e: e.matmul(psd[:, 0:128], lhsT=U[:, :], rhs=rhsD[:, :], start=True, stop=True),
                            lambda e: e.matmul(psd[:, 128:256], lhsT=rhsD[:, :], rhs=U[:, :], start=True, stop=True)],
                     reads=["U", T("rhsD")], writes=[ptd])
                sop("act", lambda e: e.activation(out=E[:, :], in_=psd[:, 0:128], func=AF.Exp), reads=[ptd], writes=[T("E")])
                sop("act", lambda e: e.activation(out=ET[:, :], in_=psd[:, 128:256], func=AF.Exp), reads=[ptd], writes=[T("ET")])
                psk, ptk = self.PS.get()
                sop("pe", [lambda e: e.matmul(psk[:, 0:128], lhsT=fm[1][:, ct], rhs=fm[1][:, ct], start=True, stop=True),
                            lambda e: e.matmul(psk[:, 128:256], lhsT=fm[1][:, ct], rhs=fm[0][:, ct], start=True, stop=True)],
                     reads=FM(0) + FM(1), writes=[ptk])
                sop("dve", lambda e: e.tensor_tensor(out=E[:, :], in0=psk[:, 0:128], in1=E[:, :], op=ALU.mult), reads=[ptk, T("E")], writes=[T("E")])
                sop("dve", lambda e: e.scalar_tensor_tensor(out=Pm[:, :], in0=E[:, :], scalar=col(nbeta), in1=SL[:, :], op0=ALU.mult, op1=ALU.mult),
                     reads=[T("E"), "nbeta", "SL"], writes=[T("Pm")])
                sop("dve", lambda e: e.tensor_tensor(out=ET[:, :], in0=psk[:, 128:256], in1=ET[:, :], op=ALU.mult), reads=[ptk, T("ET")], writes=[T("ET")])
                sop("pool", lambda e: e.tensor_tensor(out=qkT[:, :], in0=ET[:, :], in1=U[:, :], op=ALU.mult), reads=[T("ET"), "U"], writes=[T("qkT")])
                pst, ptt = self.PS.get()
                sop("pe", lambda e: e.transpose(out=pst[:, 0:128], in_=Pm[:, :], identity=identf[:, :]), reads=[T("Pm"), "identf"], writes=[ptt])
                evac(pst[:, 0:128], YR[0][:, 0:128], [ptt], [T("Y0")])
                sop("pool", lambda e: e.tensor_copy(out=YR[0][:, 128:256], in_=identf[:, :]), reads=["identf"], writes=[T("R0")])
                Xc, xt = Pm, T("Pm")
                for k in range(6):
                    a_, nb = k % 2, (k + 1) % 2
                    YRa, YRn = YR[a_], YR[nb]
                    Xn = X[nb]
                    psx, ptx = self.PS.get()
                    sop("pe", lambda e, YRa=YRa, Xc=Xc, psx=psx: e.matmul(psx[:, 0:128], lhsT=YRa[:, 0:128], rhs=Xc[:, :], start=True, stop=True),
                         reads=[T("Y%d" % a_), xt], writes=[ptx])
                    evac(psx[:, 0:128], Xn[:, :], [ptx], [T("X%d" % nb)])
                    psb, ptb = self.PS.get()
                    if k < 5:
                        sop("pe", lambda e, YRa=YRa, Xc=Xc, psb=psb: e.matmul(psb[:, 0:256], lhsT=Xc[:, :], rhs=YRa[:, 0:256], start=True, stop=True),
                             reads=[T("Y%d" % a_), T("R%d" % a_), xt], writes=[ptb])
                        evac(psb[:, 0:128], YRn[:, 0:128], [ptb], [T("Y%d" % nb)])
                        sop("dve", lambda e, YRa=YRa, YRn=YRn, psb=psb: e.tensor_tensor(out=YRn[:, 128:256], in0=psb[:, 128:256], in1=YRa[:, 128:256], op=ALU.add),
                             reads=[ptb, T("R%d" % a_)], writes=[T("R%d" % nb)])
                    else:
                        sop("pe", lambda e, YRa=YRa, Xc=Xc, psb=psb: e.matmul(psb[:, 0:128], lhsT=Xc[:, :], rhs=YRa[:, 128:256], start=True, stop=True),
                             reads=[T("R%d" % a_), xt], writes=[ptb])
                        sop("dve", lambda e, YRa=YRa, YRn=YRn, psb=psb: e.tensor_tensor(out=YRn[:, 128:256], in0=psb[:, 0:128], in1=YRa[:, 128:256], op=ALU.add),
                             reads=[ptb, T("R%d" % a_)], writes=[T("R%d" % nb)])
                    Xc, xt = Xn, T("X%d" % nb)
                psf, ptf = self.PS.get()
                sop("pe", lambda e, psf=psf: e.matmul(psf[:, 0:128], lhsT=X[0][:, :], rhs=YR[0][:, 128:256], start=True, stop=True),
                     reads=[T("X0"), T("R0")], writes=[ptf])
                sop("dve", lambda e, psf=psf: e.tensor_tensor(out=Rtb[:, :], in0=psf[:, 0:128], in1=YR[0][:, 128:256], op=ALU.add),
                     reads=[ptf, T("R0")], writes=[T("Rtb")])
                kvc = sl["kvc"][par]
                pskv, ptkv = self.PS.get()
                sop("pe", [lambda e, q=q: e.matmul(pskv[:, q * 128:(q + 1) * 128], lhsT=fm[1 + q][:, ct], rhs=ident[:, :],
                                                    start=True, stop=True) for q in range(2)],
                     reads=FM(1) + FM(2) + ["ident"], writes=[ptkv])
                evac(pskv[:, 0:256], kvc[:, :], [ptkv], [T("kvc")])
                sop("pool", lambda e: e.tensor_scalar(out=bv[:, :], in0=kvc[:, 128:256], scalar1=col(beta), scalar2=1.0, op0=ALU.mult, op1=ALU.mult),
                     reads=[T("kvc"), "beta"], writes=[T("bv")])
                sop("pool", lambda e: e.tensor_scalar(out=kbg[:, :], in0=kvc[:, 0:128], scalar1=col(nbeg), scalar2=1.0, op0=ALU.mult, op1=ALU.mult),
                     reads=[T("kvc"), "nbeg"], writes=[T("kbg")])
                sop("pool", lambda e: e.tensor_scalar(out=kd[:, :], in0=kvc[:, 0:128], scalar1=col(ed), scalar2=1.0, op0=ALU.mult, op1=ALU.mult),
                     reads=[T("kvc"), "ed"], writes=[T("kd")])
                psw, ptw = self.PS.get()
                sop("pe", lambda e: e.matmul(psw[:, 0:128], lhsT=kbg[:, :], rhs=Rtb[:, :], start=True, stop=True), reads=[T("kbg"), T("Rtb")], writes=[ptw])
                evac(psw[:, 0:128], wTn[:, :], [ptw], [T("wTn")])
                stage2["ops"] = tail
                self.PS.lo, self.PS.hi = 4 * slot + 2, 4 * slot + 4
                psv_, ptv = self.PS.get()
                sop("pe", [lambda e: e.matmul(psv_[:, 0:128], lhsT=Rtb[:, :], rhs=bv[:, :], start=True, stop=False),
                            lambda e: e.matmul(psv_[:, 0:128], lhsT=wTn[:, :], rhs=Sbf[:, :], start=False, stop=True)],
                     reads=[T("Rtb"), T("bv"), T("wTn"), "Sbf"], writes=[ptv])
                evac(psv_[:, 0:128], vnew[:, :], [ptv], ["vnew"])
                pso, pto = self.PS.get()
                sop("pe", [lambda e: e.matmul(pso[:, 0:128], lhsT=fm[0][:, ct], rhs=Sbf[:, :], start=True, stop=True),
                            lambda e: e.matmul(pso[:, 128:256], lhsT=qkT[:, :], rhs=vnew[:, :], start=True, stop=True)],
                     reads=FM(0) + ["Sbf", T("qkT"), "vnew"], writes=[pto])
                pss, pts = self.PS.get()
                sop("pe", lambda e: e.matmul(pss[:, 0:128], lhsT=kd[:, :], rhs=vnew[:, :], start=True, stop=True), reads=[T("kd"), "vnew"], writes=[pts])
                sop("dve", lambda e: e.scalar_tensor_tensor(out=S32[:, :], in0=S32[:, :], scalar=col(cdrep), in1=pss[:, 0:128], op0=ALU.mult, op1=ALU.add),
                     reads=[pts, "S32", "cdrep"], writes=["S32"])
                sop("act", lambda e: e.activation(out=Sbf[:, :], in_=S32[:, :], func=AF.Identity), reads=["S32"], writes=["Sbf"])
                sop("act", lambda e: e.activation(out=o1[:, :], in_=pso[:, 0:128], func=AF.Identity, scale=col(eG)), reads=[pto, "eG"], writes=["o1"])
                sop("dve", lambda e: e.tensor_tensor(out=o1[:, :], in0=pso[:, 128:256], in1=o1[:, :], op=ALU.add), reads=[pto, "o1"], writes=["o1"])
                sop("act", lambda e: e.activation(out=onb[:, :], in_=o1[:, :], func=AF.Square, accum_out=ss[:, 0:1]),
                     reads=["o1"], writes=["onb", "ss"])
                sop("act", lambda e: e.activation(out=ss[:, :], in_=ss[:, :], func=AF.Ln, scale=1.0 / 128, bias=self.eps_col[:, 0:1]),
                     reads=["ss", "eps"], writes=["ss"])
                sop("act", lambda e: e.activation(out=ss[:, :], in_=ss[:, :], func=AF.Exp, scale=-0.5), reads=["ss"], writes=["ss"])
                sop("dve", lambda e: e.scalar_tensor_tensor(out=o1[:, :], in0=o1[:, :], scalar=ss[:, 0:1], in1=ong, op0=ALU.mult, op1=ALU.mult),
                     reads=["o1", "ss", "rows"], writes=["o1"])
                sop("dve", lambda e: e.tensor_tensor(out=onb[:, :], in0=o1[:, :], in1=gsall[:, c, :], op=ALU.mult), reads=["o1", "gsall"], writes=["onb"])
                psT, ptT = self.PS.get()
                sop("pe", lambda e: e.matmul(psT[:, 0:128], lhsT=onb[:, :], rhs=ident[:, :], start=True, stop=True), reads=["onb", "ident"], writes=[ptT])
                evac(psT[:, 0:128], oT[:, ct], [ptT], [("oT", c)])
                return prep, tail

            def zipl(x, y):
                out = []
                for i in range(max(len(x), len(y))):
                    if i < len(x):
                        out.append(x[i])
                    if i < len(y):
                        out.append(y[i])
                return out

            pts_ = [chunk(c) for c in range(16)]
            stream = list(pts_[0][0])
            for c in range(16):
                stream += zipl(pts_[c][1], pts_[c + 1][0] if c + 1 < 16 else [])
            stage2["ops"] = stream
            self.PS.lo, self.PS.hi = 4 * slot, 4 * slot + 4
            OT = [("oT", c) for c in range(16)]
            for nt in range(4):
                ts = slice(nt * 512, (nt + 1) * 512)
                for dc in range(KC):
                    ps, pt = self.PS.get()
                    sop("pe", lambda e, ps=ps, dc=dc, ts=ts: e.matmul(ps[:, :], lhsT=wo[:, dc * 128:(dc + 1) * 128], rhs=oT[:, ts], start=True, stop=True),
                         reads=OT + ["wo"], writes=[pt])
                    sop("dve", lambda e, ps=ps, dc=dc, ts=ts: e.tensor_tensor(out=hT[:, dc, ts], in0=ps[:, :], in1=hT[:, dc, ts], op=ALU.add),
                         reads=[pt, ("hT", nt, dc)], writes=[("hT", nt, dc)])
            self.PS.lo, self.PS.hi = 0, 8
            return stage2["ops"]

        for pair in range(4):
            lists = []
            for slot in range(2):
                h = 2 * pair + slot
                load_head(h, slot)
                lists.append(head(h, slot))
            for i in range(max(len(x) for x in lists)):
                for ops in lists:
                    if i < len(ops):
                        eng, fn, r, w = ops[i]
                        S.op(eng, fn, reads=r, writes=w)
        self.phase_end()

    def s5_phase(self, li, d):
        self.phase_begin()
        import os
        stop3 = float(os.environ.get("KDBG3", "99"))
        S, nc, hT, hn, vecs = self.S, self.nc, self.hT, self.hn, self.vecs
        A = self.A
        ident, identf = self.ident_bf, self.ident_f
        PI = float(np.pi)
        w_in = d["s5_w_in"][0].rearrange("(kc p) f -> p kc f", p=128)
        w_out = d["s5_w_out"][0].rearrange("(kc p) f -> p kc f", p=128)
        gcol = lambda kc: vecs[:, li * 8 + kc:li * 8 + kc + 1]
        self.rmsnorm_T(gcol, hn, "hn")
        HN = [("hn", nt, kc) for nt in range(4) for kc in range(KC)]

        def dma(eng, out, in_, wtok, sem, reads=()):
            S.op(eng, lambda e: e.dma_start(out=out, in_=in_), reads=list(reads), writes=[wtok], dsem=sem)

        cur = {"ops": None}

        def op(eng, fn, rd, wr):
            if cur["ops"] is None:
                S.op(eng, fn, reads=rd, writes=wr)
            else:
                cur["ops"].append((eng, fn, list(rd), list(wr)))

        HL = 512
        prm = A("prm", [128, 3 * 64 + 1 + 8], F32)
        dma("sp", prm[:, :], d["s5_prm"], "prm", "prm")
        LR, LIM, LDT, sgn = prm[:, 0:64], prm[:, 64:128], prm[:, 128:192], prm[:, 192:193]
        dcol = lambda mc: prm[:, 193 + mc:194 + mc]
        dtv, ar, th, rdec = (A(n, [128, 64], F32) for n in ("dtv", "ar", "th", "rdec"))
        q0, q1, q2, q3, q4, q5, q6, q7 = (A("q%d" % i, [128, 64], F32) for i in range(8))
        qi = A("qi", [128, 64], mybir.dt.int32)
        cfr, cfi = A("cfr", [128, 64], F32), A("cfi", [128, 64], F32)

        def exp_acc(out, x, xt, ot, k, n):
            sc = 1.0 / (2 ** k)
            op("dve", lambda e: e.tensor_scalar(out=out, in0=x, scalar1=sc / n, scalar2=1.0, op0=ALU.mult, op1=ALU.add), [xt], [ot])
            for m in range(n - 1, 0, -1):
                op("dve", lambda e: e.tensor_tensor(out=out, in0=out, in1=x, op=ALU.mult), [xt, ot], [ot])
                op("dve", lambda e, m=m: e.tensor_scalar(out=out, in0=out, scalar1=sc / m, scalar2=1.0, op0=ALU.mult, op1=ALU.add), [ot], [ot])
            for _ in range(k):
                op("dve", lambda e: e.tensor_tensor(out=out, in0=out, in1=out, op=ALU.mult), [ot], [ot])

        exp_acc(dtv[:, :], LDT, "prm", "dtv", 4, 12)
        op("dve", lambda e: e.tensor_tensor(out=ar[:, :], in0=LR, in1=dtv[:, :], op=ALU.mult), ["prm", "dtv"], ["ar"])
        op("dve", lambda e: e.tensor_tensor(out=th[:, :], in0=LIM, in1=dtv[:, :], op=ALU.mult), ["prm", "dtv"], ["th"])
        exp_acc(rdec[:, :], ar[:, :], "ar", "rdec", 0, 8)
        thoff = A("thoff", [128, 4, 64], F32)
        for hf_ in range(4):
            op("dve", lambda e, hf_=hf_: e.tensor_scalar(out=thoff[:, hf_, :], in0=th[:, :], scalar1=float(hf_ * 512), scalar2=None, op0=ALU.mult), ["th"], ["thoff"])
        NS = 20
        op("dve", lambda e: e.tensor_scalar(out=q0[:, :], in0=ar[:, :], scalar1=1.0 / (NS + 1), scalar2=1.0, op0=ALU.mult, op1=ALU.add), ["ar"], ["q0"])
        op("dve", lambda e: e.tensor_scalar(out=q1[:, :], in0=th[:, :], scalar1=1.0 / (NS + 1), scalar2=None, op0=ALU.mult), ["th"], ["q1"])
        for m in range(NS, 1, -1):
            op("dve", lambda e: e.tensor_tensor(out=q2[:, :], in0=ar[:, :], in1=q0[:, :], op=ALU.mult), ["ar", "q0"], ["q2"])
            op("dve", lambda e: e.tensor_tensor(out=q3[:, :], in0=th[:, :], in1=q1[:, :], op=ALU.mult), ["th", "q1"], ["q3"])
            op("dve", lambda e: e.tensor_tensor(out=q4[:, :], in0=ar[:, :], in1=q1[:, :], op=ALU.mult), ["ar", "q1"], ["q4"])
            op("dve", lambda e: e.tensor_tensor(out=q5[:, :], in0=th[:, :], in1=q0[:, :], op=ALU.mult), ["th", "q0"], ["q5"])
            op("dve", lambda e: e.tensor_tensor(out=q2[:, :], in0=q2[:, :], in1=q3[:, :], op=ALU.subtract), ["q2", "q3"], ["q2"])
            op("dve", lambda e: e.tensor_tensor(out=q4[:, :], in0=q4[:, :], in1=q5[:, :], op=ALU.add), ["q4", "q5"], ["q4"])
            op("dve", lambda e, m=m: e.tensor_scalar(out=q0[:, :], in0=q2[:, :], scalar1=1.0 / m, scalar2=1.0, op0=ALU.mult, op1=ALU.add), ["q2"], ["q0"])
            op("dve", lambda e, m=m: e.tensor_scalar(out=q1[:, :], in0=q4[:, :], scalar1=1.0 / m, scalar2=None, op0=ALU.mult), ["q4"], ["q1"])
        op("dve", lambda e: e.tensor_copy(out=q2[:, :], in_=th[:, :]), ["th"], ["q2"])
        op("dve", lambda e: e.tensor_scalar(out=q3[:, :], in0=th[:, :], scalar1=PI / 2, scalar2=None, op0=ALU.add), ["th"], ["q3"])

        def rr_small(x, xt):
            op("dve", lambda e: e.tensor_scalar(out=qi[:, :], in0=x, scalar1=1.0 / (2 * PI), scalar2=None, op0=ALU.mult), [xt], ["qi"])
            op("dve", lambda e: e.tensor_copy(out=q4[:, :], in_=qi[:, :]), ["qi"], ["q4"])
            op("dve", lambda e: e.scalar_tensor_tensor(out=x, in0=q4[:, :], scalar=-2 * PI, in1=x, op0=ALU.mult, op1=ALU.add), ["q4", xt], [xt])
            op("dve", lambda e: e.tensor_scalar(out=q4[:, :], in0=x, scalar1=PI, scalar2=-2 * PI, op0=ALU.is_gt, op1=ALU.mult), [xt], ["q4"])
            op("dve", lambda e: e.tensor_tensor(out=x, in0=x, in1=q4[:, :], op=ALU.add), [xt, "q4"], [xt])
            op("dve", lambda e: e.tensor_scalar(out=q4[:, :], in0=x, scalar1=-PI, scalar2=2 * PI, op0=ALU.is_lt, op1=ALU.mult), [xt], ["q4"])
            op("dve", lambda e: e.tensor_tensor(out=x, in0=x, in1=q4[:, :], op=ALU.add), [xt, "q4"], [xt])
        rr_small(q2[:, :], "q2")
        rr_small(q3[:, :], "q3")
        op("act", lambda e: e.activation(out=q2[:, :], in_=q2[:, :], func=AF.Sin, scale=0.999995), ["q2"], ["q2"])
        op("act", lambda e: e.activation(out=q3[:, :], in_=q3[:, :], func=AF.Sin, scale=0.999995), ["q3"], ["q3"])
        op("dve", lambda e: e.tensor_tensor(out=q2[:, :], in0=q2[:, :], in1=rdec[:, :], op=ALU.mult), ["q2", "rdec"], ["q2"])
        op("dve", lambda e: e.tensor_tensor(out=q3[:, :], in0=q3[:, :], in1=rdec[:, :], op=ALU.mult), ["q3", "rdec"], ["q3"])
        op("dve", lambda e: e.tensor_scalar(out=q3[:, :], in0=q3[:, :], scalar1=-1.0, scalar2=None, op0=ALU.add), ["q3"], ["q3"])
        op("dve", lambda e: e.tensor_tensor(out=q4[:, :], in0=ar[:, :], in1=ar[:, :], op=ALU.mult), ["ar"], ["q4"])
        op("dve", lambda e: e.tensor_tensor(out=q5[:, :], in0=th[:, :], in1=th[:, :], op=ALU.mult), ["th"], ["q5"])
        op("dve", lambda e: e.tensor_tensor(out=q4[:, :], in0=q4[:, :], in1=q5[:, :], op=ALU.add), ["q4", "q5"], ["q4"])
        op("dve", lambda e: e.reciprocal(out=q5[:, :], in_=q4[:, :]), ["q4"], ["q5"])
        op("dve", lambda e: e.tensor_scalar(out=q4[:, :], in0=q4[:, :], scalar1=4.0, scalar2=None, op0=ALU.is_lt), ["q4"], ["q4"])
        op("dve", lambda e: e.tensor_tensor(out=q6[:, :], in0=q3[:, :], in1=ar[:, :], op=ALU.mult), ["q3", "ar"], ["q6"])
        op("dve", lambda e: e.tensor_tensor(out=q7[:, :], in0=q2[:, :], in1=th[:, :], op=ALU.mult), ["q2", "th"], ["q7"])
        op("dve", lambda e: e.tensor_tensor(out=q6[:, :], in0=q6[:, :], in1=q7[:, :], op=ALU.add), ["q6", "q7"], ["q6"])
        op("dve", lambda e: e.tensor_tensor(out=q6[:, :], in0=q6[:, :], in1=q5[:, :], op=ALU.mult), ["q6", "q5"], ["q6"])
        op("dve", lambda e: e.tensor_tensor(out=q7[:, :], in0=q2[:, :], in1=ar[:, :], op=ALU.mult), ["q2", "ar", "q6"], ["q7"])
        op("dve", lambda e: e.tensor_tensor(out=q2[:, :], in0=q3[:, :], in1=th[:, :], op=ALU.mult), ["q3", "th", "q7"], ["q2"])
        op("dve", lambda e: e.tensor_tensor(out=q7[:, :], in0=q7[:, :], in1=q2[:, :], op=ALU.subtract), ["q7", "q2"], ["q7"])
        op("dve", lambda e: e.tensor_tensor(out=q7[:, :], in0=q7[:, :], in1=q5[:, :], op=ALU.mult), ["q7", "q5"], ["q7"])
        op("dve", lambda e: e.tensor_tensor(out=q0[:, :], in0=q0[:, :], in1=q6[:, :], op=ALU.subtract), ["q0", "q6"], ["q0"])
        op("dve", lambda e: e.tensor_tensor(out=q0[:, :], in0=q0[:, :], in1=q4[:, :], op=ALU.mult), ["q0", "q4"], ["q0"])
        op("dve", lambda e: e.tensor_tensor(out=q0[:, :], in0=q0[:, :], in1=q6[:, :], op=ALU.add), ["q0", "q6"], ["q0"])
        op("dve", lambda e: e.tensor_tensor(out=cfr[:, :], in0=q0[:, :], in1=dtv[:, :], op=ALU.mult), ["q0", "dtv"], ["cfr"])
        op("dve", lambda e: e.tensor_tensor(out=q1[:, :], in0=q1[:, :], in1=q7[:, :], op=ALU.subtract), ["q1", "q7"], ["q1"])
        op("dve", lambda e: e.tensor_tensor(out=q1[:, :], in0=q1[:, :], in1=q4[:, :], op=ALU.mult), ["q1", "q4"], ["q1"])
        op("dve", lambda e: e.tensor_tensor(out=q1[:, :], in0=q1[:, :], in1=q7[:, :], op=ALU.add), ["q1", "q7"], ["q1"])
        op("dve", lambda e: e.tensor_tensor(out=cfi[:, :], in0=q1[:, :], in1=dtv[:, :], op=ALU.mult), ["q1", "dtv"], ["cfi"])
        rx = A("rx", [64, 2, 8, 64], F32)
        self.dbg_names = {}
        ramp = A("ramp", [128, HL], F32)
        op("pool", lambda e: e.iota(ramp[:, :], pattern=[[1, HL]], base=0, channel_multiplier=0, allow_small_or_imprecise_dtypes=True), [], ["ramp"])

        ang, kf, cosT, sinT, wv, Wsc = (A(n, [128, HL], F32) for n in ("ang", "kf", "cosT", "sinT", "wv", "Wsc"))
        cosT2, sinT2, Wsc2 = (A(n, [128, HL], F32) for n in ("cosT2", "sinT2", "Wsc2"))
        ki = A("ki", [128, HL], mybir.dt.int32)
        Ycs = A("Ycs", [128, HL], BF16)
        Ysn = A("Ysn", [128, HL], BF16)
        ub = A("ub", [128, L], BF16)
        yT = A("yT", [128, KC, L], BF16)
        ddiag = A("ddiag", [128, 128], BF16)
        c0, c1, c2, c3, c4, c5 = ang, kf, cosT, sinT, wv, Wsc
        ci = ki
        BT = A("BT", [128, 2, 8, 64], F32)
        Bp = A("Bp", [128, 8, 128], BF16)
        Bs = A("Bs", [128, 8, 128], BF16)
        Cp = A("Cp", [128, 8, 128], BF16)
        Cq = A("Cq", [128, 8, 128], BF16)
        Cf = A("Cf", [128, 2, 8, 128], F32)
        wu = A("wu", [128, KC, 128], BF16)
        wo = [A("wo%d" % i, [128, KC, 256], BF16) for i in range(1)] * 2

        def range_reduce(x, n, tmp_i, tmp_f, rd, tf):
            op("dve", lambda e: e.tensor_scalar(out=tmp_i, in0=x, scalar1=1.0 / (2 * PI), scalar2=None, op0=ALU.mult), rd, ["ki"])
            op("dve", lambda e: e.tensor_copy(out=tmp_f, in_=tmp_i), ["ki"], [tf])
            op("dve", lambda e: e.scalar_tensor_tensor(out=x, in0=tmp_f, scalar=-2 * PI, in1=x, op0=ALU.mult, op1=ALU.add), [tf] + rd, rd)
            op("dve", lambda e: e.tensor_scalar(out=tmp_f, in0=x, scalar1=PI, scalar2=-2 * PI, op0=ALU.is_gt, op1=ALU.mult), rd, [tf])
            op("dve", lambda e: e.tensor_tensor(out=x, in0=x, in1=tmp_f, op=ALU.add), rd + [tf], rd)
            op("dve", lambda e: e.tensor_scalar(out=tmp_f, in0=x, scalar1=-PI, scalar2=2 * PI, op0=ALU.is_lt, op1=ALU.mult), rd, [tf])
            op("dve", lambda e: e.tensor_tensor(out=x, in0=x, in1=tmp_f, op=ALU.add), rd + [tf], rd)

        self.PS.lo, self.PS.hi = 0, 4
        self.PS.i = 0
        yacc = [(self.PS.tiles[4 + q], ("ps", 4 + q)) for q in range(4)]

        def mc_block(mc):
            passes = []
            dma("pool", wu[:, :, :], w_in[:, :, mc * 128:(mc + 1) * 128], "wu", "wu")
            for nt in range(4):
                ts = slice(nt * 512, (nt + 1) * 512)
                ps, pt = self.PS.get()
                S.op("pe", [lambda e, kc=kc, ps=ps, ts=ts: e.matmul(ps[:, :], lhsT=wu[:, kc, :], rhs=hn[:, kc, ts],
                                                                   start=(kc == 0), stop=(kc == KC - 1)) for kc in range(KC)],
                     reads=HN + ["wu"], writes=[pt])
                op("act", lambda e, ps=ps, ts=ts: e.activation(out=ub[:, ts], in_=ps[:, :], func=AF.Identity), [pt], [("ub", nt)])
            UB = [("ub", nt) for nt in range(4)]
            dma("sp", BT[:, :, :, :], d["s5_bT"][:, :, mc * 8:(mc + 1) * 8, :], "BT", "BT")
            dma("sp", Cf[:, :, :, :], d["s5_cpad"][:, :, mc * 8:(mc + 1) * 8, :], "Cf", "Cf")
            for ri, cf, ctile, ctok in ((0, cfr, c3, "sinT"), (1, cfi, c4, "wv")):
                op("dve", lambda e, ri=ri, cf=cf: e.tensor_tensor(
                    out=rx[:, ri, :, :], in0=identf[0:64, 0:64].unsqueeze(1).to_broadcast([64, 8, 64]),
                    in1=cf[0:64, mc * 8:(mc + 1) * 8].unsqueeze(2).to_broadcast([64, 8, 64]), op=ALU.mult),
                    ["identf", "cfr", "cfi"], [("rx", ri)])
                psx, ptx = self.PS.get()
                S.op("pe", lambda e, psx=psx, ri=ri: e.matmul(psx[:, :], lhsT=self.ones_f[0:64, :], rhs=rx[:, ri, :, :].rearrange("p g q -> p (g q)"),
                                                             start=True, stop=True), reads=[("rx", ri), "onesf"], writes=[ptx])
                op("act", lambda e, psx=psx, ctile=ctile: e.activation(out=ctile[:, :], in_=psx[:, :], func=AF.Identity), [ptx], [ctok])
            cr3 = c3[:, :].rearrange("p (g q) -> p g q", q=64)
            ci3 = c4[:, :].rearrange("p (g q) -> p g q", q=64)
            t5 = c5[:, :].rearrange("p (g q) -> p g q", q=64)
            t0 = c0[:, :].rearrange("p (g q) -> p g q", q=64)
            bre, bim = BT[:, 0, :, :], BT[:, 1, :, :]
            op("dve", lambda e: e.tensor_tensor(out=t5, in0=cr3, in1=bre, op=ALU.mult), ["sinT", "BT", "Wsc"], ["Wsc"])
            op("dve", lambda e: e.tensor_tensor(out=t0, in0=ci3, in1=bim, op=ALU.mult), ["wv", "BT", "ang"], ["ang"])
            op("dve", lambda e: e.tensor_tensor(out=Bp[:, :, 0:64], in0=t5, in1=t0, op=ALU.subtract), ["Wsc", "ang"], [("Bp", 0)])
            op("dve", lambda e: e.tensor_scalar(out=Bs[:, :, 64:128], in0=Bp[:, :, 0:64], scalar1=-1.0, scalar2=None, op0=ALU.mult), [("Bp", 0)], [("Bs", 1)])
            op("dve", lambda e: e.tensor_tensor(out=t5, in0=cr3, in1=bim, op=ALU.mult), ["sinT", "BT", "Wsc", ("Bp", 0)], ["Wsc"])
            op("dve", lambda e: e.tensor_tensor(out=t0, in0=ci3, in1=bre, op=ALU.mult), ["wv", "BT", "ang", ("Bp", 0)], ["ang"])
            op("dve", lambda e: e.tensor_tensor(out=Bp[:, :, 64:128], in0=t5, in1=t0, op=ALU.add), ["Wsc", "ang"], [("Bp", 1)])
            op("dve", lambda e: e.tensor_copy(out=Bs[:, :, 0:64], in_=Bp[:, :, 64:128]), [("Bp", 1)], [("Bs", 0)])
            BP = [("Bp", 0), ("Bp", 1), ("Bs", 0), ("Bs", 1)]
            op("dve", lambda e: e.tensor_scalar(out=Cp[:, :, :], in0=Cf[:, 0, :, :], scalar1=sgn, scalar2=None, op0=ALU.mult), ["Cf", "prm"], ["Cp"])
            op("dve", lambda e: e.tensor_scalar(out=Cq[:, :, :], in0=Cf[:, 1, :, :], scalar1=-1.0, scalar2=None, op0=ALU.mult), ["Cf"], ["Cq"])
            op("dve", lambda e: e.tensor_scalar(out=ddiag[:, :], in0=identf[:, :], scalar1=dcol(mc), scalar2=None, op0=ALU.mult), ["identf", "prm"], ["ddiag"])
            for q in range(4):
                ya, yt = yacc[q]
                S.op("pe", lambda e, ya=ya, q=q: e.matmul(ya[:, :], lhsT=ddiag[:, :], rhs=ub[:, q * 512:(q + 1) * 512], start=True, stop=False),
                     reads=["ddiag"] + UB, writes=[yt])

            if stop3 <= 1:
                return

            def group(gl):
                g = 8 * mc + gl
                def one_pass(hf):
                    par = hf % 2
                    cosT_, sinT_, Wsc_, Ycs_, Ysn_ = (cosT2, sinT2, Wsc2, Ycs, Ysn) if par else (cosT, sinT, Wsc, Ycs, Ysn)
                    WscP, twP = (Wsc, 'Wsc') if par else (Wsc2, 'Wsc2')
                    tc, tsn, tw, tyc, tys = ('cosT2', 'sinT2', 'Wsc2', 'Ycs', 'Ysn') if par else ('cosT', 'sinT', 'Wsc', 'Ycs', 'Ysn')
                    Tops, Dops = [], []
                    cur["ops"] = Tops
                    nt = hf
                    ts = slice(nt * 512, (nt + 1) * 512)
                    qs = slice(0, 512)
                    psz, ptz = self.PS.get()
                    pss, pts = self.PS.get()
                    op("pe", lambda e: e.matmul(psz[:, :], lhsT=Bp[:, gl, :], rhs=ub[:, ts], start=True, stop=True), BP + UB, [ptz])
                    op("pe", lambda e: e.matmul(pss[:, :], lhsT=Bs[:, gl, :], rhs=ub[:, ts], start=True, stop=True), BP + UB, [pts])
                    op("act", lambda e, hf=hf: e.activation(out=ang[:, :], in_=ramp[:, :], func=AF.Identity, scale=th[:, g:g + 1],
                                                            bias=thoff[:, hf, g:g + 1]), ["ramp", "th", "thoff"], ["ang"])
                    op("dve", lambda e: e.tensor_scalar(out=ki[:, :], in0=ang[:, :], scalar1=1.0 / (2 * PI), scalar2=None, op0=ALU.mult), ["ang"], ["ki"])
                    op("dve", lambda e: e.scalar_tensor_tensor(out=ang[:, :], in0=ki[:, :], scalar=-2 * PI, in1=ang[:, :], op0=ALU.mult, op1=ALU.add),
                       ["ki", "ang"], ["ang"])
                    op("dve", lambda e: e.tensor_scalar(out=ang[:, :], in0=ang[:, :], scalar1=-PI, scalar2=PI, op0=ALU.max, op1=ALU.min), ["ang"], ["ang"])
                    op("act", lambda e: e.activation(out=sinT_[:, :], in_=ang[:, :], func=AF.Sin, scale=0.999995), ["ang"], [tsn])
                    op("act", lambda e: e.activation(out=cosT_[:, :], in_=ang[:, :], func=AF.Sin, scale=0.5), ["ang"], [tc])
                    op("act", lambda e: e.activation(out=cosT_[:, :], in_=cosT_[:, :], func=AF.Square), [tc], [tc])
                    op("act", lambda e: e.activation(out=cosT_[:, :], in_=cosT_[:, :], func=AF.Identity, scale=-2.0, bias=1.0), [tc], [tc])
                    cur["ops"] = Dops
                    for q in range(1):
                        op("dve", lambda e, pss=pss, qs=qs: e.tensor_tensor(out=kf[:, qs], in0=pss[:, :], in1=sinT_[:, qs], op=ALU.mult), [pts, tsn], ["kf"])
                        op("dve", lambda e, psz=psz, qs=qs: e.tensor_tensor(out=wv[:, qs], in0=psz[:, :], in1=cosT_[:, qs], op=ALU.mult), [ptz, tc], ["wv"])
                        op("dve", lambda e, qs=qs: e.tensor_tensor(out=wv[:, qs], in0=wv[:, qs], in1=kf[:, qs], op=ALU.add), ["kf", "wv"], ["wv"])
                    WV = ["wv"]
                    if hf > 0:
                        op("dve", lambda e: e.scalar_tensor_tensor(out=wv[:, 0:1], in0=WscP[:, HL - 1:HL], scalar=rdec[:, g:g + 1], in1=wv[:, 0:1],
                                                                   op0=ALU.mult, op1=ALU.add), WV + ["rdec", twP], WV)
                    op("dve", lambda e: e.tensor_tensor_scan(out=Wsc_[:, :], data0=rdec[:, g:g + 1].to_broadcast([128, HL]), data1=wv[:, :],
                                                             initial=0.0, op0=ALU.mult, op1=ALU.add), WV + ["rdec"], [tw])
                    op("pool", lambda e: e.tensor_tensor(out=Ycs_[:, :], in0=Wsc_[:, :], in1=cosT_[:, :], op=ALU.mult), [tw, tc], [tyc])
                    op("pool", lambda e: e.tensor_tensor(out=Ysn_[:, :], in0=Wsc_[:, :], in1=sinT_[:, :], op=ALU.mult), [tw, tsn], [tys])
                    for q in range(1):
                        nt = hf
                        qs = slice(0, 512)
                        ya, yt = yacc[nt]
                        last = (gl == 7)
                        op("pe", [lambda e, ya=ya, qs=qs: e.matmul(ya[:, :], lhsT=Cp[:, gl, :], rhs=Ycs_[:, qs], start=False, stop=False),
                                  lambda e, ya=ya, qs=qs: e.matmul(ya[:, :], lhsT=Cq[:, gl, :], rhs=Ysn_[:, qs], start=False, stop=last)],
                           ["Cp", "Cq", tyc, tys, yt], [yt])
                    cur["ops"] = None
                    return Tops, Dops

                for hf in range(4):
                    passes.append(one_pass(hf))

            for gl in range(8):
                group(gl)
            def emit(ops):
                for eng, fn, r, w in ops:
                    S.op(eng, fn, reads=r, writes=w)
            emit(passes[0][0])
            for n in range(len(passes)):
                x, y = passes[n][1], (passes[n + 1][0] if n + 1 < len(passes) else [])
                for k in range(max(len(x), len(y))):
                    if k < len(x):
                        emit([x[k]])
                    if k < len(y):
                        emit([y[k]])
            for nt in range(4):
                ts = slice(nt * 512, (nt + 1) * 512)
                ya, yt = yacc[nt]
                x = cosT[:, 0:512]
                x2 = sinT[:, 0:512]
                op("act", lambda e, ya=ya: e.activation(out=x, in_=ya[:, :], func=AF.Identity), [yt], ["cosT"])
                op("dve", lambda e: e.tensor_tensor(out=x2, in0=x, in1=x, op=ALU.mult), ["cosT"], ["sinT"])
                op("dve", lambda e: e.tensor_scalar(out=x2, in0=x2, scalar1=0.044715, scalar2=1.0, op0=ALU.mult, op1=ALU.add), ["sinT"], ["sinT"])
                op("dve", lambda e: e.tensor_tensor(out=x2, in0=x2, in1=x, op=ALU.mult), ["sinT", "cosT"], ["sinT"])
                op("act", lambda e: e.activation(out=x2, in_=x2, func=AF.Sigmoid, scale=1.5957691216057308), ["sinT"], ["sinT"])
                op("dve", lambda e, ts=ts: e.tensor_tensor(out=yT[:, mc, ts], in0=x2, in1=x, op=ALU.mult), ["sinT", "cosT"], [("yT", mc, nt)])

        for mc in range(8 if stop3 > 4 else 1):
            mc_block(mc)
        if stop3 <= 4:
            self.PS.lo, self.PS.hi = 0, 8
            self.phase_end()
            return
        self.PS.lo, self.PS.hi = 0, 8
        YT = [("yT", mc, nt) for mc in range(8) for nt in range(4)]

        def outp(dc):
            b = dc % 2
            dma("pool", wo[b][:, :, 0:128], w_out[:, :, dc * 128:(dc + 1) * 128], ("wo", 0, 0), "wo0a")
            dma("pool", wo[b][:, :, 128:256], w_out[:, :, 1024 + dc * 128:1024 + (dc + 1) * 128], ("wo", 0, 1), "wo0b")
            for nt in range(4):
                ts = slice(nt * 512, (nt + 1) * 512)
                psv, ptv = self.PS.get()
                psg, ptg = self.PS.get()
                S.op("pe", [lambda e, kc=kc, psv=psv, ts=ts: e.matmul(psv[:, :], lhsT=wo[b][:, kc, 0:128], rhs=yT[:, kc, ts],
                                                                     start=(kc == 0), stop=(kc == KC - 1)) for kc in range(KC)],
                     reads=YT + [("wo", 0, 0)], writes=[ptv])
                S.op("pe", [lambda e, kc=kc, psg=psg, ts=ts: e.matmul(psg[:, :], lhsT=wo[b][:, kc, 128:256], rhs=yT[:, kc, ts],
                                                                     start=(kc == 0), stop=(kc == KC - 1)) for kc in range(KC)],
                     reads=YT + [("wo", 0, 1)], writes=[ptg])
                sg = cosT[:, 0:512]
                op("act", lambda e, psg=psg: e.activation(out=sg, in_=psg[:, :], func=AF.Sigmoid), [ptg], ["cosT"])
                op("dve", lambda e, psv=psv: e.tensor_tensor(out=sg, in0=psv[:, :], in1=sg, op=ALU.mult), [ptv, "cosT"], ["cosT"])
                op("dve", lambda e, ts=ts: e.tensor_tensor(out=hT[:, dc, ts], in0=hT[:, dc, ts], in1=sg, op=ALU.add),
                   ["cosT", ("hT", nt, dc)], [("hT", nt, dc)])

        for dc in range(KC):
            outp(dc)
        self.phase_end()

    def build(self):
        nc, S, stack = self.nc, self.S, self.stack
        d = {}
        xT = self.din("xT", [D, L])
        vecs_d = self.din("vecs", [128, NV])
        for name, shape in DRAM_INPUTS:
            d[name] = self.din(name, shape)
        yT = nc.dram_tensor("yT", [D, L], F32, kind="ExternalOutput").ap()

        self.PS = PsumPool(nc, stack)
        self.hT = hT = self.sb("hT", [128, KC, L], F32)
        self.hn = hn = self.sb("hn", [128, KC, L], BF16)
        self.sq = self.sb("sq", [128, KC, 512], BF16)
        self.rstd = self.sb("rstd", [128, 512], F32)
        self.ones_bf = self.sb("ones_bf", [128, 128], BF16)
        self.ones_f = self.sb("ones_f", [128, 128], F32)
        self.ident_bf = self.sb("ident_bf", [128, 128], BF16)
        self.ident_f = self.sb("ident_f", [128, 128], F32)
        self.U = self.sb("U", [128, 128], F32)
        self.SL = self.sb("SL", [128, 128], F32)
        self.eps_col = self.sb("eps_col", [128, 1], F32)
        self.vecs = vecs = self.sb("vecs", [128, NV], F32)

        S.op("dve", lambda e: e.memset(self.ones_bf[:, :], 1.0), writes=["ones"])
        S.op("dve", lambda e: e.memset(self.ones_f[:, :], 1.0), writes=["onesf"])
        S.op("dve", lambda e: e.memset(self.eps_col[:, :], EPS), writes=["eps"])
        S.op("pool", lambda e: e.affine_select(out=self.U[:, :], in_=self.ones_f[:, :], pattern=[[1, 128]], compare_op=ALU.is_ge,
                                               fill=0.0, base=0, channel_multiplier=-1), reads=["onesf"], writes=["U"])
        S.op("pool", lambda e: e.affine_select(out=self.SL[:, :], in_=self.ones_f[:, :], pattern=[[-1, 128]], compare_op=ALU.is_gt,
                                               fill=0.0, base=0, channel_multiplier=1), reads=["onesf"], writes=["SL"])
        S.op("pool", lambda e: e.affine_select(out=self.ident_f[:, :], in_=self.ones_f[:, :], pattern=[[1, 128]], compare_op=ALU.is_equal,
                                               fill=0.0, base=0, channel_multiplier=-1), reads=["onesf"], writes=["identf"])
        S.op("dve", lambda e: e.tensor_copy(out=self.ident_bf[:, :], in_=self.ident_f[:, :]), reads=["identf"], writes=["ident"])
        S.op("sp", lambda e: e.dma_start(out=vecs[:, :], in_=vecs_d[:, :]), writes=["vecs"], dsem="vecs")
        xv = xT.rearrange("(kc p) t -> p kc t", p=128)
        for nt in range(4):
            ts = slice(nt * 512, (nt + 1) * 512)
            S.op("sp", lambda e, ts=ts: e.dma_start(out=hT[:, :, ts], in_=xv[:, :, ts]),
                 writes=[("hT", nt, kc) for kc in range(KC)], dsem="x%d" % nt)
        S.flush()

        for st in self.stages:
            if st[0] == "mlp":
                self.mlp_phase(st[1], d)
            elif st[0] == "final":
                self.final_phase()
            elif st[0] == "mix":
                li = st[1]
                if li % 3 == 2:
                    self.mamba2_phase(li, d)
                elif li % 3 == 0:
                    self.gdn_phase(li, d)
                else:
                    self.s5_phase(li, d)

        S.barrier()
        yv = yT.rearrange("(kc p) t -> p kc t", p=128)
        outtok = []
        for nt in range(4):
            ts = slice(nt * 512, (nt + 1) * 512)
            S.op("sp", lambda e, ts=ts: e.dma_start(out=yv[:, :, ts], in_=hT[:, :, ts]),
                 writes=[("y", nt)], dsem="y%d" % nt)
            outtok.append(("y", nt))
        S.wait_tokens("sp", outtok)
        S.flush()
        self.stack.close()
        return nc


NV = 80 + 160 + 192
DRAM_INPUTS = [
    ("mlp_w1", [4, D, DFF]), ("mlp_w2", [4, DFF, D]),
    ("m2_w_in", [1, D, 6176]), ("m2_w_out", [1, 2048, D]), ("m2_rows", [128, 96 + 2048]),
    ("gdn_w_in", [2, D, 4112]), ("gdn_w_out", [2, D, D]), ("gdn_rows", [2, 128, 144]),
    ("s5_w_in", [1, D, D]), ("s5_w_out", [1, D, 2 * D]), ("s5_prm", [128, 201]),
    ("s5_bT", [128, 2, 64, 64]), ("s5_cpad", [128, 2, 64, 128]),
]


def make_host_inputs(inp):
    v = np.zeros((128, NV), np.float32)

    def colmajor(a):
        return np.ascontiguousarray(a.reshape(-1, 128).T)
    for l in range(4):
        v[:, l * 8:(l + 1) * 8] = colmajor(inp["norm_mix_g"][l])
        v[:, 32 + l * 8:32 + (l + 1) * 8] = colmajor(inp["norm_mlp_g"][l])
    v[:, 64:72] = colmajor(inp["final_norm_g"])
    cw = inp["m2_conv_w"][0]
    cb = inp["m2_conv_b"][0]
    for j in range(32):
        for tap in range(4):
            v[:, 80 + j * 5 + tap] = cw[tap, j * 128:(j + 1) * 128]
        v[:, 80 + j * 5 + 4] = cb[j * 128:(j + 1) * 128]
    m2_rows = np.concatenate([inp["m2_dt_bias"][0], inp["m2_a_log"][0], inp["m2_d"][0], inp["m2_norm_g"][0]])[None, :]
    m2_rows = np.ascontiguousarray(np.broadcast_to(m2_rows, (128, m2_rows.shape[1]))).astype(np.float32)
    gr = []
    for jl in range(2):
        gcw = inp["gdn_conv_w"][jl]
        for jj in range(24):
            for tap in range(4):
                v[:, 240 + jl * 96 + jj * 4 + tap] = gcw[tap, jj * 128:(jj + 1) * 128]
        r = np.concatenate([inp["gdn_dt_bias"][jl], inp["gdn_a_log"][jl], inp["gdn_o_norm_g"][jl]])[None, :]
        gr.append(np.broadcast_to(r, (128, 144)))
    gdn_rows = np.ascontiguousarray(np.stack(gr, 0)).astype(np.float32)
    lre, lim, ldt = inp["s5_lam_re"][0], inp["s5_lam_im"][0], inp["s5_log_dt"][0]
    prm = np.zeros((128, 201), np.float32)
    prm[:, 0:64] = np.concatenate([lre.T, lre.T], 0)
    prm[:, 64:128] = np.concatenate([lim.T, lim.T], 0)
    prm[:, 128:192] = np.broadcast_to(ldt[None, :], (128, 64))
    prm[0:64, 192] = 1.0
    prm[64:128, 192] = -1.0
    prm[:, 193:201] = inp["s5_d"][0].reshape(8, 128).T
    bre, bim = inp["s5_b_re"][0], inp["s5_b_im"][0]
    cre, cim = inp["s5_c_re"][0], inp["s5_c_im"][0]
    bT = np.zeros((128, 2, 64, 64), np.float32)
    cpad = np.zeros((128, 2, 64, 128), np.float32)
    for g in range(64):
        gl = g % 8
        bT[16 * gl:16 * gl + 16, 0, g, :] = bre[g].T
        bT[16 * gl:16 * gl + 16, 1, g, :] = bim[g].T
        cpad[0:64, 0, g, 16 * gl:16 * gl + 16] = cre[g].T
        cpad[64:128, 0, g, 16 * gl:16 * gl + 16] = cim[g].T
        cpad[0:64, 1, g, 16 * gl:16 * gl + 16] = cim[g].T
        cpad[64:128, 1, g, 16 * gl:16 * gl + 16] = cre[g].T
    out = {"vecs": v, "m2_rows": m2_rows, "gdn_rows": gdn_rows, "s5_prm": prm, "s5_bT": bT, "s5_cpad": cpad}
    for k in ("s5_w_in", "s5_w_out"):
        out[k] = np.ascontiguousarray(inp[k], dtype=np.float32)
    for k in ("mlp_w1", "mlp_w2", "m2_w_in", "m2_w_out", "gdn_w_in", "gdn_w_out"):
        out[k] = np.ascontiguousarray(inp[k], dtype=np.float32)
    return out


ALL_STAGES = [("mix", 0), ("mlp", 0), ("mix", 1), ("mlp", 1), ("mix", 2), ("mlp", 2), ("mix", 3), ("mlp", 3), ("final",)]


def run_stages(stages, xT_list, inp, trace=False):
    prog = Prog(stages)
    nc = prog.build()
    host = make_host_inputs(inp)
    n = len(xT_list)
    in_maps = []
    for c in range(n):
        m = dict(host)
        m["xT"] = np.ascontiguousarray(xT_list[c], dtype=np.float32)
        in_maps.append(m)
    res = run_bass_kernel_spmd(nc, in_maps, core_ids=list(range(n)), trace=trace)
    return [r["yT"] for r in res.results], res


def kernel(**inputs):
    x = inputs["x"]
    xT_list = [np.ascontiguousarray(x[b].T) for b in range(x.shape[0])]
    import os
    stages = ALL_STAGES
    if os.environ.get("KSTAGES"):
        stages = [tuple(t) for t in json.loads(os.environ["KSTAGES"])]
    outs, _ = run_stages(stages, xT_list, inputs)
    return np.stack([np.ascontiguousarray(o.T) for o in outs], axis=0).astype(np.float32)
```

```python
import json
import numpy as np
from contextlib import ExitStack
import concourse.bass as bass
import concourse.mybir as mybir
from concourse.bass_utils import run_bass_kernel_spmd

F32 = mybir.dt.float32
BF16 = mybir.dt.bfloat16
AF = mybir.ActivationFunctionType
ALU = mybir.AluOpType

L = 2048
D = 1024
KC = 8
DFF = 4096
EPS = 1e-6


class Sched:
    ENG = ("pe", "act", "dve", "pool", "sp")

    def __init__(self, nc, stack):
        self.nc = nc
        self.stack = stack
        self.items = {e: [] for e in self.ENG}
        self.sems = {}
        self.val = {}
        self.seen = {e: {} for e in self.ENG}
        self.last_w = {}
        self.readers = {}
        self.epoch = {e: 0 for e in self.ENG}
        self.nsem = 0

    def _sem(self, key):
        if key not in self.sems:
            self.nsem += 1
            self.sems[key] = self.stack.enter_context(self.nc.semaphore("s%d" % self.nsem))
            self.val[key] = 0
        return self.sems[key]

    def _engkey(self, e):
        k = (e, self.epoch[e])
        if self.val.get(k, 0) >= 30000:
            self.epoch[e] += 1
            k = (e, self.epoch[e])
        return k

    def op(self, eng, fn, reads=(), writes=(), dsem=None):
        fns = fn if isinstance(fn, (list, tuple)) else [fn]
        deps = {}

        def add(ev):
            k, v = ev
            if deps.get(k, 0) < v:
                deps[k] = v

        for t in reads:
            if t in self.last_w:
                add(self.last_w[t])
            if isinstance(t, tuple) and t[0] == "ps":
                for k, v in self.readers.get(t, {}).items():
                    add((k, v))
        for t in writes:
            if t in self.last_w:
                add(self.last_w[t])
            for k, v in self.readers.get(t, {}).items():
                add((k, v))
        if dsem is None:
            key = self._engkey(eng)
            inc = 1
        else:
            key = ("dma", dsem)
            inc = 16
        self._sem(key)
        self.val[key] += inc
        ev = (key, self.val[key])
        waits = []
        for k, v in deps.items():
            if self.seen[eng].get(k, 0) >= v:
                continue
            self.seen[eng][k] = v
            waits.append((self.sems[k], v, k))
        self.items[eng].append((waits, fns, self.sems[key], inc, key))
        for t in writes:
            self.last_w[t] = ev
            self.readers[t] = {}
        for t in reads:
            r = self.readers.setdefault(t, {})
            if r.get(ev[0], 0) < ev[1]:
                r[ev[0]] = ev[1]
        return ev

    def wait_tokens(self, eng, tokens):
        waits = []
        for t in tokens:
            if t in self.last_w:
                k, v = self.last_w[t]
                if self.seen[eng].get(k, 0) < v:
                    self.seen[eng][k] = v
                    waits.append((self.sems[k], v, k))
        self.items[eng].append((waits, [], None, 0, None))

    def barrier(self):
        snap = dict(self.val)
        for e in self.ENG:
            waits = []
            for k, v in snap.items():
                if v > 0 and self.seen[e].get(k, 0) < v:
                    self.seen[e][k] = v
                    waits.append((self.sems[k], v, k))
            self.items[e].append((waits, [], None, 0, None))
        self.last_w = {}
        self.readers = {}

    def flush(self):
        self.simulate()
        nc = self.nc
        with nc.Block() as block:
            @block.tensor
            def _(e):
                self.emit("pe", e)

            @block.scalar
            def _(e):
                self.emit("act", e)

            @block.vector
            def _(e):
                self.emit("dve", e)

            @block.gpsimd
            def _(e):
                self.emit("pool", e)

            @block.sync
            def _(e):
                self.emit("sp", e)
        self.simvals = getattr(self, "simvals", None)
        self.items = {e: [] for e in self.ENG}

    def simulate(self):
        val = dict(getattr(self, "_simval", {}))
        for k in self.sems:
            val.setdefault(k, 0)
        pc = {e: 0 for e in self.ENG}
        progress = True
        while progress:
            progress = False
            for e in self.ENG:
                while pc[e] < len(self.items[e]):
                    waits, fns, sem, inc, key = self.items[e][pc[e]]
                    if all(val[k] >= v for _, v, k in waits):
                        if key is not None:
                            val[key] += inc
                        pc[e] += 1
                        progress = True
                    else:
                        break
        stuck = {e: (pc[e], len(self.items[e])) for e in self.ENG if pc[e] < len(self.items[e])}
        if stuck:
            for e in stuck:
                waits = self.items[e][pc[e]][0]
                print("STUCK", e, pc[e], [(k, v, val[k]) for _, v, k in waits if val[k] < v])
            raise RuntimeError("schedule deadlock: %s" % stuck)
        self._simval = val

    def emit(self, name, e):
        for waits, fns, sem, inc, _k in self.items[name]:
            for s, v, _kk in waits:
                e.wait_ge(s, v)
            last = None
            for f in fns:
                last = f(e)
            if last is not None and sem is not None:
                last.then_inc(sem, inc)


class PsumPool:
    def __init__(self, nc, stack, n=8):
        self.tiles = [stack.enter_context(nc.psum_tensor("ps%d" % i, [128, 512], F32)) for i in range(n)]
        self.i = 0
        self.n = n
        self.lo, self.hi = 0, n

    def get(self):
        if not (self.lo <= self.i < self.hi):
            self.i = self.lo
        i = self.i
        self.i = self.i + 1
        if self.i >= self.hi:
            self.i = self.lo
        return self.tiles[i], ("ps", i)


class Prog:
    def __init__(self, stages):
        self.stages = stages
        nc = bass.Bass("TRN2", target_bir_lowering=False)
        self.nc = nc
        self.stack = ExitStack()
        self.S = Sched(nc, self.stack)
        self.dram = {}

    def din(self, name, shape, dt=F32):
        t = self.nc.dram_tensor(name, list(shape), dt, kind="ExternalInput")
        self.dram[name] = t
        return t.ap()

    def sb(self, name, shape, dt):
        return self.stack.enter_context(self.nc.sbuf_tensor("sb_" + name, list(shape), dt))

    def rmsnorm_T(self, gcol, out_bf, tag):
        S, nc = self.S, self.nc
        hT, sq, ones = self.hT, self.sq, self.ones_bf
        for nt in range(4):
            ts = slice(nt * 512, (nt + 1) * 512)
            S.op("act", lambda e, ts=ts: e.activation(out=sq[:, :, :], in_=hT[:, :, ts], func=AF.Square),
                 reads=[("hT", nt, kc) for kc in range(KC)], writes=["sq"])
            ps, pt = self.PS.get()
            fns = [lambda e, kc=kc, ps=ps: e.matmul(ps[:, :], lhsT=ones[:, :], rhs=sq[:, kc, :],
                                                       start=(kc == 0), stop=(kc == KC - 1)) for kc in range(KC)]
            S.op("pe", fns, reads=["sq", "ones"], writes=[pt])
            rs = self.rstd
            S.op("act", lambda e, ps=ps: e.activation(out=rs[:, :], in_=ps[:, :], func=AF.Sqrt,
                                                     scale=1.0 / D, bias=self.eps_col[:, 0:1]),
                 reads=[pt, "eps"], writes=["rstd"])
            S.op("dve", lambda e: e.reciprocal(out=rs[:, :], in_=rs[:, :]), reads=["rstd"], writes=["rstd"])
            for kc in range(KC):
                S.op("dve", lambda e, kc=kc, ts=ts: e.scalar_tensor_tensor(
                    out=out_bf[:, kc, ts], in0=hT[:, kc, ts], scalar=gcol(kc), in1=rs[:, :],
                    op0=ALU.mult, op1=ALU.mult),
                    reads=[("hT", nt, kc), "rstd", "vecs"], writes=[(tag, nt, kc)])

    def mlp(self, l, w1_d, w2_d):
        S, nc = self.S, self.nc
        hT, hn = self.hT, self.hn
        gcol = lambda kc: self.vecs[:, 32 + l * 8 + kc:32 + l * 8 + kc + 1]
        w1v = w1_d[l].rearrange("(kc p) f -> p kc f", p=128)
        w2v = w2_d[l].rearrange("(fc p) d -> p fc d", p=128)
        NG = 8

        def load_w1(g):
            b = g % 2
            S.op("pool", lambda e: e.dma_start(out=self.w1b[b][:, :, :], in_=w1v[:, :, g * 512:(g + 1) * 512]),
                 writes=[("w1b", b)], dsem="w1b%d" % b)

        def load_w2(g):
            b = g % 2
            S.op("pool", lambda e: e.dma_start(out=self.w2b[b][:, :, :], in_=w2v[:, g * 4:(g + 1) * 4, :]),
                 writes=[("w2b", b)], dsem="w2b%d" % b)

        def W1(g):
            b = g % 2
            w1 = self.w1b[b]
            aT = self.aT[b]
            for nt in range(4):
                ts = slice(nt * 512, (nt + 1) * 512)
                for mc in range(4):
                    ps, pt = self.PS.get()
                    fns = [lambda e, kc=kc, ps=ps, mc=mc, ts=ts: e.matmul(
                        ps[:, :], lhsT=w1[:, kc, mc * 128:(mc + 1) * 128], rhs=hn[:, kc, ts],
                        start=(kc == 0), stop=(kc == KC - 1)) for kc in range(KC)]
                    S.op("pe", fns, reads=[("w1b", b)] + [("hn", nt, kc) for kc in range(KC)], writes=[pt])
                    r = self.rbuf[self.ri % 2]
                    rt = ("rbuf", self.ri % 2)
                    self.ri += 1
                    S.op("act", lambda e, ps=ps, r=r: e.activation(out=r[:, :], in_=ps[:, :], func=AF.Relu),
                         reads=[pt], writes=[rt])
                    S.op("act", lambda e, r=r, mc=mc, ts=ts: e.activation(out=aT[:, mc, ts], in_=r[:, :], func=AF.Square),
                         reads=[rt], writes=[("aT", b, nt, mc)])

        def W2(g):
            b = g % 2
            w2 = self.w2b[b]
            aT = self.aT[b]
            for nt in range(4):
                ts = slice(nt * 512, (nt + 1) * 512)
                for dc in range(KC):
                    ps, pt = self.PS.get()
                    fns = [lambda e, mc=mc, ps=ps, dc=dc, ts=ts: e.matmul(
                        ps[:, :], lhsT=w2[:, mc, dc * 128:(dc + 1) * 128], rhs=aT[:, mc, ts],
                        start=(mc == 0), stop=(mc == 3)) for mc in range(4)]
                    S.op("pe", fns, reads=[("w2b", b)] + [("aT", b, nt, mc) for mc in range(4)], writes=[pt])
                    S.op("dve", lambda e, ps=ps, dc=dc, ts=ts: e.tensor_tensor(
                        out=hT[:, dc, ts], in0=ps[:, :], in1=hT[:, dc, ts], op=ALU.add),
                        reads=[pt, ("hT", nt, dc)], writes=[("hT", nt, dc)])

        import os
        dbg = int(os.environ.get("KDBG", "9"))
        load_w1(0); load_w2(0); load_w1(1); load_w2(1)
        if dbg <= 1:
            return
        self.rmsnorm_T(gcol, hn, "hn")
        W1(0)
        if dbg <= 2:
            return
        if dbg == 3:
            NG = 2
        if dbg == 4:
            NG = 3
        for g in range(NG):
            if g + 1 < NG:
                W1(g + 1)
            if g + 2 < NG:
                load_w1(g + 2)
            W2(g)
            if g + 2 < NG:
                load_w2(g + 2)

    def final(self):
        S = self.S
        hT = self.hT
        rs = self.rstd
        gcol = lambda kc: self.vecs[:, 64 + kc:64 + kc + 1]
        self.rmsnorm_T(gcol, hT, "hT")

    def phase_begin(self):
        self.ph = ExitStack()
        self.S.barrier()

    def phase_end(self):
        self.S.flush()
        self.ph.close()

    def A(self, name, shape, dt):
        self.nbuf = getattr(self, "nbuf", 0) + 1
        return self.ph.enter_context(self.nc.sbuf_tensor("p%d_%s" % (self.nbuf, name), list(shape), dt))

    def mlp_phase(self, l, d):
        self.phase_begin()
        self.w1b = [self.A("w1b%d" % i, [128, KC, 512], BF16) for i in range(2)]
        self.w2b = [self.A("w2b%d" % i, [128, 4, D], BF16) for i in range(2)]
        self.aT = [self.A("aT%d" % i, [128, 4, L], BF16) for i in range(2)]
        self.rbuf = [self.A("rbuf%d" % i, [128, 512], BF16) for i in range(2)]
        self.ri = 0
        self.mlp(l, d["mlp_w1"], d["mlp_w2"])
        self.phase_end()

    def final_phase(self):
        self.phase_begin()
        self.final()
        self.phase_end()

    def mamba2_phase(self, li, d):
        self.phase_begin()
        import os
        stop = float(os.environ.get("KDBG2", "99"))
        S, nc, hT, hn, vecs = self.S, self.nc, self.hT, self.hn, self.vecs
        A = self.A
        U, SL, ident, onesf = self.U, self.SL, self.ident_bf, self.ones_f
        w_in = d["m2_w_in"][0].rearrange("(kc p) f -> p kc f", p=128)
        w_out = d["m2_w_out"][0].rearrange("(c p) f -> p c f", p=128)
        gcol = lambda kc: vecs[:, li * 8 + kc:li * 8 + kc + 1]
        self.rmsnorm_T(gcol, hn, "hn")
        HN = [("hn", nt, kc) for nt in range(4) for kc in range(KC)]

        rows = A("rows", [128, 96], F32)
        ngd = d["m2_rows"][:, 96:96 + 2048]
        ng = [A("ng%d" % i, [128, 256], F32) for i in range(2)]
        S.op("sp", lambda e: e.dma_start(out=rows[:, :], in_=d["m2_rows"][:, 0:96]), writes=["rows"], dsem="rows")
        dtb, alog, Dh = rows[:, 0:32], rows[:, 32:64], rows[:, 64:96]
        wdt = A("wdt", [128, KC, 32], BF16)
        S.op("pool", lambda e: e.dma_start(out=wdt[:, :, :], in_=w_in[:, :, 6144:6176]), writes=["wdt"], dsem="wdt")
        if stop <= 1:
            self.phase_end()
            return
        wxbc = A("wxbc", [128, KC, 512], BF16)
        wz = [A("wz0", [128, KC, 256], BF16)] * 2
        wo = [A("wo%d" % i, [128, 2, D], BF16) for i in range(1)]

        def dma(eng, out, in_, wtok, sem, reads=()):
            S.op(eng, lambda e: e.dma_start(out=out, in_=in_), reads=list(reads), writes=[wtok], dsem=sem)

        def load_group(g):
            b = g % 2
            dma("pool", wxbc[:, :, 0:256], w_in[:, :, 2048 + 256 * g:2048 + 256 * (g + 1)], ("wxbc", 0), "wxbc0")
            dma("pool", wxbc[:, :, 256:384], w_in[:, :, 4096 + 128 * g:4096 + 128 * (g + 1)], ("wxbc", 1), "wxbc1")
            dma("pool", wxbc[:, :, 384:512], w_in[:, :, 5120 + 128 * g:5120 + 128 * (g + 1)], ("wxbc", 2), "wxbc2")
            dma("pool", wz[0][:, :, :], w_in[:, :, 256 * g:256 * (g + 1)], "wz", "wz0")
            dma("pool", wo[0][:, :, :], w_out[:, 2 * g:2 * g + 2, :], ("wo", 0), "wo0")
            dma("sp", ng[b][:, :], ngd[:, 256 * g:256 * (g + 1)], ("ng", b), "ng%d" % b)

        dt = A("dt", [128, 16, 32], F32)
        dA = A("dA", [128, 16, 32], F32)
        ecum = A("ecum", [128, 16, 32], F32)
        ed = A("ed", [128, 16, 32], F32)
        cdrep = A("cdrep", [128, 16, 32], F32)
        negA = A("negA", [128, 32], F32)
        HL = L // 2
        cbufs = [A("cbufb0", [128, 3 + L], BF16)] * 2
        dgs = [[A("dg0_%d" % j, [128, 128], BF16) for j in range(4)]] * 2
        accc = A("accc", [128, 512], F32)
        cum = accc[:, :].rearrange("p (c h) -> p c h", h=32)
        ps, pt = self.PS.get()
        for c in range(16):
            fns = [lambda e, kc=kc, c=c, ps=ps: e.matmul(ps[:, c * 32:(c + 1) * 32], lhsT=hn[:, kc, c * 128:(c + 1) * 128],
                                                        rhs=wdt[:, kc, :], start=(kc == 0), stop=(kc == KC - 1))
                   for kc in range(KC)]
            S.op("pe", fns, reads=HN + ["wdt"], writes=[pt])
        psv = lambda p: p[:, :].rearrange("p (c h) -> p c h", h=32)
        S.op("dve", lambda e, ps=ps: e.tensor_tensor(out=dt[:, :, :], in0=psv(ps), in1=dtb.unsqueeze(1).to_broadcast([128, 16, 32]),
                                                    op=ALU.add), reads=[pt, "rows"], writes=["dt"])
        if stop <= 1.1:
            self.phase_end()
            return
        S.op("act", lambda e: e.activation(out=dt[:, :, :], in_=dt[:, :, :], func=AF.Exp), reads=["dt"], writes=["dt"])
        S.op("act", lambda e: e.activation(out=dt[:, :, :], in_=dt[:, :, :], func=AF.Ln, bias=1.0), reads=["dt"], writes=["dt"])
        if stop <= 1.2:
            self.phase_end()
            return
        S.op("act", lambda e: e.activation(out=negA[:, :], in_=alog, func=AF.Exp), reads=["rows"], writes=["negA"])
        S.op("dve", lambda e: e.tensor_scalar(out=negA[:, :], in0=negA[:, :], scalar1=-1.0, scalar2=None, op0=ALU.mult),
             reads=["negA"], writes=["negA"])
        S.op("dve", lambda e: e.tensor_tensor(out=dA[:, :, :], in0=dt[:, :, :], in1=negA[:, :].unsqueeze(1).to_broadcast([128, 16, 32]),
                                              op=ALU.mult), reads=["dt", "negA"], writes=["dA"])
        if stop <= 1.3:
            self.phase_end()
            return
        ps1, pt1 = self.PS.get()
        ps2, pt2 = self.PS.get()
        S.op("pe", [lambda e, c=c: e.matmul(ps1[:, c * 32:(c + 1) * 32], lhsT=U[:, :], rhs=dA[:, c, :], start=True, stop=True)
                    for c in range(16)], reads=["dA", "U"], writes=[pt1])
        S.op("pe", [lambda e, c=c: e.matmul(ps2[:, c * 32:(c + 1) * 32], lhsT=onesf[:, :], rhs=dA[:, c, :], start=True, stop=True)
                    for c in range(16)], reads=["dA", "onesf"], writes=[pt2])
        if stop <= 1.4:
            self.phase_end()
            return
        S.op("act", lambda e: e.activation(out=cum, in_=psv(ps1), func=AF.Identity), reads=[pt1], writes=["cum"])
        if stop <= 1.5:
            self.phase_end()
            return
        S.op("act", lambda e: e.activation(out=ecum[:, :, :], in_=psv(ps1), func=AF.Exp), reads=[pt1], writes=["ecum"])
        if stop <= 1.6:
            self.phase_end()
            return

        S.op("act", lambda e: e.activation(out=cdrep[:, :, :], in_=psv(ps2), func=AF.Exp), reads=[pt2], writes=["cdrep"])
        if stop <= 1.7:
            self.phase_end()
            return
        S.op("dve", lambda e: e.tensor_tensor(out=ed[:, :, :], in0=psv(ps2), in1=cum, op=ALU.subtract),
             reads=[pt2, "cum"], writes=["ed"])
        if stop <= 1.8:
            self.phase_end()
            return
        S.op("act", lambda e: e.activation(out=ed[:, :, :], in_=ed[:, :, :], func=AF.Exp), reads=["ed"], writes=["ed"])
        DEC = ["dt", "dA", "cum", "ecum", "ed", "cdrep"]
        if stop <= 2:
            self.phase_end()
            return

        fm = [A("fm%d" % i, [128, L], BF16) for i in range(4)]
        xB = A("xB", [128, 16, 384], BF16)
        yTg = A("yTg", [128, 2, L], BF16)
        S32 = A("S32", [128, 256], F32)
        Sbf = A("Sbf", [128, 256], BF16)
        zsall = A("zsall", [128, 16, 256], BF16)
        cbU = [A("cbU0", [128, 128], F32)] * 2
        rhsD = [A("rhsD0", [128, 4, 128], F32)] * 2
        E = [A("E0", [128, 4, 128], F32)] * 2
        Mt = [A("Mt%d" % i, [128, 4, 128], BF16) for i in range(2)]
        xdt = [A("xdt%d" % i, [128, 256], BF16) for i in range(2)]
        xdtd = [A("xdtd%d" % i, [128, 256], BF16) for i in range(2)]
        t1 = [A("t1%d" % i, [128, 256], F32) for i in range(2)]
        t2 = [A("t2%d" % i, [128, 256], F32) for i in range(1)] * 2
        yn = [A("yn%d" % i, [128, 256], BF16) for i in range(2)]
        ss = [A("ss%d" % i, [128, 1], F32) for i in range(2)]
        S.op("dve", lambda e: e.memset(cbufs[0][:, 0:3], 0.0), writes=[("cbuf", 0, "pad")])

        load_group(0)
        def group(g):
            b = g % 2
            for ch in range(4):
                jj = [2 * g, 2 * g + 1, 16 + g, 24 + g][ch]
                cw = lambda tap, jj=jj: vecs[:, 80 + jj * 5 + tap:80 + jj * 5 + tap + 1]
                cbi = 0
                cbuf = cbufs[cbi]
                dg = dgs[cbi]
                for tap in range(4):
                    S.op("dve", lambda e, tap=tap, dg=dg, cw=cw: e.tensor_scalar(out=dg[tap][:, :], in0=ident[:, :], scalar1=cw(tap), scalar2=None, op0=ALU.mult),
                         reads=["ident", "vecs"], writes=[("dg", cbi, tap)])
                for nt in range(4):
                    ts = slice(nt * 512, (nt + 1) * 512)
                    ps, pt = self.PS.get()
                    S.op("pe", [lambda e, kc=kc, ps=ps, ch=ch, ts=ts: e.matmul(
                        ps[:, :], lhsT=wxbc[:, kc, ch * 128:(ch + 1) * 128], rhs=hn[:, kc, ts],
                        start=(kc == 0), stop=(kc == KC - 1)) for kc in range(KC)],
                        reads=HN + [("wxbc", 0), ("wxbc", 1), ("wxbc", 2)], writes=[pt])
                    S.op("act", lambda e, ps=ps, nt=nt, cbuf=cbuf: e.activation(out=cbuf[:, 3 + nt * 512:3 + (nt + 1) * 512], in_=ps[:, :], func=AF.Identity),
                         reads=[pt], writes=[("cbuf", cbi, nt)])
                for nt in range(4):
                    ps, pt = self.PS.get()
                    rd = [("cbuf", cbi, nt), ("cbuf", cbi, "pad")] + ([("cbuf", cbi, nt - 1)] if nt > 0 else []) + [("dg", cbi, t_) for t_ in range(4)]
                    S.op("pe", [lambda e, tap=tap, ps=ps, nt=nt, cbuf=cbuf, dg=dg: e.matmul(
                        ps[:, :], lhsT=dg[tap][:, :], rhs=cbuf[:, tap + nt * 512:tap + (nt + 1) * 512],
                        start=(tap == 0), stop=(tap == 3)) for tap in range(4)], reads=rd, writes=[pt])
                    S.op("act", lambda e, ps=ps, nt=nt, ch=ch, cw=cw: e.activation(out=fm[ch][:, nt * 512:(nt + 1) * 512], in_=ps[:, :], func=AF.Silu, bias=cw(4)),
                         reads=[pt, "vecs"], writes=[("fm", ch, nt // 2, nt % 2)])
            if g + 1 < 8:
                pass
            if stop <= 3:
                return
            for c in range(16):
                ct = slice(c * 128, (c + 1) * 128)
                ps, pt = self.PS.get()
                S.op("pe", [lambda e, ch=ch, ps=ps, ct=ct: e.matmul(ps[:, ch * 128:(ch + 1) * 128], lhsT=fm[ch][:, ct], rhs=ident[:, :],
                                                                   start=True, stop=True) for ch in range(3)],
                     reads=[("fm", ch, hf, q) for ch in range(3) for hf in range(2) for q in range(2)] + ["ident"], writes=[pt])
                S.op("act", lambda e, ps=ps, c=c: e.activation(out=xB[:, c, :], in_=ps[:, 0:384], func=AF.Identity),
                     reads=[pt], writes=[("xB", c)])
            S.op("dve", lambda e: e.memset(S32[:, :], 0.0), writes=["S32"])
            S.op("dve", lambda e: e.memset(Sbf[:, :], 0.0), writes=["Sbf"])
            if stop <= 4:
                return
            hs = slice(4 * g, 4 * g + 4)
            for cq in range(8):
                ps, pt = self.PS.get()
                for cc in range(2):
                    c_ = 2 * cq + cc
                    S.op("pe", [lambda e, kc=kc, ps=ps, cc=cc, c_=c_: e.matmul(ps[:, cc * 256:(cc + 1) * 256], lhsT=hn[:, kc, c_ * 128:(c_ + 1) * 128],
                                                                             rhs=wz[0][:, kc, :], start=(kc == 0), stop=(kc == KC - 1)) for kc in range(KC)],
                         reads=HN + ["wz"], writes=[pt])
                S.op("act", lambda e, ps=ps, cq=cq: e.activation(out=zsall[:, 2 * cq:2 * cq + 2, :], in_=ps[:, :].rearrange("p (c v) -> p c v", v=256), func=AF.Silu),
                     reads=[pt], writes=["zsall"])
            v3 = lambda ap: ap.rearrange("p (h q) -> p h q", q=64)
            cur = {"ops": None}

            def sop(eng, fn, reads=(), writes=()):
                cur["ops"].append((eng, fn, list(reads), list(writes)))

            def chunk(c):
                ct = slice(c * 128, (c + 1) * 128)
                i = c % 2
                prep, tail = [], []
                cur["ops"] = prep
                self.PS.lo, self.PS.hi = 0, 4
                ps, pt = self.PS.get()
                sop("pe", lambda e, ps=ps: e.matmul(ps[:, 0:128], lhsT=fm[2][:, ct], rhs=fm[3][:, ct], start=True, stop=True),
                    reads=[("fm", ch, hf, q) for ch in (2, 3) for hf in range(2) for q in range(2)], writes=[pt])
                sop("dve", lambda e, ps=ps: e.tensor_tensor(out=cbU[i][:, :], in0=ps[:, 0:128], in1=U[:, :], op=ALU.mult),
                    reads=[pt, "U"], writes=["cbU"])
                sop("dve", lambda e: e.tensor_tensor(
                    out=rhsD[i][:, :, :], in0=U[:, :].unsqueeze(1).to_broadcast([128, 4, 128]),
                    in1=dA[:, c, hs].unsqueeze(2).to_broadcast([128, 4, 128]), op=ALU.mult),
                    reads=["U", "dA"], writes=["rhsD"])
                psD, ptD = self.PS.get()
                sop("pe", lambda e: e.matmul(psD[:, :], lhsT=SL[:, :], rhs=rhsD[i][:, :, :].rearrange("p h l -> p (h l)"),
                                             start=True, stop=True), reads=["rhsD", "SL"], writes=[ptD])
                sop("act", lambda e: e.activation(out=E[i][:, :, :].rearrange("p h l -> p (h l)"), in_=psD[:, :], func=AF.Exp),
                    reads=[ptD], writes=["E"])
                sop("dve", lambda e: e.tensor_tensor(out=Mt[i][:, :, :], in0=E[i][:, :, :],
                                                     in1=cbU[i][:, :].unsqueeze(1).to_broadcast([128, 4, 128]), op=ALU.mult),
                    reads=["E", "cbU"], writes=[("Mt", i)])
                sop("dve", lambda e: e.tensor_tensor(
                    out=v3(xdt[i][:, :]), in0=v3(xB[:, c, 0:256]),
                    in1=dt[:, c, hs].unsqueeze(2).to_broadcast([128, 4, 64]), op=ALU.mult),
                    reads=[("xB", c), "dt"], writes=[("xdt", i)])
                sop("dve", lambda e: e.tensor_tensor(
                    out=v3(xdtd[i][:, :]), in0=v3(xdt[i][:, :]),
                    in1=ed[:, c, hs].unsqueeze(2).to_broadcast([128, 4, 64]), op=ALU.mult),
                    reads=[("xdt", i), "ed"], writes=[("xdtd", i)])
                psy, pty = self.PS.get()
                sop("pe", [lambda e, h=h: e.matmul(psy[:, h * 64:(h + 1) * 64], lhsT=Mt[i][:, h, :], rhs=xdt[i][:, h * 64:(h + 1) * 64],
                                                  start=True, stop=True) for h in range(4)],
                    reads=[("Mt", i), ("xdt", i)], writes=[pty])
                sop("dve", lambda e: e.tensor_tensor(
                    out=v3(t2[i][:, :]), in0=v3(xB[:, c, 0:256]), in1=Dh[:, hs].unsqueeze(2).to_broadcast([128, 4, 64]), op=ALU.mult),
                    reads=[("xB", c), "rows"], writes=[("t2", i)])
                sop("dve", lambda e: e.tensor_tensor(out=t2[i][:, :], in0=psy[:, 0:256], in1=t2[i][:, :], op=ALU.add),
                    reads=[pty, ("t2", i)], writes=[("t2", i)])
                cur["ops"] = tail
                self.PS.lo, self.PS.hi = 4, 8
                pso, pto = self.PS.get()
                sop("pe", lambda e: e.matmul(pso[:, 0:256], lhsT=fm[3][:, ct], rhs=Sbf[:, :], start=True, stop=True),
                    reads=[("fm", 3, hf, q) for hf in range(2) for q in range(2)] + ["Sbf"], writes=[pto])
                pss, pts = self.PS.get()
                sop("pe", lambda e: e.matmul(pss[:, 0:256], lhsT=xB[:, c, 256:384], rhs=xdtd[i][:, :], start=True, stop=True),
                    reads=[("xB", c), ("xdtd", i)], writes=[pts])
                sop("dve", lambda e: e.tensor_tensor(
                    out=v3(t1[i][:, :]), in0=v3(pso[:, 0:256]), in1=ecum[:, c, hs].unsqueeze(2).to_broadcast([128, 4, 64]), op=ALU.mult),
                    reads=[pto, "ecum"], writes=[("t1", i)])
                sop("dve", lambda e: e.tensor_tensor(out=v3(S32[:, :]), in0=v3(S32[:, :]),
                                                     in1=cdrep[:, c, hs].unsqueeze(2).to_broadcast([128, 4, 64]), op=ALU.mult),
                    reads=["S32", "cdrep"], writes=["S32"])
                sop("dve", lambda e: e.tensor_tensor(out=S32[:, :], in0=pss[:, 0:256], in1=S32[:, :], op=ALU.add),
                    reads=[pts, "S32"], writes=["S32"])
                sop("act", lambda e: e.activation(out=Sbf[:, :], in_=S32[:, :], func=AF.Identity), reads=["S32"], writes=["Sbf"])
                sop("dve", lambda e: e.tensor_tensor(out=t1[i][:, :], in0=t1[i][:, :], in1=t2[i][:, :], op=ALU.add),
                    reads=[("t1", i), ("t2", i)], writes=[("t1", i)])
                sop("dve", lambda e: e.tensor_tensor(out=t1[i][:, :], in0=t1[i][:, :], in1=zsall[:, c, :], op=ALU.mult),
                    reads=[("t1", i), "zsall"], writes=[("t1", i)])
                sop("act", lambda e: e.activation(out=yn[i][:, :], in_=t1[i][:, :], func=AF.Square, accum_out=ss[i][:, 0:1]),
                    reads=[("t1", i)], writes=[("yn", i), ("ss", i)])
                sop("act", lambda e: e.activation(out=ss[i][:, :], in_=ss[i][:, :], func=AF.Ln, scale=1.0 / 256, bias=self.eps_col[:, 0:1]),
                    reads=[("ss", i), "eps"], writes=[("ss", i)])
                sop("act", lambda e: e.activation(out=ss[i][:, :], in_=ss[i][:, :], func=AF.Exp, scale=-0.5), reads=[("ss", i)], writes=[("ss", i)])
                sop("dve", lambda e: e.scalar_tensor_tensor(
                    out=yn[i][:, :], in0=t1[i][:, :], scalar=ss[i][:, 0:1], in1=ng[b][:, :], op0=ALU.mult, op1=ALU.mult),
                    reads=[("t1", i), ("ss", i), ("ng", b)], writes=[("yn", i)])
                psT, ptT = self.PS.get()
                sop("pe", [lambda e, j=j: e.matmul(psT[:, j * 128:(j + 1) * 128], lhsT=yn[i][:, j * 128:(j + 1) * 128], rhs=ident[:, :],
                                                  start=True, stop=True) for j in range(2)],
                    reads=[("yn", i), "ident"], writes=[ptT])
                sop("act", lambda e: e.activation(out=yTg[:, :, ct], in_=psT[:, 0:256].rearrange("p (j l) -> p j l", l=128), func=AF.Identity),
                    reads=[ptT], writes=[("yTg", c)])
                return prep, tail

            def zipl(x, y):
                out = []
                for k in range(max(len(x), len(y))):
                    if k < len(x):
                        out.append(x[k])
                    if k < len(y):
                        out.append(y[k])
                return out

            pts_ = [chunk(c) for c in range(16)]
            self.PS.lo, self.PS.hi = 0, 8
            stream = list(pts_[0][0])
            for c in range(16):
                stream += zipl(pts_[c][1], pts_[c + 1][0] if c + 1 < 16 else [])
            for eng, fn, r, w in stream:
                S.op(eng, fn, reads=r, writes=w)
            YT = [("yTg", c) for c in range(16)]
            for nt in range(4):
                ts = slice(nt * 512, (nt + 1) * 512)
                for dc in range(KC):
                    ps, pt = self.PS.get()
                    S.op("pe", [lambda e, j=j, ps=ps, dc=dc, ts=ts: e.matmul(ps[:, :], lhsT=wo[0][:, j, dc * 128:(dc + 1) * 128], rhs=yTg[:, j, ts],
                                                                            start=(j == 0), stop=(j == 1)) for j in range(2)],
                         reads=YT + [("wo", 0)], writes=[pt])
                    S.op("dve", lambda e, ps=ps, dc=dc, ts=ts: e.tensor_tensor(out=hT[:, dc, ts], in0=ps[:, :], in1=hT[:, dc, ts], op=ALU.add),
                         reads=[pt, ("hT", nt, dc)], writes=[("hT", nt, dc)])
        for g in range(8 if stop > 7 else 1):
            group(g)
            if g + 1 < 8 and stop > 7:
                load_group(g + 1)
        self.phase_end()

    def gdn_phase(self, li, d):
        self.phase_begin()
        S, nc, hT, hn, vecs = self.S, self.nc, self.hT, self.hn, self.vecs
        A = self.A
        U, SL, ident, identf, onesf, ones_bf = self.U, self.SL, self.ident_bf, self.ident_f, self.ones_f, self.ones_bf
        jl = li // 3
        w_in = d["gdn_w_in"][jl].rearrange("(kc p) f -> p kc f", p=128)
        w_out = d["gdn_w_out"][jl]
        gcol = lambda kc: vecs[:, li * 8 + kc:li * 8 + kc + 1]
        self.rmsnorm_T(gcol, hn, "hn")
        HN = [("hn", nt, kc) for nt in range(4) for kc in range(KC)]
        CW0 = 240 + jl * 96

        def dma(eng, out, in_, wtok, sem, reads=()):
            S.op(eng, lambda e: e.dma_start(out=out, in_=in_), reads=list(reads), writes=[wtok], dsem=sem)

        rows = A("rows", [128, 144], F32)
        dma("sp", rows[:, :], d["gdn_rows"][jl], "rows", "rows")
        dtb, alog, ong = rows[:, 0:8], rows[:, 8:16], rows[:, 16:144]
        wab = A("wab", [128, KC, 16], BF16)
        dma("pool", wab[:, :, :], w_in[:, :, 4096:4112], "wab", "wab")
        wqk = [A("wqkv%d" % i, [128, KC, 128], BF16) for i in range(2)]
        wgate = [A("wgate0", [128, KC, 128], BF16)] * 2
        wos = [A("wo%d" % i, [128, D], BF16) for i in range(2)]

        def load_head(h, slot):
            b = slot
            wo = wos[slot]
            for q in range(2):
                dma("pool", wqk[q % 2][:, :, :], w_in[:, :, q * 1024 + 128 * h:q * 1024 + 128 * (h + 1)], ("wqkv", q % 2), "wqkv%d" % (q % 2))
            dma("pool", wgate[0][:, :, :], w_in[:, :, 3072 + 128 * h:3072 + 128 * (h + 1)], "wgate", "wgate0")
            dma("pool", wo[:, :], w_out[128 * h:128 * (h + 1), :], (slot, "wo"), "wo%d" % slot)

        F3 = lambda nm: A(nm, [128, 16, 8], F32)
        gg, beta, nbeg, eG, ed, cdrep, G = F3("gg"), F3("beta"), F3("nbeg"), F3("eG"), F3("ed"), F3("cdrep"), F3("G")
        negA = A("negA", [128, 8], F32)
        ps, pt = self.PS.get()
        for c in range(16):
            S.op("pe", [lambda e, kc=kc, c=c, ps=ps: e.matmul(ps[:, c * 16:(c + 1) * 16], lhsT=hn[:, kc, c * 128:(c + 1) * 128],
                                                            rhs=wab[:, kc, :], start=(kc == 0), stop=(kc == KC - 1)) for kc in range(KC)],
                 reads=HN + ["wab"], writes=[pt])
        pab = ps[:, 0:256].rearrange("p (c t) -> p c t", t=16)
        S.op("dve", lambda e: e.tensor_tensor(out=gg[:, :, :], in0=pab[:, :, 0:8], in1=dtb.unsqueeze(1).to_broadcast([128, 16, 8]), op=ALU.add),
             reads=[pt, "rows"], writes=["gg"])
        S.op("act", lambda e: e.activation(out=beta[:, :, :], in_=pab[:, :, 8:16], func=AF.Sigmoid), reads=[pt], writes=["beta"])
        S.op("act", lambda e: e.activation(out=gg[:, :, :], in_=gg[:, :, :], func=AF.Exp), reads=["gg"], writes=["gg"])
        S.op("act", lambda e: e.activation(out=gg[:, :, :], in_=gg[:, :, :], func=AF.Ln, bias=1.0), reads=["gg"], writes=["gg"])
        S.op("act", lambda e: e.activation(out=negA[:, :], in_=alog, func=AF.Exp), reads=["rows"], writes=["negA"])
        S.op("dve", lambda e: e.tensor_scalar(out=negA[:, :], in0=negA[:, :], scalar1=-1.0, scalar2=None, op0=ALU.mult),
             reads=["negA"], writes=["negA"])
        S.op("dve", lambda e: e.tensor_tensor(out=gg[:, :, :], in0=gg[:, :, :], in1=negA[:, :].unsqueeze(1).to_broadcast([128, 16, 8]), op=ALU.mult),
             reads=["gg", "negA"], writes=["gg"])
        ps1, pt1 = self.PS.get()
        ps2, pt2 = self.PS.get()
        S.op("pe", [lambda e, c=c: e.matmul(ps1[:, c * 8:(c + 1) * 8], lhsT=U[:, :], rhs=gg[:, c, :], start=True, stop=True)
                    for c in range(16)], reads=["gg", "U"], writes=[pt1])
        S.op("pe", [lambda e, c=c: e.matmul(ps2[:, c * 8:(c + 1) * 8], lhsT=onesf[:, :], rhs=gg[:, c, :], start=True, stop=True)
                    for c in range(16)], reads=["gg", "onesf"], writes=[pt2])
        pv = lambda p: p[:, 0:128].rearrange("p (c h) -> p c h", h=8)
        S.op("act", lambda e: e.activation(out=G[:, :, :], in_=pv(ps1), func=AF.Identity), reads=[pt1], writes=["G"])
        S.op("act", lambda e: e.activation(out=eG[:, :, :], in_=pv(ps1), func=AF.Exp), reads=[pt1], writes=["eG"])
        S.op("act", lambda e: e.activation(out=cdrep[:, :, :], in_=pv(ps2), func=AF.Exp), reads=[pt2], writes=["cdrep"])
        S.op("dve", lambda e: e.tensor_tensor(out=ed[:, :, :], in0=pv(ps2), in1=G[:, :, :], op=ALU.subtract), reads=[pt2, "G"], writes=["ed"])
        S.op("act", lambda e: e.activation(out=ed[:, :, :], in_=ed[:, :, :], func=AF.Exp), reads=["ed"], writes=["ed"])
        S.op("dve", lambda e: e.tensor_tensor(out=nbeg[:, :, :], in0=beta[:, :, :], in1=eG[:, :, :], op=ALU.mult), reads=["beta", "eG"], writes=["nbeg"])
        S.op("dve", lambda e: e.tensor_scalar(out=nbeg[:, :, :], in0=nbeg[:, :, :], scalar1=-1.0, scalar2=None, op0=ALU.mult),
             reads=["nbeg"], writes=["nbeg"])
        nbeta = F3("nbeta")
        S.op("dve", lambda e: e.tensor_scalar(out=nbeta[:, :, :], in0=beta[:, :, :], scalar1=-1.0, scalar2=None, op0=ALU.mult),
             reads=["beta"], writes=["nbeta"])
        DECR = ["gg", "beta", "nbeg", "eG", "ed", "cdrep", "nbeta"]

        HL = 512
        NP, QN = L // HL, HL // 512
        cbuf = A("cbufb", [128, 3 + L], BF16)
        dg = [A("dg%d" % j, [128, 128], BF16) for j in range(4)]
        accb = A("accb", [128, L], BF16)
        sqb = self.sq[:, 0, :]
        rsn = self.rstd
        f128 = lambda nm: A(nm, [128, 128], F32)
        b128 = lambda nm: A(nm, [128, 128], BF16)
        slots = []
        for si in range(2):
            sl = {}
            sl["fm"] = [A("fmq%d" % si, [128, L], BF16), A("fmk%d" % si, [128, L], BF16), A("fmv%d" % si, [128, L], BF16)]
            sl["kvc"] = [A("kvc%d_%d" % (si, j), [128, 256], BF16) for j in range(2)]
            sl["oT"] = A("oT%d" % si, [128, L], BF16)
            sl["S32"] = A("S32_%d" % si, [128, 128], F32)
            sl["Sbf"] = A("Sbf_%d" % si, [128, 128], BF16)
            for par in range(2):
                for nm in ("rhsD", "E", "ET", "Pm", "X0", "X1"):
                    sl[(nm, par)] = f128("%s_%d_%d" % (nm, si, par))
                for nm in ("YR0", "YR1"):
                    sl[(nm, par)] = A("%s_%d_%d" % (nm, si, par), [128, 256], F32)
                for nm in ("Rtb", "bv", "kbg", "kd", "wTn", "qkT"):
                    sl[(nm, par)] = b128("%s_%d_%d" % (nm, si, par))
            sl["o1"] = f128("o1_%d" % si)
            sl["gsall"] = A("gsall%d" % si, [128, 16, 128], BF16)
            for nm in ("vnew", "onb"):
                sl[nm] = b128("%s_%d" % (nm, si))
            sl["ss"] = A("ss_%d" % si, [128, 1], F32)
            slots.append(sl)
        S.op("dve", lambda e: e.memset(cbuf[:, 0:3], 0.0), writes=[("cbuf", "pad")])
        self.evi = 0

        PER = {"vnew", "onb", "o1", "ss", "S32", "Sbf", "wo", "gsall"}

        def head(h, slot):
            b = slot
            sl = slots[slot]
            fm, oT, S32, Sbf, ss = sl["fm"], sl["oT"], sl["S32"], sl["Sbf"], sl["ss"]
            o1, vnew, onb, gsall = sl["o1"], sl["vnew"], sl["onb"], sl["gsall"]
            wo = wos[slot]
            stage2 = {"ops": None}
            PARTOK = {"rhsD", "E", "ET", "Pm", "Rt", "gs", "Rtb", "bv", "kbg", "kd", "wTn", "qkT"}

            def ns(t):
                if isinstance(t, str):
                    return (slot, t) if t in PER else t
                if t[0] in ("X", "Y", "kv", "oT", "par"):
                    return (slot,) + tuple(t)
                if t[0] == "fm":
                    return (slot,) + tuple(t)
                return t

            def sop(eng, fn, reads=(), writes=()):
                r, w = [ns(t) for t in reads], [ns(t) for t in writes]
                if stage2["ops"] is None:
                    S.op(eng, fn, reads=r, writes=w)
                else:
                    stage2["ops"].append((eng, fn, r, w))

            def evac(ps_ap, out_ap, rd, wr):
                self.evi += 1
                if self.evi % 2:
                    sop("act", lambda e: e.activation(out=out_ap, in_=ps_ap, func=AF.Identity), reads=rd, writes=wr)
                else:
                    sop("dve", lambda e: e.tensor_copy(out=out_ap, in_=ps_ap), reads=rd, writes=wr)

            for ch in range(3):
                jj = ch * 8 + h
                cw = lambda tap, jj=jj: vecs[:, CW0 + jj * 4 + tap:CW0 + jj * 4 + tap + 1]
                if ch == 2:
                    dma("pool", wqk[0][:, :, :], w_in[:, :, 2048 + 128 * h:2048 + 128 * (h + 1)], ("wqkv", 0), "wqkv0")
                for tap in range(4):
                    sop("dve", lambda e, tap=tap, cw=cw: e.tensor_scalar(out=dg[tap][:, :], in0=ident[:, :], scalar1=cw(tap), scalar2=None, op0=ALU.mult),
                        reads=["ident", "vecs"], writes=[("dg", tap)])
                for hf in range(NP):
                    ts = slice(hf * 512, (hf + 1) * 512)
                    ps, pt = self.PS.get()
                    sop("pe", [lambda e, kc=kc, ps=ps, ch=ch, ts=ts: e.matmul(
                        ps[:, :], lhsT=wqk[ch % 2][:, kc, :], rhs=hn[:, kc, ts],
                        start=(kc == 0), stop=(kc == KC - 1)) for kc in range(KC)],
                        reads=HN + [("wqkv", ch % 2)], writes=[pt])
                    sop("act", lambda e, ps=ps, hf=hf: e.activation(out=cbuf[:, 3 + hf * 512:3 + (hf + 1) * 512], in_=ps[:, :], func=AF.Identity),
                        reads=[pt], writes=[("cbuf", hf)])
                pss_ = []
                for hf in range(NP):
                    hsl = slice(hf * HL, (hf + 1) * HL)
                    psc, ptc = self.PS.get()
                    rd = [("cbuf", hf), ("cbuf", "pad")] + ([("cbuf", hf - 1)] if hf > 0 else []) + [("dg", t_) for t_ in range(4)]
                    sop("pe", [lambda e, tap=tap, psc=psc, hf=hf: e.matmul(psc[:, :], lhsT=dg[tap][:, :], rhs=cbuf[:, tap + hf * 512:tap + (hf + 1) * 512],
                                                                        start=(tap == 0), stop=(tap == 3)) for tap in range(4)], reads=rd, writes=[ptc])
                    if ch == 2:
                        sop("act", lambda e, hsl=hsl, psc=psc: e.activation(out=fm[2][:, hsl], in_=psc[:, :], func=AF.Silu),
                            reads=[ptc], writes=[("fm", 2, hf)])
                    else:
                        sop("act", lambda e, psc=psc, hsl=hsl: e.activation(out=accb[:, hsl], in_=psc[:, :], func=AF.Silu), reads=[ptc], writes=[("accb", hf)])
                        sop("act", lambda e, hf=hf, hsl=hsl: e.activation(out=self.sq[:, hf, :], in_=accb[:, hsl], func=AF.Square), reads=[("accb", hf)], writes=[("sqp", hf)])
                        ps, pt = self.PS.get()
                        sop("pe", lambda e, ps=ps, hf=hf: e.matmul(ps[:, :], lhsT=ones_bf[:, :], rhs=self.sq[:, hf, :], start=True, stop=True),
                            reads=[("sqp", hf), "ones"], writes=[pt])
                        pss_.append((ps, pt))
                if ch != 2:
                    sc = (128.0 ** -0.5) if ch == 0 else 1.0
                    for hf in range(NP):
                        hsl = slice(hf * HL, (hf + 1) * HL)
                        ps, pt = pss_[hf]
                        sop("act", lambda e, ps=ps: e.activation(out=rsn[:, :], in_=ps[:, :], func=AF.Ln, bias=self.eps_col[:, 0:1]), reads=[pt, "eps"], writes=["rstd"])
                        sop("act", lambda e: e.activation(out=rsn[:, :], in_=rsn[:, :], func=AF.Exp, scale=-0.5), reads=["rstd"], writes=["rstd"])
                        sop("dve", lambda e, ch=ch, hsl=hsl, sc=sc: e.scalar_tensor_tensor(
                            out=fm[ch][:, hsl], in0=accb[:, hsl], scalar=sc, in1=rsn[:, :], op0=ALU.mult, op1=ALU.mult),
                            reads=[("accb", hf), "rstd"], writes=[("fm", ch, hf)])
            FM = lambda ch: [("fm", ch, hf) for hf in range(NP)]
            sop("dve", lambda e: e.memset(S32[:, :], 0.0), writes=["S32"])
            sop("dve", lambda e: e.memset(Sbf[:, :], 0.0), writes=["Sbf"])
            for cq in range(4):
                psg, ptg = self.PS.get()
                for cc in range(4):
                    c_ = 4 * cq + cc
                    sop("pe", [lambda e, kc=kc, psg=psg, cc=cc, c_=c_: e.matmul(psg[:, cc * 128:(cc + 1) * 128], lhsT=hn[:, kc, c_ * 128:(c_ + 1) * 128],
                                                                              rhs=wgate[b][:, kc, :], start=(kc == 0), stop=(kc == KC - 1)) for kc in range(KC)],
                        reads=HN + ["wgate"], writes=[ptg])
                sop("act", lambda e, psg=psg, cq=cq: e.activation(out=gsall[:, 4 * cq:4 * cq + 4, :], in_=psg[:, :].rearrange("p (c v) -> p c v", v=128), func=AF.Silu),
                    reads=[ptg], writes=["gsall"])
            stage2["ops"] = []
            self.PS.lo, self.PS.hi = 4 * slot, 4 * slot + 4

            def chunk(c):
                ct = slice(c * 128, (c + 1) * 128)
                par = c % 2
                col = lambda t: t[:, c, h:h + 1]
                rhsD, E, ET, Pm = (sl[(n, par)] for n in ("rhsD", "E", "ET", "Pm"))
                YR = [sl[("YR0", par)], sl[("YR1", par)]]
                X = [sl[("X0", par)], sl[("X1", par)]]
                Rtb, bv, kbg, kd, wTn, qkT = (sl[(n, par)] for n in ("Rtb", "bv", "kbg", "kd", "wTn", "qkT"))
                T = lambda n: ("par", n, par)
                prep, tail = [], []
                stage2["ops"] = prep
                self.PS.lo, self.PS.hi = 4 * slot, 4 * slot + 2
                sop("pool", lambda e: e.tensor_scalar(out=rhsD[:, :], in0=SL[:, :], scalar1=col(gg), scalar2=1.0, op0=ALU.mult, op1=ALU.mult),
                     reads=["SL", "gg"], writes=[T("rhsD")])
                psd, ptd = self.PS.get()
                sop("pe", [lambda e: e.matmul(psd[:, 0:128], lhsT=U[:, :], rhs=rhsD[:, :], start=True, stop=True),
                            lambda e: e.matmul(psd[:, 128:256], lhsT=rhsD[:, :], rhs=U[:, :], start=True, stop=True)],
                     reads=["U", T("rhsD")], writes=[ptd])
                sop("act", lambda e: e.activation(out=E[:, :], in_=psd[:, 0:128], func=AF.Exp), reads=[ptd], writes=[T("E")])
                sop("act", lambda e: e.activation(out=ET[:, :], in_=psd[:, 128:256], func=AF.Exp), reads=[ptd], writes=[T("ET")])
                psk, ptk = self.PS.get()
                sop("pe", [lambda e: e.matmul(psk[:, 0:128], lhsT=fm[1][:, ct], rhs=fm[1][:, ct], start=True, stop=True),
                            lambda e: e.matmul(psk[:, 128:256], lhsT=fm[1][:, ct], rhs=fm[0][:, ct], start=True, stop=True)],
                     reads=FM(0) + FM(1), writes=[ptk])
                sop("dve", lambda e: e.tensor_tensor(out=E[:, :], in0=psk[:, 0:128], in1=E[:, :], op=ALU.mult), reads=[ptk, T("E")], writes=[T("E")])
                sop("dve", lambda e: e.scalar_tensor_tensor(out=Pm[:, :], in0=E[:, :], scalar=col(nbeta), in1=SL[:, :], op0=ALU.mult, op1=ALU.mult),
                     reads=[T("E"), "nbeta", "SL"], writes=[T("Pm")])
                sop("dve", lambda e: e.tensor_tensor(out=ET[:, :], in0=psk[:, 128:256], in1=ET[:, :], op=ALU.mult), reads=[ptk, T("ET")], writes=[T("ET")])
                sop("pool", lambda e: e.tensor_tensor(out=qkT[:, :], in0=ET[:, :], in1=U[:, :], op=ALU.mult), reads=[T("ET"), "U"], writes=[T("qkT")])
                pst, ptt = self.PS.get()
                sop("pe", lambda e: e.transpose(out=pst[:, 0:128], in_=Pm[:, :], identity=identf[:, :]), reads=[T("Pm"), "identf"], writes=[ptt])
                evac(pst[:, 0:128], YR[0][:, 0:128], [ptt], [T("Y0")])
                sop("pool", lambda e: e.tensor_copy(out=YR[0][:, 128:256], in_=identf[:, :]), reads=["identf"], writes=[T("R0")])
                Xc, xt = Pm, T("Pm")
                for k in range(6):
                    a_, nb = k % 2, (k + 1) % 2
                    YRa, YRn = YR[a_], YR[nb]
                    Xn = X[nb]
                    psx, ptx = self.PS.get()
                    sop("pe", lambda e, YRa=YRa, Xc=Xc, psx=psx: e.matmul(psx[:, 0:128], lhsT=YRa[:, 0:128], rhs=Xc[:, :], start=True, stop=True),
                         reads=[T("Y%d" % a_), xt], writes=[ptx])
                    evac(psx[:, 0:128], Xn[:, :], [ptx], [T("X%d" % nb)])
                    psb, ptb = self.PS.get()
                    if k < 5:
                        sop("pe", lambda e, YRa=YRa, Xc=Xc, psb=psb: e.matmul(psb[:, 0:256], lhsT=Xc[:, :], rhs=YRa[:, 0:256], start=True, stop=True),
                             reads=[T("Y%d" % a_), T("R%d" % a_), xt], writes=[ptb])
                        evac(psb[:, 0:128], YRn[:, 0:128], [ptb], [T("Y%d" % nb)])
                        sop("dve", lambda e, YRa=YRa, YRn=YRn, psb=psb: e.tensor_tensor(out=YRn[:, 128:256], in0=psb[:, 128:256], in1=YRa[:, 128:256], op=ALU.add),
                             reads=[ptb, T("R%d" % a_)], writes=[T("R%d" % nb)])
                    else:
                        sop("pe", lambda e, YRa=YRa, Xc=Xc, psb=psb: e.matmul(psb[:, 0:128], lhsT=Xc[:, :], rhs=YRa[:, 128:256], start=True, stop=True),
                             reads=[T("R%d" % a_), xt], writes=[ptb])
                        sop("dve", lambda e, YRa=YRa, YRn=YRn, psb=psb: e.tensor_tensor(out=YRn[:, 128:256], in0=psb[:, 0:128], in1=YRa[:, 128:256], op=ALU.add),
                             reads=[ptb, T("R%d" % a_)], writes=[T("R%d" % nb)])
                    Xc, xt = Xn, T("X%d" % nb)
                psf, ptf = self.PS.get()
                sop("pe", lambda e, psf=psf: e.matmul(psf[:, 0:128], lhsT=X[0][:, :], rhs=YR[0][:, 128:256], start=True, stop=True),
                     reads=[T("X0"), T("R0")], writes=[ptf])
                sop("dve", lambda e, psf=psf: e.tensor_tensor(out=Rtb[:, :], in0=psf[:, 0:128], in1=YR[0][:, 128:256], op=ALU.add),
                     reads=[ptf, T("R0")], writes=[T("Rtb")])
                kvc = sl["kvc"][par]
                pskv, ptkv = self.PS.get()
                sop("pe", [lambda e, q=q: e.matmul(pskv[:, q * 128:(q + 1) * 128], lhsT=fm[1 + q][:, ct], rhs=ident[:, :],
                                                    start=True, stop=True) for q in range(2)],
                     reads=FM(1) + FM(2) + ["ident"], writes=[ptkv])
                evac(pskv[:, 0:256], kvc[:, :], [ptkv], [T("kvc")])
                sop("pool", lambda e: e.tensor_scalar(out=bv[:, :], in0=kvc[:, 128:256], scalar1=col(beta), scalar2=1.0, op0=ALU.mult, op1=ALU.mult),
                     reads=[T("kvc"), "beta"], writes=[T("bv")])
                sop("pool", lambda e: e.tensor_scalar(out=kbg[:, :], in0=kvc[:, 0:128], scalar1=col(nbeg), scalar2=1.0, op0=ALU.mult, op1=ALU.mult),
                     reads=[T("kvc"), "nbeg"], writes=[T("kbg")])
                sop("pool", lambda e: e.tensor_scalar(out=kd[:, :], in0=kvc[:, 0:128], scalar1=col(ed), scalar2=1.0, op0=ALU.mult, op1=ALU.mult),
                     reads=[T("kvc"), "ed"], writes=[T("kd")])
                psw, ptw = self.PS.get()
                sop("pe", lambda e: e.matmul(psw[:, 0:128], lhsT=kbg[:, :], rhs=Rtb[:, :], start=True, stop=True), reads=[T("kbg"), T("Rtb")], writes=[ptw])
                evac(psw[:, 0:128], wTn[:, :], [ptw], [T("wTn")])
                stage2["ops"] = tail
                self.PS.lo, self.PS.hi = 4 * slot + 2, 4 * slot + 4
                psv_, ptv = self.PS.get()
                sop("pe", [lambda e: e.matmul(psv_[:, 0:128], lhsT=Rtb[:, :], rhs=bv[:, :], start=True, stop=False),
                            lambda e: e.matmul(psv_[:, 0:128], lhsT=wTn[:, :], rhs=Sbf[:, :], start=False, stop=True)],
                     reads=[T("Rtb"), T("bv"), T("wTn"), "Sbf"], writes=[ptv])
                evac(psv_[:, 0:128], vnew[:, :], [ptv], ["vnew"])
                pso, pto = self.PS.get()
                sop("pe", [lambda e: e.matmul(pso[:, 0:128], lhsT=fm[0][:, ct], rhs=Sbf[:, :], start=True, stop=True),
                            lambda e: e.matmul(pso[:, 128:256], lhsT=qkT[:, :], rhs=vnew[:, :], start=True, stop=True)],
                     reads=FM(0) + ["Sbf", T("qkT"), "vnew"], writes=[pto])
                pss, pts = self.PS.get()
                sop("pe", lambda e: e.matmul(pss[:, 0:128], lhsT=kd[:, :], rhs=vnew[:, :], start=True, stop=True), reads=[T("kd"), "vnew"], writes=[pts])
                sop("dve", lambda e: e.scalar_tensor_tensor(out=S32[:, :], in0=S32[:, :], scalar=col(cdrep), in1=pss[:, 0:128], op0=ALU.mult, op1=ALU.add),
                     reads=[pts, "S32", "cdrep"], writes=["S32"])
                sop("act", lambda e: e.activation(out=Sbf[:, :], in_=S32[:, :], func=AF.Identity), reads=["S32"], writes=["Sbf"])
                sop("act", lambda e: e.activation(out=o1[:, :], in_=pso[:, 0:128], func=AF.Identity, scale=col(eG)), reads=[pto, "eG"], writes=["o1"])
                sop("dve", lambda e: e.tensor_tensor(out=o1[:, :], in0=pso[:, 128:256], in1=o1[:, :], op=ALU.add), reads=[pto, "o1"], writes=["o1"])
                sop("act", lambda e: e.activation(out=onb[:, :], in_=o1[:, :], func=AF.Square, accum_out=ss[:, 0:1]),
                     reads=["o1"], writes=["onb", "ss"])
                sop("act", lambda e: e.activation(out=ss[:, :], in_=ss[:, :], func=AF.Ln, scale=1.0 / 128, bias=self.eps_col[:, 0:1]),
                     reads=["ss", "eps"], writes=["ss"])
                sop("act", lambda e: e.activation(out=ss[:, :], in_=ss[:, :], func=AF.Exp, scale=-0.5), reads=["ss"], writes=["ss"])
                sop("dve", lambda e: e.scalar_tensor_tensor(out=o1[:, :], in0=o1[:, :], scalar=ss[:, 0:1], in1=ong, op0=ALU.mult, op1=ALU.mult),
                     reads=["o1", "ss", "rows"], writes=["o1"])
                sop("dve", lambda e: e.tensor_tensor(out=onb[:, :], in0=o1[:, :], in1=gsall[:, c, :], op=ALU.mult), reads=["o1", "gsall"], writes=["onb"])
                psT, ptT = self.PS.get()
                sop("pe", lambda e: e.matmul(psT[:, 0:128], lhsT=onb[:, :], rhs=ident[:, :], start=True, stop=True), reads=["onb", "ident"], writes=[ptT])
                evac(psT[:, 0:128], oT[:, ct], [ptT], [("oT", c)])
                return prep, tail

            def zipl(x, y):
                out = []
                for i in range(max(len(x), len(y))):
                    if i < len(x):
                        out.append(x[i])
                    if i < len(y):
                        out.append(y[i])
                return out

            pts_ = [chunk(c) for c in range(16)]
            stream = list(pts_[0][0])
            for c in range(16):
                stream += zipl(pts_[c][1], pts_[c + 1][0] if c + 1 < 16 else [])
            stage2["ops"] = stream
            self.PS.lo, self.PS.hi = 4 * slot, 4 * slot + 4
            OT = [("oT", c) for c in range(16)]
            for nt in range(4):
                ts = slice(nt * 512, (nt + 1) * 512)
                for dc in range(KC):
                    ps, pt = self.PS.get()
                    sop("pe", lambda e, ps=ps, dc=dc, ts=ts: e.matmul(ps[:, :], lhsT=wo[:, dc * 128:(dc + 1) * 128], rhs=oT[:, ts], start=True, stop=True),
                         reads=OT + ["wo"], writes=[pt])
                    sop("dve", lambda e, ps=ps, dc=dc, ts=ts: e.tensor_tensor(out=hT[:, dc, ts], in0=ps[:, :], in1=hT[:, dc, ts], op=ALU.add),
                         reads=[pt, ("hT", nt, dc)], writes=[("hT", nt, dc)])
            self.PS.lo, self.PS.hi = 0, 8
            return stage2["ops"]

        for pair in range(4):
            lists = []
            for slot in range(2):
                h = 2 * pair + slot
                load_head(h, slot)
                lists.append(head(h, slot))
            for i in range(max(len(x) for x in lists)):
                for ops in lists:
                    if i < len(ops):
                        eng, fn, r, w = ops[i]
                        S.op(eng, fn, reads=r, writes=w)
        self.phase_end()

    def s5_phase(self, li, d):
        self.phase_begin()
        import os
        stop3 = float(os.environ.get("KDBG3", "99"))
        S, nc, hT, hn, vecs = self.S, self.nc, self.hT, self.hn, self.vecs
        A = self.A
        ident, identf = self.ident_bf, self.ident_f
        PI = float(np.pi)
        w_in = d["s5_w_in"][0].rearrange("(kc p) f -> p kc f", p=128)
        w_out = d["s5_w_out"][0].rearrange("(kc p) f -> p kc f", p=128)
        gcol = lambda kc: vecs[:, li * 8 + kc:li * 8 + kc + 1]
        self.rmsnorm_T(gcol, hn, "hn")
        HN = [("hn", nt, kc) for nt in range(4) for kc in range(KC)]

        def dma(eng, out, in_, wtok, sem, reads=()):
            S.op(eng, lambda e: e.dma_start(out=out, in_=in_), reads=list(reads), writes=[wtok], dsem=sem)

        cur = {"ops": None}

        def op(eng, fn, rd, wr):
            if cur["ops"] is None:
                S.op(eng, fn, reads=rd, writes=wr)
            else:
                cur["ops"].append((eng, fn, list(rd), list(wr)))

        HL = 512
        prm = A("prm", [128, 3 * 64 + 1 + 8], F32)
        dma("sp", prm[:, :], d["s5_prm"], "prm", "prm")
        LR, LIM, LDT, sgn = prm[:, 0:64], prm[:, 64:128], prm[:, 128:192], prm[:, 192:193]
        dcol = lambda mc: prm[:, 193 + mc:194 + mc]
        dtv, ar, th, rdec = (A(n, [128, 64], F32) for n in ("dtv", "ar", "th", "rdec"))
        q0, q1, q2, q3, q4, q5, q6, q7 = (A("q%d" % i, [128, 64], F32) for i in range(8))
        qi = A("qi", [128, 64], mybir.dt.int32)
        cfr, cfi = A("cfr", [128, 64], F32), A("cfi", [128, 64], F32)

        def exp_acc(out, x, xt, ot, k, n):
            sc = 1.0 / (2 ** k)
            op("dve", lambda e: e.tensor_scalar(out=out, in0=x, scalar1=sc / n, scalar2=1.0, op0=ALU.mult, op1=ALU.add), [xt], [ot])
            for m in range(n - 1, 0, -1):
                op("dve", lambda e: e.tensor_tensor(out=out, in0=out, in1=x, op=ALU.mult), [xt, ot], [ot])
                op("dve", lambda e, m=m: e.tensor_scalar(out=out, in0=out, scalar1=sc / m, scalar2=1.0, op0=ALU.mult, op1=ALU.add), [ot], [ot])
            for _ in range(k):
                op("dve", lambda e: e.tensor_tensor(out=out, in0=out, in1=out, op=ALU.mult), [ot], [ot])

        exp_acc(dtv[:, :], LDT, "prm", "dtv", 4, 12)
        op("dve", lambda e: e.tensor_tensor(out=ar[:, :], in0=LR, in1=dtv[:, :], op=ALU.mult), ["prm", "dtv"], ["ar"])
        op("dve", lambda e: e.tensor_tensor(out=th[:, :], in0=LIM, in1=dtv[:, :], op=ALU.mult), ["prm", "dtv"], ["th"])
        exp_acc(rdec[:, :], ar[:, :], "ar", "rdec", 0, 8)
        thoff = A("thoff", [128, 4, 64], F32)
        for hf_ in range(4):
            op("dve", lambda e, hf_=hf_: e.tensor_scalar(out=thoff[:, hf_, :], in0=th[:, :], scalar1=float(hf_ * 512), scalar2=None, op0=ALU.mult), ["th"], ["thoff"])
        NS = 20
        op("dve", lambda e: e.tensor_scalar(out=q0[:, :], in0=ar[:, :], scalar1=1.0 / (NS + 1), scalar2=1.0, op0=ALU.mult, op1=ALU.add), ["ar"], ["q0"])
        op("dve", lambda e: e.tensor_scalar(out=q1[:, :], in0=th[:, :], scalar1=1.0 / (NS + 1), scalar2=None, op0=ALU.mult), ["th"], ["q1"])
        for m in range(NS, 1, -1):
            op("dve", lambda e: e.tensor_tensor(out=q2[:, :], in0=ar[:, :], in1=q0[:, :], op=ALU.mult), ["ar", "q0"], ["q2"])
            op("dve", lambda e: e.tensor_tensor(out=q3[:, :], in0=th[:, :], in1=q1[:, :], op=ALU.mult), ["th", "q1"], ["q3"])
            op("dve", lambda e: e.tensor_tensor(out=q4[:, :], in0=ar[:, :], in1=q1[:, :], op=ALU.mult), ["ar", "q1"], ["q4"])
            op("dve", lambda e: e.tensor_tensor(out=q5[:, :], in0=th[:, :], in1=q0[:, :], op=ALU.mult), ["th", "q0"], ["q5"])
            op("dve", lambda e: e.tensor_tensor(out=q2[:, :], in0=q2[:, :], in1=q3[:, :], op=ALU.subtract), ["q2", "q3"], ["q2"])
            op("dve", lambda e: e.tensor_tensor(out=q4[:, :], in0=q4[:, :], in1=q5[:, :], op=ALU.add), ["q4", "q5"], ["q4"])
            op("dve", lambda e, m=m: e.tensor_scalar(out=q0[:, :], in0=q2[:, :], scalar1=1.0 / m, scalar2=1.0, op0=ALU.mult, op1=ALU.add), ["q2"], ["q0"])
            op("dve", lambda e, m=m: e.tensor_scalar(out=q1[:, :], in0=q4[:, :], scalar1=1.0 / m, scalar2=None, op0=ALU.mult), ["q4"], ["q1"])
        op("dve", lambda e: e.tensor_copy(out=q2[:, :], in_=th[:, :]), ["th"], ["q2"])
        op("dve", lambda e: e.tensor_scalar(out=q3[:, :], in0=th[:, :], scalar1=PI / 2, scalar2=None, op0=ALU.add), ["th"], ["q3"])

        def rr_small(x, xt):
            op("dve", lambda e: e.tensor_scalar(out=qi[:, :], in0=x, scalar1=1.0 / (2 * PI), scalar2=None, op0=ALU.mult), [xt], ["qi"])
            op("dve", lambda e: e.tensor_copy(out=q4[:, :], in_=qi[:, :]), ["qi"], ["q4"])
            op("dve", lambda e: e.scalar_tensor_tensor(out=x, in0=q4[:, :], scalar=-2 * PI, in1=x, op0=ALU.mult, op1=ALU.add), ["q4", xt], [xt])
            op("dve", lambda e: e.tensor_scalar(out=q4[:, :], in0=x, scalar1=PI, scalar2=-2 * PI, op0=ALU.is_gt, op1=ALU.mult), [xt], ["q4"])
            op("dve", lambda e: e.tensor_tensor(out=x, in0=x, in1=q4[:, :], op=ALU.add), [xt, "q4"], [xt])
            op("dve", lambda e: e.tensor_scalar(out=q4[:, :], in0=x, scalar1=-PI, scalar2=2 * PI, op0=ALU.is_lt, op1=ALU.mult), [xt], ["q4"])
            op("dve", lambda e: e.tensor_tensor(out=x, in0=x, in1=q4[:, :], op=ALU.add), [xt, "q4"], [xt])
        rr_small(q2[:, :], "q2")
        rr_small(q3[:, :], "q3")
        op("act", lambda e: e.activation(out=q2[:, :], in_=q2[:, :], func=AF.Sin, scale=0.999995), ["q2"], ["q2"])
        op("act", lambda e: e.activation(out=q3[:, :], in_=q3[:, :], func=AF.Sin, scale=0.999995), ["q3"], ["q3"])
        op("dve", lambda e: e.tensor_tensor(out=q2[:, :], in0=q2[:, :], in1=rdec[:, :], op=ALU.mult), ["q2", "rdec"], ["q2"])
        op("dve", lambda e: e.tensor_tensor(out=q3[:, :], in0=q3[:, :], in1=rdec[:, :], op=ALU.mult), ["q3", "rdec"], ["q3"])
        op("dve", lambda e: e.tensor_scalar(out=q3[:, :], in0=q3[:, :], scalar1=-1.0, scalar2=None, op0=ALU.add), ["q3"], ["q3"])
        op("dve", lambda e: e.tensor_tensor(out=q4[:, :], in0=ar[:, :], in1=ar[:, :], op=ALU.mult), ["ar"], ["q4"])
        op("dve", lambda e: e.tensor_tensor(out=q5[:, :], in0=th[:, :], in1=th[:, :], op=ALU.mult), ["th"], ["q5"])
        op("dve", lambda e: e.tensor_tensor(out=q4[:, :], in0=q4[:, :], in1=q5[:, :], op=ALU.add), ["q4", "q5"], ["q4"])
        op("dve", lambda e: e.reciprocal(out=q5[:, :], in_=q4[:, :]), ["q4"], ["q5"])
        op("dve", lambda e: e.tensor_scalar(out=q4[:, :], in0=q4[:, :], scalar1=4.0, scalar2=None, op0=ALU.is_lt), ["q4"], ["q4"])
        op("dve", lambda e: e.tensor_tensor(out=q6[:, :], in0=q3[:, :], in1=ar[:, :], op=ALU.mult), ["q3", "ar"], ["q6"])
        op("dve", lambda e: e.tensor_tensor(out=q7[:, :], in0=q2[:, :], in1=th[:, :], op=ALU.mult), ["q2", "th"], ["q7"])
        op("dve", lambda e: e.tensor_tensor(out=q6[:, :], in0=q6[:, :], in1=q7[:, :], op=ALU.add), ["q6", "q7"], ["q6"])
        op("dve", lambda e: e.tensor_tensor(out=q6[:, :], in0=q6[:, :], in1=q5[:, :], op=ALU.mult), ["q6", "q5"], ["q6"])
        op("dve", lambda e: e.tensor_tensor(out=q7[:, :], in0=q2[:, :], in1=ar[:, :], op=ALU.mult), ["q2", "ar", "q6"], ["q7"])
        op("dve", lambda e: e.tensor_tensor(out=q2[:, :], in0=q3[:, :], in1=th[:, :], op=ALU.mult), ["q3", "th", "q7"], ["q2"])
        op("dve", lambda e: e.tensor_tensor(out=q7[:, :], in0=q7[:, :], in1=q2[:, :], op=ALU.subtract), ["q7", "q2"], ["q7"])
        op("dve", lambda e: e.tensor_tensor(out=q7[:, :], in0=q7[:, :], in1=q5[:, :], op=ALU.mult), ["q7", "q5"], ["q7"])
        op("dve", lambda e: e.tensor_tensor(out=q0[:, :], in0=q0[:, :], in1=q6[:, :], op=ALU.subtract), ["q0", "q6"], ["q0"])
        op("dve", lambda e: e.tensor_tensor(out=q0[:, :], in0=q0[:, :], in1=q4[:, :], op=ALU.mult), ["q0", "q4"], ["q0"])
        op("dve", lambda e: e.tensor_tensor(out=q0[:, :], in0=q0[:, :], in1=q6[:, :], op=ALU.add), ["q0", "q6"], ["q0"])
        op("dve", lambda e: e.tensor_tensor(out=cfr[:, :], in0=q0[:, :], in1=dtv[:, :], op=ALU.mult), ["q0", "dtv"], ["cfr"])
        op("dve", lambda e: e.tensor_tensor(out=q1[:, :], in0=q1[:, :], in1=q7[:, :], op=ALU.subtract), ["q1", "q7"], ["q1"])
        op("dve", lambda e: e.tensor_tensor(out=q1[:, :], in0=q1[:, :], in1=q4[:, :], op=ALU.mult), ["q1", "q4"], ["q1"])
        op("dve", lambda e: e.tensor_tensor(out=q1[:, :], in0=q1[:, :], in1=q7[:, :], op=ALU.add), ["q1", "q7"], ["q1"])
        op("dve", lambda e: e.tensor_tensor(out=cfi[:, :], in0=q1[:, :], in1=dtv[:, :], op=ALU.mult), ["q1", "dtv"], ["cfi"])
        rx = A("rx", [64, 2, 8, 64], F32)
        self.dbg_names = {}
        ramp = A("ramp", [128, HL], F32)
        op("pool", lambda e: e.iota(ramp[:, :], pattern=[[1, HL]], base=0, channel_multiplier=0, allow_small_or_imprecise_dtypes=True), [], ["ramp"])

        ang, kf, cosT, sinT, wv, Wsc = (A(n, [128, HL], F32) for n in ("ang", "kf", "cosT", "sinT", "wv", "Wsc"))
        cosT2, sinT2, Wsc2 = (A(n, [128, HL], F32) for n in ("cosT2", "sinT2", "Wsc2"))
        ki = A("ki", [128, HL], mybir.dt.int32)
        Ycs = A("Ycs", [128, HL], BF16)
        Ysn = A("Ysn", [128, HL], BF16)
        ub = A("ub", [128, L], BF16)
        yT = A("yT", [128, KC, L], BF16)
        ddiag = A("ddiag", [128, 128], BF16)
        c0, c1, c2, c3, c4, c5 = ang, kf, cosT, sinT, wv, Wsc
        ci = ki
        BT = A("BT", [128, 2, 8, 64], F32)
        Bp = A("Bp", [128, 8, 128], BF16)
        Bs = A("Bs", [128, 8, 128], BF16)
        Cp = A("Cp", [128, 8, 128], BF16)
        Cq = A("Cq", [128, 8, 128], BF16)
        Cf = A("Cf", [128, 2, 8, 128], F32)
        wu = A("wu", [128, KC, 128], BF16)
        wo = [A("wo%d" % i, [128, KC, 256], BF16) for i in range(1)] * 2

        def range_reduce(x, n, tmp_i, tmp_f, rd, tf):
            op("dve", lambda e: e.tensor_scalar(out=tmp_i, in0=x, scalar1=1.0 / (2 * PI), scalar2=None, op0=ALU.mult), rd, ["ki"])
            op("dve", lambda e: e.tensor_copy(out=tmp_f, in_=tmp_i), ["ki"], [tf])
            op("dve", lambda e: e.scalar_tensor_tensor(out=x, in0=tmp_f, scalar=-2 * PI, in1=x, op0=ALU.mult, op1=ALU.add), [tf] + rd, rd)
            op("dve", lambda e: e.tensor_scalar(out=tmp_f, in0=x, scalar1=PI, scalar2=-2 * PI, op0=ALU.is_gt, op1=ALU.mult), rd, [tf])
            op("dve", lambda e: e.tensor_tensor(out=x, in0=x, in1=tmp_f, op=ALU.add), rd + [tf], rd)
            op("dve", lambda e: e.tensor_scalar(out=tmp_f, in0=x, scalar1=-PI, scalar2=2 * PI, op0=ALU.is_lt, op1=ALU.mult), rd, [tf])
            op("dve", lambda e: e.tensor_tensor(out=x, in0=x, in1=tmp_f, op=ALU.add), rd + [tf], rd)

        self.PS.lo, self.PS.hi = 0, 4
        self.PS.i = 0
        yacc = [(self.PS.tiles[4 + q], ("ps", 4 + q)) for q in range(4)]

        def mc_block(mc):
            passes = []
            dma("pool", wu[:, :, :], w_in[:, :, mc * 128:(mc + 1) * 128], "wu", "wu")
            for nt in range(4):
                ts = slice(nt * 512, (nt + 1) * 512)
                ps, pt = self.PS.get()
                S.op("pe", [lambda e, kc=kc, ps=ps, ts=ts: e.matmul(ps[:, :], lhsT=wu[:, kc, :], rhs=hn[:, kc, ts],
                                                                   start=(kc == 0), stop=(kc == KC - 1)) for kc in range(KC)],
                     reads=HN + ["wu"], writes=[pt])
                op("act", lambda e, ps=ps, ts=ts: e.activation(out=ub[:, ts], in_=ps[:, :], func=AF.Identity), [pt], [("ub", nt)])
            UB = [("ub", nt) for nt in range(4)]
            dma("sp", BT[:, :, :, :], d["s5_bT"][:, :, mc * 8:(mc + 1) * 8, :], "BT", "BT")
            dma("sp", Cf[:, :, :, :], d["s5_cpad"][:, :, mc * 8:(mc + 1) * 8, :], "Cf", "Cf")
            for ri, cf, ctile, ctok in ((0, cfr, c3, "sinT"), (1, cfi, c4, "wv")):
                op("dve", lambda e, ri=ri, cf=cf: e.tensor_tensor(
                    out=rx[:, ri, :, :], in0=identf[0:64, 0:64].unsqueeze(1).to_broadcast([64, 8, 64]),
                    in1=cf[0:64, mc * 8:(mc + 1) * 8].unsqueeze(2).to_broadcast([64, 8, 64]), op=ALU.mult),
                    ["identf", "cfr", "cfi"], [("rx", ri)])
                psx, ptx = self.PS.get()
                S.op("pe", lambda e, psx=psx, ri=ri: e.matmul(psx[:, :], lhsT=self.ones_f[0:64, :], rhs=rx[:, ri, :, :].rearrange("p g q -> p (g q)"),
                                                             start=True, stop=True), reads=[("rx", ri), "onesf"], writes=[ptx])
                op("act", lambda e, psx=psx, ctile=ctile: e.activation(out=ctile[:, :], in_=psx[:, :], func=AF.Identity), [ptx], [ctok])
            cr3 = c3[:, :].rearrange("p (g q) -> p g q", q=64)
            ci3 = c4[:, :].rearrange("p (g q) -> p g q", q=64)
            t5 = c5[:, :].rearrange("p (g q) -> p g q", q=64)
            t0 = c0[:, :].rearrange("p (g q) -> p g q", q=64)
            bre, bim = BT[:, 0, :, :], BT[:, 1, :, :]
            op("dve", lambda e: e.tensor_tensor(out=t5, in0=cr3, in1=bre, op=ALU.mult), ["sinT", "BT", "Wsc"], ["Wsc"])
            op("dve", lambda e: e.tensor_tensor(out=t0, in0=ci3, in1=bim, op=ALU.mult), ["wv", "BT", "ang"], ["ang"])
            op("dve", lambda e: e.tensor_tensor(out=Bp[:, :, 0:64], in0=t5, in1=t0, op=ALU.subtract), ["Wsc", "ang"], [("Bp", 0)])
            op("dve", lambda e: e.tensor_scalar(out=Bs[:, :, 64:128], in0=Bp[:, :, 0:64], scalar1=-1.0, scalar2=None, op0=ALU.mult), [("Bp", 0)], [("Bs", 1)])
            op("dve", lambda e: e.tensor_tensor(out=t5, in0=cr3, in1=bim, op=ALU.mult), ["sinT", "BT", "Wsc", ("Bp", 0)], ["Wsc"])
            op("dve", lambda e: e.tensor_tensor(out=t0, in0=ci3, in1=bre, op=ALU.mult), ["wv", "BT", "ang", ("Bp", 0)], ["ang"])
            op("dve", lambda e: e.tensor_tensor(out=Bp[:, :, 64:128], in0=t5, in1=t0, op=ALU.add), ["Wsc", "ang"], [("Bp", 1)])
            op("dve", lambda e: e.tensor_copy(out=Bs[:, :, 0:64], in_=Bp[:, :, 64:128]), [("Bp", 1)], [("Bs", 0)])
            BP = [("Bp", 0), ("Bp", 1), ("Bs", 0), ("Bs", 1)]
            op("dve", lambda e: e.tensor_scalar(out=Cp[:, :, :], in0=Cf[:, 0, :, :], scalar1=sgn, scalar2=None, op0=ALU.mult), ["Cf", "prm"], ["Cp"])
            op("dve", lambda e: e.tensor_scalar(out=Cq[:, :, :], in0=Cf[:, 1, :, :], scalar1=-1.0, scalar2=None, op0=ALU.mult), ["Cf"], ["Cq"])
            op("dve", lambda e: e.tensor_scalar(out=ddiag[:, :], in0=identf[:, :], scalar1=dcol(mc), scalar2=None, op0=ALU.mult), ["identf", "prm"], ["ddiag"])
            for q in range(4):
                ya, yt = yacc[q]
                S.op("pe", lambda e, ya=ya, q=q: e.matmul(ya[:, :], lhsT=ddiag[:, :], rhs=ub[:, q * 512:(q + 1) * 512], start=True, stop=False),
                     reads=["ddiag"] + UB, writes=[yt])

            if stop3 <= 1:
                return

            def group(gl):
                g = 8 * mc + gl
                def one_pass(hf):
                    par = hf % 2
                    cosT_, sinT_, Wsc_, Ycs_, Ysn_ = (cosT2, sinT2, Wsc2, Ycs, Ysn) if par else (cosT, sinT, Wsc, Ycs, Ysn)
                    WscP, twP = (Wsc, 'Wsc') if par else (Wsc2, 'Wsc2')
                    tc, tsn, tw, tyc, tys = ('cosT2', 'sinT2', 'Wsc2', 'Ycs', 'Ysn') if par else ('cosT', 'sinT', 'Wsc', 'Ycs', 'Ysn')
                    Tops, Dops = [], []
                    cur["ops"] = Tops
                    nt = hf
                    ts = slice(nt * 512, (nt + 1) * 512)
                    qs = slice(0, 512)
                    psz, ptz = self.PS.get()
                    pss, pts = self.PS.get()
                    op("pe", lambda e: e.matmul(psz[:, :], lhsT=Bp[:, gl, :], rhs=ub[:, ts], start=True, stop=True), BP + UB, [ptz])
                    op("pe", lambda e: e.matmul(pss[:, :], lhsT=Bs[:, gl, :], rhs=ub[:, ts], start=True, stop=True), BP + UB, [pts])
                    op("act", lambda e, hf=hf: e.activation(out=ang[:, :], in_=ramp[:, :], func=AF.Identity, scale=th[:, g:g + 1],
                                                            bias=thoff[:, hf, g:g + 1]), ["ramp", "th", "thoff"], ["ang"])
                    op("dve", lambda e: e.tensor_scalar(out=ki[:, :], in0=ang[:, :], scalar1=1.0 / (2 * PI), scalar2=None, op0=ALU.mult), ["ang"], ["ki"])
                    op("dve", lambda e: e.scalar_tensor_tensor(out=ang[:, :], in0=ki[:, :], scalar=-2 * PI, in1=ang[:, :], op0=ALU.mult, op1=ALU.add),
                       ["ki", "ang"], ["ang"])
                    op("dve", lambda e: e.tensor_scalar(out=ang[:, :], in0=ang[:, :], scalar1=-PI, scalar2=PI, op0=ALU.max, op1=ALU.min), ["ang"], ["ang"])
                    op("act", lambda e: e.activation(out=sinT_[:, :], in_=ang[:, :], func=AF.Sin, scale=0.999995), ["ang"], [tsn])
                    op("act", lambda e: e.activation(out=cosT_[:, :], in_=ang[:, :], func=AF.Sin, scale=0.5), ["ang"], [tc])
                    op("act", lambda e: e.activation(out=cosT_[:, :], in_=cosT_[:, :], func=AF.Square), [tc], [tc])
                    op("act", lambda e: e.activation(out=cosT_[:, :], in_=cosT_[:, :], func=AF.Identity, scale=-2.0, bias=1.0), [tc], [tc])
                    cur["ops"] = Dops
                    for q in range(1):
                        op("dve", lambda e, pss=pss, qs=qs: e.tensor_tensor(out=kf[:, qs], in0=pss[:, :], in1=sinT_[:, qs], op=ALU.mult), [pts, tsn], ["kf"])
                        op("dve", lambda e, psz=psz, qs=qs: e.tensor_tensor(out=wv[:, qs], in0=psz[:, :], in1=cosT_[:, qs], op=ALU.mult), [ptz, tc], ["wv"])
                        op("dve", lambda e, qs=qs: e.tensor_tensor(out=wv[:, qs], in0=wv[:, qs], in1=kf[:, qs], op=ALU.add), ["kf", "wv"], ["wv"])
                    WV = ["wv"]
                    if hf > 0:
                        op("dve", lambda e: e.scalar_tensor_tensor(out=wv[:, 0:1], in0=WscP[:, HL - 1:HL], scalar=rdec[:, g:g + 1], in1=wv[:, 0:1],
                                                                   op0=ALU.mult, op1=ALU.add), WV + ["rdec", twP], WV)
                    op("dve", lambda e: e.tensor_tensor_scan(out=Wsc_[:, :], data0=rdec[:, g:g + 1].to_broadcast([128, HL]), data1=wv[:, :],
                                                             initial=0.0, op0=ALU.mult, op1=ALU.add), WV + ["rdec"], [tw])
                    op("pool", lambda e: e.tensor_tensor(out=Ycs_[:, :], in0=Wsc_[:, :], in1=cosT_[:, :], op=ALU.mult), [tw, tc], [tyc])
                    op("pool", lambda e: e.tensor_tensor(out=Ysn_[:, :], in0=Wsc_[:, :], in1=sinT_[:, :], op=ALU.mult), [tw, tsn], [tys])
                    for q in range(1):
                        nt = hf
                        qs = slice(0, 512)
                        ya, yt = yacc[nt]
                        last = (gl == 7)
                        op("pe", [lambda e, ya=ya, qs=qs: e.matmul(ya[:, :], lhsT=Cp[:, gl, :], rhs=Ycs_[:, qs], start=False, stop=False),
                                  lambda e, ya=ya, qs=qs: e.matmul(ya[:, :], lhsT=Cq[:, gl, :], rhs=Ysn_[:, qs], start=False, stop=last)],
                           ["Cp", "Cq", tyc, tys, yt], [yt])
                    cur["ops"] = None
                    return Tops, Dops

                for hf in range(4):
                    passes.append(one_pass(hf))

            for gl in range(8):
                group(gl)
            def emit(ops):
                for eng, fn, r, w in ops:
                    S.op(eng, fn, reads=r, writes=w)
            emit(passes[0][0])
            for n in range(len(passes)):
                if n + 1 < len(passes):
                    emit(passes[n + 1][0])
                emit(passes[n][1])
            for nt in range(4):
                ts = slice(nt * 512, (nt + 1) * 512)
                ya, yt = yacc[nt]
                x = cosT[:, 0:512]
                x2 = sinT[:, 0:512]
                op("act", lambda e, ya=ya: e.activation(out=x, in_=ya[:, :], func=AF.Identity), [yt], ["cosT"])
                op("dve", lambda e: e.tensor_tensor(out=x2, in0=x, in1=x, op=ALU.mult), ["cosT"], ["sinT"])
                op("dve", lambda e: e.tensor_scalar(out=x2, in0=x2, scalar1=0.044715, scalar2=1.0, op0=ALU.mult, op1=ALU.add), ["sinT"], ["sinT"])
                op("dve", lambda e: e.tensor_tensor(out=x2, in0=x2, in1=x, op=ALU.mult), ["sinT", "cosT"], ["sinT"])
                op("act", lambda e: e.activation(out=x2, in_=x2, func=AF.Sigmoid, scale=1.5957691216057308), ["sinT"], ["sinT"])
                op("dve", lambda e, ts=ts: e.tensor_tensor(out=yT[:, mc, ts], in0=x2, in1=x, op=ALU.mult), ["sinT", "cosT"], [("yT", mc, nt)])

        for mc in range(8 if stop3 > 4 else 1):
            mc_block(mc)
        if stop3 <= 4:
            self.PS.lo, self.PS.hi = 0, 8
            self.phase_end()
            return
        self.PS.lo, self.PS.hi = 0, 8
        YT = [("yT", mc, nt) for mc in range(8) for nt in range(4)]

        def outp(dc):
            b = dc % 2
            dma("pool", wo[b][:, :, 0:128], w_out[:, :, dc * 128:(dc + 1) * 128], ("wo", 0, 0), "wo0a")
            dma("pool", wo[b][:, :, 128:256], w_out[:, :, 1024 + dc * 128:1024 + (dc + 1) * 128], ("wo", 0, 1), "wo0b")
            for nt in range(4):
                ts = slice(nt * 512, (nt + 1) * 512)
                psv, ptv = self.PS.get()
                psg, ptg = self.PS.get()
                S.op("pe", [lambda e, kc=kc, psv=psv, ts=ts: e.matmul(psv[:, :], lhsT=wo[b][:, kc, 0:128], rhs=yT[:, kc, ts],
                                                                     start=(kc == 0), stop=(kc == KC - 1)) for kc in range(KC)],
                     reads=YT + [("wo", 0, 0)], writes=[ptv])
                S.op("pe", [lambda e, kc=kc, psg=psg, ts=ts: e.matmul(psg[:, :], lhsT=wo[b][:, kc, 128:256], rhs=yT[:, kc, ts],
                                                                     start=(kc == 0), stop=(kc == KC - 1)) for kc in range(KC)],
                     reads=YT + [("wo", 0, 1)], writes=[ptg])
                sg = cosT[:, 0:512]
                op("act", lambda e, psg=psg: e.activation(out=sg, in_=psg[:, :], func=AF.Sigmoid), [ptg], ["cosT"])
                op("dve", lambda e, psv=psv: e.tensor_tensor(out=sg, in0=psv[:, :], in1=sg, op=ALU.mult), [ptv, "cosT"], ["cosT"])
                op("dve", lambda e, ts=ts: e.tensor_tensor(out=hT[:, dc, ts], in0=hT[:, dc, ts], in1=sg, op=ALU.add),
                   ["cosT", ("hT", nt, dc)], [("hT", nt, dc)])

        for dc in range(KC):
            outp(dc)
        self.phase_end()

    def build(self):
        nc, S, stack = self.nc, self.S, self.stack
        d = {}
        xT = self.din("xT", [D, L])
        vecs_d = self.din("vecs", [128, NV])
        for name, shape in DRAM_INPUTS:
            d[name] = self.din(name, shape)
        yT = nc.dram_tensor("yT", [D, L], F32, kind="ExternalOutput").ap()

        self.PS = PsumPool(nc, stack)
        self.hT = hT = self.sb("hT", [128, KC, L], F32)
        self.hn = hn = self.sb("hn", [128, KC, L], BF16)
        self.sq = self.sb("sq", [128, KC, 512], BF16)
        self.rstd = self.sb("rstd", [128, 512], F32)
        self.ones_bf = self.sb("ones_bf", [128, 128], BF16)
        self.ones_f = self.sb("ones_f", [128, 128], F32)
        self.ident_bf = self.sb("ident_bf", [128, 128], BF16)
        self.ident_f = self.sb("ident_f", [128, 128], F32)
        self.U = self.sb("U", [128, 128], F32)
        self.SL = self.sb("SL", [128, 128], F32)
        self.eps_col = self.sb("eps_col", [128, 1], F32)
        self.vecs = vecs = self.sb("vecs", [128, NV], F32)

        S.op("dve", lambda e: e.memset(self.ones_bf[:, :], 1.0), writes=["ones"])
        S.op("dve", lambda e: e.memset(self.ones_f[:, :], 1.0), writes=["onesf"])
        S.op("dve", lambda e: e.memset(self.eps_col[:, :], EPS), writes=["eps"])
        S.op("pool", lambda e: e.affine_select(out=self.U[:, :], in_=self.ones_f[:, :], pattern=[[1, 128]], compare_op=ALU.is_ge,
                                               fill=0.0, base=0, channel_multiplier=-1), reads=["onesf"], writes=["U"])
        S.op("pool", lambda e: e.affine_select(out=self.SL[:, :], in_=self.ones_f[:, :], pattern=[[-1, 128]], compare_op=ALU.is_gt,
                                               fill=0.0, base=0, channel_multiplier=1), reads=["onesf"], writes=["SL"])
        S.op("pool", lambda e: e.affine_select(out=self.ident_f[:, :], in_=self.ones_f[:, :], pattern=[[1, 128]], compare_op=ALU.is_equal,
                                               fill=0.0, base=0, channel_multiplier=-1), reads=["onesf"], writes=["identf"])
        S.op("dve", lambda e: e.tensor_copy(out=self.ident_bf[:, :], in_=self.ident_f[:, :]), reads=["identf"], writes=["ident"])
        S.op("sp", lambda e: e.dma_start(out=vecs[:, :], in_=vecs_d[:, :]), writes=["vecs"], dsem="vecs")
        xv = xT.rearrange("(kc p) t -> p kc t", p=128)
        for nt in range(4):
            ts = slice(nt * 512, (nt + 1) * 512)
            S.op("sp", lambda e, ts=ts: e.dma_start(out=hT[:, :, ts], in_=xv[:, :, ts]),
                 writes=[("hT", nt, kc) for kc in range(KC)], dsem="x%d" % nt)
        S.flush()

        for st in self.stages:
            if st[0] == "mlp":
                self.mlp_phase(st[1], d)
            elif st[0] == "final":
                self.final_phase()
            elif st[0] == "mix":
                li = st[1]
                if li % 3 == 2:
                    self.mamba2_phase(li, d)
                elif li % 3 == 0:
                    self.gdn_phase(li, d)
                else:
                    self.s5_phase(li, d)

        S.barrier()
        yv = yT.rearrange("(kc p) t -> p kc t", p=128)
        outtok = []
        for nt in range(4):
            ts = slice(nt * 512, (nt + 1) * 512)
            S.op("sp", lambda e, ts=ts: e.dma_start(out=yv[:, :, ts], in_=hT[:, :, ts]),
                 writes=[("y", nt)], dsem="y%d" % nt)
            outtok.append(("y", nt))
        S.wait_tokens("sp", outtok)
        S.flush()
        self.stack.close()
        return nc


NV = 80 + 160 + 192
DRAM_INPUTS = [
    ("mlp_w1", [4, D, DFF]), ("mlp_w2", [4, DFF, D]),
    ("m2_w_in", [1, D, 6176]), ("m2_w_out", [1, 2048, D]), ("m2_rows", [128, 96 + 2048]),
    ("gdn_w_in", [2, D, 4112]), ("gdn_w_out", [2, D, D]), ("gdn_rows", [2, 128, 144]),
    ("s5_w_in", [1, D, D]), ("s5_w_out", [1, D, 2 * D]), ("s5_prm", [128, 201]),
    ("s5_bT", [128, 2, 64, 64]), ("s5_cpad", [128, 2, 64, 128]),
]


def make_host_inputs(inp):
    v = np.zeros((128, NV), np.float32)

    def colmajor(a):
        return np.ascontiguousarray(a.reshape(-1, 128).T)
    for l in range(4):
        v[:, l * 8:(l + 1) * 8] = colmajor(inp["norm_mix_g"][l])
        v[:, 32 + l * 8:32 + (l + 1) * 8] = colmajor(inp["norm_mlp_g"][l])
    v[:, 64:72] = colmajor(inp["final_norm_g"])
    cw = inp["m2_conv_w"][0]
    cb = inp["m2_conv_b"][0]
    for j in range(32):
        for tap in range(4):
            v[:, 80 + j * 5 + tap] = cw[tap, j * 128:(j + 1) * 128]
        v[:, 80 + j * 5 + 4] = cb[j * 128:(j + 1) * 128]
    m2_rows = np.concatenate([inp["m2_dt_bias"][0], inp["m2_a_log"][0], inp["m2_d"][0], inp["m2_norm_g"][0]])[None, :]
    m2_rows = np.ascontiguousarray(np.broadcast_to(m2_rows, (128, m2_rows.shape[1]))).astype(np.float32)
    gr = []
    for jl in range(2):
        gcw = inp["gdn_conv_w"][jl]
        for jj in range(24):
            for tap in range(4):
                v[:, 240 + jl * 96 + jj * 4 + tap] = gcw[tap, jj * 128:(jj + 1) * 128]
        r = np.concatenate([inp["gdn_dt_bias"][jl], inp["gdn_a_log"][jl], inp["gdn_o_norm_g"][jl]])[None, :]
        gr.append(np.broadcast_to(r, (128, 144)))
    gdn_rows = np.ascontiguousarray(np.stack(gr, 0)).astype(np.float32)
    lre, lim, ldt = inp["s5_lam_re"][0], inp["s5_lam_im"][0], inp["s5_log_dt"][0]
    prm = np.zeros((128, 201), np.float32)
    prm[:, 0:64] = np.concatenate([lre.T, lre.T], 0)
    prm[:, 64:128] = np.concatenate([lim.T, lim.T], 0)
    prm[:, 128:192] = np.broadcast_to(ldt[None, :], (128, 64))
    prm[0:64, 192] = 1.0
    prm[64:128, 192] = -1.0
    prm[:, 193:201] = inp["s5_d"][0].reshape(8, 128).T
    bre, bim = inp["s5_b_re"][0], inp["s5_b_im"][0]
    cre, cim = inp["s5_c_re"][0], inp["s5_c_im"][0]
    bT = np.zeros((128, 2, 64, 64), np.float32)
    cpad = np.zeros((128, 2, 64, 128), np.float32)
    for g in range(64):
        gl = g % 8
        bT[16 * gl:16 * gl + 16, 0, g, :] = bre[g].T
        bT[16 * gl:16 * gl + 16, 1, g, :] = bim[g].T
        cpad[0:64, 0, g, 16 * gl:16 * gl + 16] = cre[g].T
        cpad[64:128, 0, g, 16 * gl:16 * gl + 16] = cim[g].T
        cpad[0:64, 1, g, 16 * gl:16 * gl + 16] = cim[g].T
        cpad[64:128, 1, g, 16 * gl:16 * gl + 16] = cre[g].T
    out = {"vecs": v, "m2_rows": m2_rows, "gdn_rows": gdn_rows, "s5_prm": prm, "s5_bT": bT, "s5_cpad": cpad}
    for k in ("s5_w_in", "s5_w_out"):
        out[k] = np.ascontiguousarray(inp[k], dtype=np.float32)
    for k in ("mlp_w1", "mlp_w2", "m2_w_in", "m2_w_out", "gdn_w_in", "gdn_w_out"):
        out[k] = np.ascontiguousarray(inp[k], dtype=np.float32)
    return out


ALL_STAGES = [("mix", 0), ("mlp", 0), ("mix", 1), ("mlp", 1), ("mix", 2), ("mlp", 2), ("mix", 3), ("mlp", 3), ("final",)]


def run_stages(stages, xT_list, inp, trace=False):
    prog = Prog(stages)
    nc = prog.build()
    host = make_host_inputs(inp)
    n = len(xT_list)
    in_maps = []
    for c in range(n):
        m = dict(host)
        m["xT"] = np.ascontiguousarray(xT_list[c], dtype=np.float32)
        in_maps.append(m)
    res = run_bass_kernel_spmd(nc, in_maps, core_ids=list(range(n)), trace=trace)
    return [r["yT"] for r in res.results], res


def kernel(**inputs):
    x = inputs["x"]
    xT_list = [np.ascontiguousarray(x[b].T) for b in range(x.shape[0])]
    import os
    stages = ALL_STAGES
    if os.environ.get("KSTAGES"):
        stages = [tuple(t) for t in json.loads(os.environ["KSTAGES"])]
    outs, _ = run_stages(stages, xT_list, inputs)
    return np.stack([np.ascontiguousarray(o.T) for o in outs], axis=0).astype(np.float32)
```

```python
import json
import numpy as np
from contextlib import ExitStack
import concourse.bass as bass
import concourse.mybir as mybir
from concourse.bass_utils import run_bass_kernel_spmd

F32 = mybir.dt.float32
BF16 = mybir.dt.bfloat16
AF = mybir.ActivationFunctionType
ALU = mybir.AluOpType

L = 2048
D = 1024
KC = 8
DFF = 4096
EPS = 1e-6


class Sched:
    ENG = ("pe", "act", "dve", "pool", "sp")

    def __init__(self, nc, stack):
        self.nc = nc
        self.stack = stack
        self.items = {e: [] for e in self.ENG}
        self.sems = {}
        self.val = {}
        self.seen = {e: {} for e in self.ENG}
        self.last_w = {}
        self.readers = {}
        self.epoch = {e: 0 for e in self.ENG}
        self.nsem = 0

    def _sem(self, key):
        if key not in self.sems:
            self.nsem += 1
            self.sems[key] = self.stack.enter_context(self.nc.semaphore("s%d" % self.nsem))
            self.val[key] = 0
        return self.sems[key]

    def _engkey(self, e):
        k = (e, self.epoch[e])
        if self.val.get(k, 0) >= 30000:
            self.epoch[e] += 1
            k = (e, self.epoch[e])
        return k

    def op(self, eng, fn, reads=(), writes=(), dsem=None):
        fns = fn if isinstance(fn, (list, tuple)) else [fn]
        deps = {}

        def add(ev):
            k, v = ev
            if deps.get(k, 0) < v:
                deps[k] = v

        for t in reads:
            if t in self.last_w:
                add(self.last_w[t])
            if isinstance(t, tuple) and t[0] == "ps":
                for k, v in self.readers.get(t, {}).items():
                    add((k, v))
        for t in writes:
            if t in self.last_w:
                add(self.last_w[t])
            for k, v in self.readers.get(t, {}).items():
                add((k, v))
        if dsem is None:
            key = self._engkey(eng)
            inc = 1
        else:
            key = ("dma", dsem)
            inc = 16
        self._sem(key)
        self.val[key] += inc
        ev = (key, self.val[key])
        waits = []
        for k, v in deps.items():
            if self.seen[eng].get(k, 0) >= v:
                continue
            self.seen[eng][k] = v
            waits.append((self.sems[k], v, k))
        self.items[eng].append((waits, fns, self.sems[key], inc, key))
        for t in writes:
            self.last_w[t] = ev
            self.readers[t] = {}
        for t in reads:
            r = self.readers.setdefault(t, {})
            if r.get(ev[0], 0) < ev[1]:
                r[ev[0]] = ev[1]
        return ev

    def wait_tokens(self, eng, tokens):
        waits = []
        for t in tokens:
            if t in self.last_w:
                k, v = self.last_w[t]
                if self.seen[eng].get(k, 0) < v:
                    self.seen[eng][k] = v
                    waits.append((self.sems[k], v, k))
        self.items[eng].append((waits, [], None, 0, None))

    def barrier(self):
        snap = dict(self.val)
        for e in self.ENG:
            waits = []
            for k, v in snap.items():
                if v > 0 and self.seen[e].get(k, 0) < v:
                    self.seen[e][k] = v
                    waits.append((self.sems[k], v, k))
            self.items[e].append((waits, [], None, 0, None))
        self.last_w = {}
        self.readers = {}

    def flush(self):
        self.simulate()
        nc = self.nc
        with nc.Block() as block:
            @block.tensor
            def _(e):
                self.emit("pe", e)

            @block.scalar
            def _(e):
                self.emit("act", e)

            @block.vector
            def _(e):
                self.emit("dve", e)

            @block.gpsimd
            def _(e):
                self.emit("pool", e)

            @block.sync
            def _(e):
                self.emit("sp", e)
        self.simvals = getattr(self, "simvals", None)
        self.items = {e: [] for e in self.ENG}

    def simulate(self):
        val = dict(getattr(self, "_simval", {}))
        for k in self.sems:
            val.setdefault(k, 0)
        pc = {e: 0 for e in self.ENG}
        progress = True
        while progress:
            progress = False
            for e in self.ENG:
                while pc[e] < len(self.items[e]):
                    waits, fns, sem, inc, key = self.items[e][pc[e]]
                    if all(val[k] >= v for _, v, k in waits):
                        if key is not None:
                            val[key] += inc
                        pc[e] += 1
                        progress = True
                    else:
                        break
        stuck = {e: (pc[e], len(self.items[e])) for e in self.ENG if pc[e] < len(self.items[e])}
        if stuck:
            for e in stuck:
                waits = self.items[e][pc[e]][0]
                print("STUCK", e, pc[e], [(k, v, val[k]) for _, v, k in waits if val[k] < v])
            raise RuntimeError("schedule deadlock: %s" % stuck)
        self._simval = val

    def emit(self, name, e):
        for waits, fns, sem, inc, _k in self.items[name]:
            for s, v, _kk in waits:
                e.wait_ge(s, v)
            last = None
            for f in fns:
                last = f(e)
            if last is not None and sem is not None:
                last.then_inc(sem, inc)


class PsumPool:
    def __init__(self, nc, stack, n=8):
        self.tiles = [stack.enter_context(nc.psum_tensor("ps%d" % i, [128, 512], F32)) for i in range(n)]
        self.i = 0
        self.n = n
        self.lo, self.hi = 0, n

    def get(self):
        if not (self.lo <= self.i < self.hi):
            self.i = self.lo
        i = self.i
        self.i = self.i + 1
        if self.i >= self.hi:
            self.i = self.lo
        return self.tiles[i], ("ps", i)


class Prog:
    def __init__(self, stages):
        self.stages = stages
        nc = bass.Bass("TRN2", target_bir_lowering=False)
        self.nc = nc
        self.stack = ExitStack()
        self.S = Sched(nc, self.stack)
        self.dram = {}

    def din(self, name, shape, dt=F32):
        t = self.nc.dram_tensor(name, list(shape), dt, kind="ExternalInput")
        self.dram[name] = t
        return t.ap()

    def sb(self, name, shape, dt):
        return self.stack.enter_context(self.nc.sbuf_tensor("sb_" + name, list(shape), dt))

    def rmsnorm_T(self, gcol, out_bf, tag):
        S, nc = self.S, self.nc
        hT, sq, ones = self.hT, self.sq, self.ones_bf
        for nt in range(4):
            ts = slice(nt * 512, (nt + 1) * 512)
            S.op("act", lambda e, ts=ts: e.activation(out=sq[:, :, :], in_=hT[:, :, ts], func=AF.Square),
                 reads=[("hT", nt, kc) for kc in range(KC)], writes=["sq"])
            ps, pt = self.PS.get()
            fns = [lambda e, kc=kc, ps=ps: e.matmul(ps[:, :], lhsT=ones[:, :], rhs=sq[:, kc, :],
                                                       start=(kc == 0), stop=(kc == KC - 1)) for kc in range(KC)]
            S.op("pe", fns, reads=["sq", "ones"], writes=[pt])
            rs = self.rstd
            S.op("act", lambda e, ps=ps: e.activation(out=rs[:, :], in_=ps[:, :], func=AF.Sqrt,
                                                     scale=1.0 / D, bias=self.eps_col[:, 0:1]),
                 reads=[pt, "eps"], writes=["rstd"])
            S.op("dve", lambda e: e.reciprocal(out=rs[:, :], in_=rs[:, :]), reads=["rstd"], writes=["rstd"])
            for kc in range(KC):
                S.op("dve", lambda e, kc=kc, ts=ts: e.scalar_tensor_tensor(
                    out=out_bf[:, kc, ts], in0=hT[:, kc, ts], scalar=gcol(kc), in1=rs[:, :],
                    op0=ALU.mult, op1=ALU.mult),
                    reads=[("hT", nt, kc), "rstd", "vecs"], writes=[(tag, nt, kc)])

    def mlp(self, l, w1_d, w2_d):
        S, nc = self.S, self.nc
        hT, hn = self.hT, self.hn
        gcol = lambda kc: self.vecs[:, 32 + l * 8 + kc:32 + l * 8 + kc + 1]
        w1v = w1_d[l].rearrange("(kc p) f -> p kc f", p=128)
        w2v = w2_d[l].rearrange("(fc p) d -> p fc d", p=128)
        NG = 8

        def load_w1(g):
            b = g % 2
            S.op("pool", lambda e: e.dma_start(out=self.w1b[b][:, :, :], in_=w1v[:, :, g * 512:(g + 1) * 512]),
                 writes=[("w1b", b)], dsem="w1b%d" % b)

        def load_w2(g):
            b = g % 2
            S.op("pool", lambda e: e.dma_start(out=self.w2b[b][:, :, :], in_=w2v[:, g * 4:(g + 1) * 4, :]),
                 writes=[("w2b", b)], dsem="w2b%d" % b)

        def W1(g):
            b = g % 2
            w1 = self.w1b[b]
            aT = self.aT[b]
            for nt in range(4):
                ts = slice(nt * 512, (nt + 1) * 512)
                for mc in range(4):
                    ps, pt = self.PS.get()
                    fns = [lambda e, kc=kc, ps=ps, mc=mc, ts=ts: e.matmul(
                        ps[:, :], lhsT=w1[:, kc, mc * 128:(mc + 1) * 128], rhs=hn[:, kc, ts],
                        start=(kc == 0), stop=(kc == KC - 1)) for kc in range(KC)]
                    S.op("pe", fns, reads=[("w1b", b)] + [("hn", nt, kc) for kc in range(KC)], writes=[pt])
                    r = self.rbuf[self.ri % 2]
                    rt = ("rbuf", self.ri % 2)
                    self.ri += 1
                    S.op("act", lambda e, ps=ps, r=r: e.activation(out=r[:, :], in_=ps[:, :], func=AF.Relu),
                         reads=[pt], writes=[rt])
                    S.op("act", lambda e, r=r, mc=mc, ts=ts: e.activation(out=aT[:, mc, ts], in_=r[:, :], func=AF.Square),
                         reads=[rt], writes=[("aT", b, nt, mc)])

        def W2(g):
            b = g % 2
            w2 = self.w2b[b]
            aT = self.aT[b]
            for nt in range(4):
                ts = slice(nt * 512, (nt + 1) * 512)
                for dc in range(KC):
                    ps, pt = self.PS.get()
                    fns = [lambda e, mc=mc, ps=ps, dc=dc, ts=ts: e.matmul(
                        ps[:, :], lhsT=w2[:, mc, dc * 128:(dc + 1) * 128], rhs=aT[:, mc, ts],
                        start=(mc == 0), stop=(mc == 3)) for mc in range(4)]
                    S.op("pe", fns, reads=[("w2b", b)] + [("aT", b, nt, mc) for mc in range(4)], writes=[pt])
                    S.op("dve", lambda e, ps=ps, dc=dc, ts=ts: e.tensor_tensor(
                        out=hT[:, dc, ts], in0=ps[:, :], in1=hT[:, dc, ts], op=ALU.add),
                        reads=[pt, ("hT", nt, dc)], writes=[("hT", nt, dc)])

        import os
        dbg = int(os.environ.get("KDBG", "9"))
        load_w1(0); load_w2(0); load_w1(1); load_w2(1)
        if dbg <= 1:
            return
        self.rmsnorm_T(gcol, hn, "hn")
        W1(0)
        if dbg <= 2:
            return
        if dbg == 3:
            NG = 2
        if dbg == 4:
            NG = 3
        for g in range(NG):
            if g + 1 < NG:
                W1(g + 1)
            if g + 2 < NG:
                load_w1(g + 2)
            W2(g)
            if g + 2 < NG:
                load_w2(g + 2)

    def final(self):
        S = self.S
        hT = self.hT
        rs = self.rstd
        gcol = lambda kc: self.vecs[:, 64 + kc:64 + kc + 1]
        self.rmsnorm_T(gcol, hT, "hT")

    def phase_begin(self):
        self.ph = ExitStack()
        self.S.barrier()

    def phase_end(self):
        self.S.flush()
        self.ph.close()

    def A(self, name, shape, dt):
        self.nbuf = getattr(self, "nbuf", 0) + 1
        return self.ph.enter_context(self.nc.sbuf_tensor("p%d_%s" % (self.nbuf, name), list(shape), dt))

    def mlp_phase(self, l, d):
        self.phase_begin()
        self.w1b = [self.A("w1b%d" % i, [128, KC, 512], BF16) for i in range(2)]
        self.w2b = [self.A("w2b%d" % i, [128, 4, D], BF16) for i in range(2)]
        self.aT = [self.A("aT%d" % i, [128, 4, L], BF16) for i in range(2)]
        self.rbuf = [self.A("rbuf%d" % i, [128, 512], BF16) for i in range(2)]
        self.ri = 0
        self.mlp(l, d["mlp_w1"], d["mlp_w2"])
        self.phase_end()

    def final_phase(self):
        self.phase_begin()
        self.final()
        self.phase_end()

    def mamba2_phase(self, li, d):
        self.phase_begin()
        import os
        stop = float(os.environ.get("KDBG2", "99"))
        S, nc, hT, hn, vecs = self.S, self.nc, self.hT, self.hn, self.vecs
        A = self.A
        U, SL, ident, onesf = self.U, self.SL, self.ident_bf, self.ones_f
        w_in = d["m2_w_in"][0].rearrange("(kc p) f -> p kc f", p=128)
        w_out = d["m2_w_out"][0].rearrange("(c p) f -> p c f", p=128)
        gcol = lambda kc: vecs[:, li * 8 + kc:li * 8 + kc + 1]
        self.rmsnorm_T(gcol, hn, "hn")
        HN = [("hn", nt, kc) for nt in range(4) for kc in range(KC)]

        rows = A("rows", [128, 96], F32)
        ngd = d["m2_rows"][:, 96:96 + 2048]
        ng = [A("ng%d" % i, [128, 256], F32) for i in range(2)]
        S.op("sp", lambda e: e.dma_start(out=rows[:, :], in_=d["m2_rows"][:, 0:96]), writes=["rows"], dsem="rows")
        dtb, alog, Dh = rows[:, 0:32], rows[:, 32:64], rows[:, 64:96]
        wdt = A("wdt", [128, KC, 32], BF16)
        S.op("pool", lambda e: e.dma_start(out=wdt[:, :, :], in_=w_in[:, :, 6144:6176]), writes=["wdt"], dsem="wdt")
        if stop <= 1:
            self.phase_end()
            return
        wxbc = A("wxbc", [128, KC, 512], BF16)
        wz = [A("wz0", [128, KC, 256], BF16)] * 2
        wo = [A("wo%d" % i, [128, 2, D], BF16) for i in range(1)]

        def dma(eng, out, in_, wtok, sem, reads=()):
            S.op(eng, lambda e: e.dma_start(out=out, in_=in_), reads=list(reads), writes=[wtok], dsem=sem)

        def load_group(g):
            b = g % 2
            dma("pool", wxbc[:, :, 0:256], w_in[:, :, 2048 + 256 * g:2048 + 256 * (g + 1)], ("wxbc", 0), "wxbc0")
            dma("pool", wxbc[:, :, 256:384], w_in[:, :, 4096 + 128 * g:4096 + 128 * (g + 1)], ("wxbc", 1), "wxbc1")
            dma("pool", wxbc[:, :, 384:512], w_in[:, :, 5120 + 128 * g:5120 + 128 * (g + 1)], ("wxbc", 2), "wxbc2")
            dma("pool", wz[0][:, :, :], w_in[:, :, 256 * g:256 * (g + 1)], "wz", "wz0")
            dma("pool", wo[0][:, :, :], w_out[:, 2 * g:2 * g + 2, :], ("wo", 0), "wo0")
            dma("sp", ng[b][:, :], ngd[:, 256 * g:256 * (g + 1)], ("ng", b), "ng%d" % b)

        dt = A("dt", [128, 16, 32], F32)
        dA = A("dA", [128, 16, 32], F32)
        ecum = A("ecum", [128, 16, 32], F32)
        ed = A("ed", [128, 16, 32], F32)
        cdrep = A("cdrep", [128, 16, 32], F32)
        negA = A("negA", [128, 32], F32)
        HL = L // 2
        cbufs = [A("cbufb0", [128, 3 + L], BF16)] * 2
        dgs = [[A("dg0_%d" % j, [128, 128], BF16) for j in range(4)]] * 2
        accc = A("accc", [128, 512], F32)
        cum = accc[:, :].rearrange("p (c h) -> p c h", h=32)
        ps, pt = self.PS.get()
        for c in range(16):
            fns = [lambda e, kc=kc, c=c, ps=ps: e.matmul(ps[:, c * 32:(c + 1) * 32], lhsT=hn[:, kc, c * 128:(c + 1) * 128],
                                                        rhs=wdt[:, kc, :], start=(kc == 0), stop=(kc == KC - 1))
                   for kc in range(KC)]
            S.op("pe", fns, reads=HN + ["wdt"], writes=[pt])
        psv = lambda p: p[:, :].rearrange("p (c h) -> p c h", h=32)
        S.op("dve", lambda e, ps=ps: e.tensor_tensor(out=dt[:, :, :], in0=psv(ps), in1=dtb.unsqueeze(1).to_broadcast([128, 16, 32]),
                                                    op=ALU.add), reads=[pt, "rows"], writes=["dt"])
        if stop <= 1.1:
            self.phase_end()
            return
        S.op("act", lambda e: e.activation(out=dt[:, :, :], in_=dt[:, :, :], func=AF.Exp), reads=["dt"], writes=["dt"])
        S.op("act", lambda e: e.activation(out=dt[:, :, :], in_=dt[:, :, :], func=AF.Ln, bias=1.0), reads=["dt"], writes=["dt"])
        if stop <= 1.2:
            self.phase_end()
            return
        S.op("act", lambda e: e.activation(out=negA[:, :], in_=alog, func=AF.Exp), reads=["rows"], writes=["negA"])
        S.op("dve", lambda e: e.tensor_scalar(out=negA[:, :], in0=negA[:, :], scalar1=-1.0, scalar2=None, op0=ALU.mult),
             reads=["negA"], writes=["negA"])
        S.op("dve", lambda e: e.tensor_tensor(out=dA[:, :, :], in0=dt[:, :, :], in1=negA[:, :].unsqueeze(1).to_broadcast([128, 16, 32]),
                                              op=ALU.mult), reads=["dt", "negA"], writes=["dA"])
        if stop <= 1.3:
            self.phase_end()
            return
        ps1, pt1 = self.PS.get()
        ps2, pt2 = self.PS.get()
        S.op("pe", [lambda e, c=c: e.matmul(ps1[:, c * 32:(c + 1) * 32], lhsT=U[:, :], rhs=dA[:, c, :], start=True, stop=True)
                    for c in range(16)], reads=["dA", "U"], writes=[pt1])
        S.op("pe", [lambda e, c=c: e.matmul(ps2[:, c * 32:(c + 1) * 32], lhsT=onesf[:, :], rhs=dA[:, c, :], start=True, stop=True)
                    for c in range(16)], reads=["dA", "onesf"], writes=[pt2])
        if stop <= 1.4:
            self.phase_end()
            return
        S.op("act", lambda e: e.activation(out=cum, in_=psv(ps1), func=AF.Identity), reads=[pt1], writes=["cum"])
        if stop <= 1.5:
            self.phase_end()
            return
        S.op("act", lambda e: e.activation(out=ecum[:, :, :], in_=psv(ps1), func=AF.Exp), reads=[pt1], writes=["ecum"])
        if stop <= 1.6:
            self.phase_end()
            return

        S.op("act", lambda e: e.activation(out=cdrep[:, :, :], in_=psv(ps2), func=AF.Exp), reads=[pt2], writes=["cdrep"])
        if stop <= 1.7:
            self.phase_end()
            return
        S.op("dve", lambda e: e.tensor_tensor(out=ed[:, :, :], in0=psv(ps2), in1=cum, op=ALU.subtract),
             reads=[pt2, "cum"], writes=["ed"])
        if stop <= 1.8:
            self.phase_end()
            return
        S.op("act", lambda e: e.activation(out=ed[:, :, :], in_=ed[:, :, :], func=AF.Exp), reads=["ed"], writes=["ed"])
        DEC = ["dt", "dA", "cum", "ecum", "ed", "cdrep"]
        if stop <= 2:
            self.phase_end()
            return

        fm = [A("fm%d" % i, [128, L], BF16) for i in range(4)]
        xB = A("xB", [128, 16, 384], BF16)
        yTg = A("yTg", [128, 2, L], BF16)
        S32 = A("S32", [128, 256], F32)
        Sbf = A("Sbf", [128, 256], BF16)
        zsall = A("zsall", [128, 16, 256], BF16)
        cbU = [A("cbU0", [128, 128], F32)] * 2
        rhsD = [A("rhsD0", [128, 4, 128], F32)] * 2
        E = [A("E0", [128, 4, 128], F32)] * 2
        Mt = [A("Mt%d" % i, [128, 4, 128], BF16) for i in range(2)]
        xdt = [A("xdt%d" % i, [128, 256], BF16) for i in range(2)]
        xdtd = [A("xdtd%d" % i, [128, 256], BF16) for i in range(2)]
        t1 = [A("t1%d" % i, [128, 256], F32) for i in range(2)]
        t2 = [A("t2%d" % i, [128, 256], F32) for i in range(1)] * 2
        yn = [A("yn%d" % i, [128, 256], BF16) for i in range(2)]
        ss = [A("ss%d" % i, [128, 1], F32) for i in range(2)]
        S.op("dve", lambda e: e.memset(cbufs[0][:, 0:3], 0.0), writes=[("cbuf", 0, "pad")])

        load_group(0)
        def group(g):
            b = g % 2
            for ch in range(4):
                jj = [2 * g, 2 * g + 1, 16 + g, 24 + g][ch]
                cw = lambda tap, jj=jj: vecs[:, 80 + jj * 5 + tap:80 + jj * 5 + tap + 1]
                cbi = 0
                cbuf = cbufs[cbi]
                dg = dgs[cbi]
                for tap in range(4):
                    S.op("dve", lambda e, tap=tap, dg=dg, cw=cw: e.tensor_scalar(out=dg[tap][:, :], in0=ident[:, :], scalar1=cw(tap), scalar2=None, op0=ALU.mult),
                         reads=["ident", "vecs"], writes=[("dg", cbi, tap)])
                for nt in range(4):
                    ts = slice(nt * 512, (nt + 1) * 512)
                    ps, pt = self.PS.get()
                    S.op("pe", [lambda e, kc=kc, ps=ps, ch=ch, ts=ts: e.matmul(
                        ps[:, :], lhsT=wxbc[:, kc, ch * 128:(ch + 1) * 128], rhs=hn[:, kc, ts],
                        start=(kc == 0), stop=(kc == KC - 1)) for kc in range(KC)],
                        reads=HN + [("wxbc", 0), ("wxbc", 1), ("wxbc", 2)], writes=[pt])
                    S.op("act", lambda e, ps=ps, nt=nt, cbuf=cbuf: e.activation(out=cbuf[:, 3 + nt * 512:3 + (nt + 1) * 512], in_=ps[:, :], func=AF.Identity),
                         reads=[pt], writes=[("cbuf", cbi, nt)])
                for nt in range(4):
                    ps, pt = self.PS.get()
                    rd = [("cbuf", cbi, nt), ("cbuf", cbi, "pad")] + ([("cbuf", cbi, nt - 1)] if nt > 0 else []) + [("dg", cbi, t_) for t_ in range(4)]
                    S.op("pe", [lambda e, tap=tap, ps=ps, nt=nt, cbuf=cbuf, dg=dg: e.matmul(
                        ps[:, :], lhsT=dg[tap][:, :], rhs=cbuf[:, tap + nt * 512:tap + (nt + 1) * 512],
                        start=(tap == 0), stop=(tap == 3)) for tap in range(4)], reads=rd, writes=[pt])
                    S.op("act", lambda e, ps=ps, nt=nt, ch=ch, cw=cw: e.activation(out=fm[ch][:, nt * 512:(nt + 1) * 512], in_=ps[:, :], func=AF.Silu, bias=cw(4)),
                         reads=[pt, "vecs"], writes=[("fm", ch, nt // 2, nt % 2)])
            if g + 1 < 8:
                pass
            if stop <= 3:
                return
            for c in range(16):
                ct = slice(c * 128, (c + 1) * 128)
                ps, pt = self.PS.get()
                S.op("pe", [lambda e, ch=ch, ps=ps, ct=ct: e.matmul(ps[:, ch * 128:(ch + 1) * 128], lhsT=fm[ch][:, ct], rhs=ident[:, :],
                                                                   start=True, stop=True) for ch in range(3)],
                     reads=[("fm", ch, hf, q) for ch in range(3) for hf in range(2) for q in range(2)] + ["ident"], writes=[pt])
                S.op("act", lambda e, ps=ps, c=c: e.activation(out=xB[:, c, :], in_=ps[:, 0:384], func=AF.Identity),
                     reads=[pt], writes=[("xB", c)])
            S.op("dve", lambda e: e.memset(S32[:, :], 0.0), writes=["S32"])
            S.op("dve", lambda e: e.memset(Sbf[:, :], 0.0), writes=["Sbf"])
            if stop <= 4:
                return
            hs = slice(4 * g, 4 * g + 4)
            for cq in range(8):
                ps, pt = self.PS.get()
                for cc in range(2):
                    c_ = 2 * cq + cc
                    S.op("pe", [lambda e, kc=kc, ps=ps, cc=cc, c_=c_: e.matmul(ps[:, cc * 256:(cc + 1) * 256], lhsT=hn[:, kc, c_ * 128:(c_ + 1) * 128],
                                                                             rhs=wz[0][:, kc, :], start=(kc == 0), stop=(kc == KC - 1)) for kc in range(KC)],
                         reads=HN + ["wz"], writes=[pt])
                S.op("act", lambda e, ps=ps, cq=cq: e.activation(out=zsall[:, 2 * cq:2 * cq + 2, :], in_=ps[:, :].rearrange("p (c v) -> p c v", v=256), func=AF.Silu),
                     reads=[pt], writes=["zsall"])
            v3 = lambda ap: ap.rearrange("p (h q) -> p h q", q=64)
            cur = {"ops": None}

            def sop(eng, fn, reads=(), writes=()):
                cur["ops"].append((eng, fn, list(reads), list(writes)))

            def chunk(c):
                ct = slice(c * 128, (c + 1) * 128)
                i = c % 2
                prep, tail = [], []
                cur["ops"] = prep
                self.PS.lo, self.PS.hi = 0, 4
                ps, pt = self.PS.get()
                sop("pe", lambda e, ps=ps: e.matmul(ps[:, 0:128], lhsT=fm[2][:, ct], rhs=fm[3][:, ct], start=True, stop=True),
                    reads=[("fm", ch, hf, q) for ch in (2, 3) for hf in range(2) for q in range(2)], writes=[pt])
                sop("dve", lambda e, ps=ps: e.tensor_tensor(out=cbU[i][:, :], in0=ps[:, 0:128], in1=U[:, :], op=ALU.mult),
                    reads=[pt, "U"], writes=["cbU"])
                sop("dve", lambda e: e.tensor_tensor(
                    out=rhsD[i][:, :, :], in0=U[:, :].unsqueeze(1).to_broadcast([128, 4, 128]),
                    in1=dA[:, c, hs].unsqueeze(2).to_broadcast([128, 4, 128]), op=ALU.mult),
                    reads=["U", "dA"], writes=["rhsD"])
                psD, ptD = self.PS.get()
                sop("pe", lambda e: e.matmul(psD[:, :], lhsT=SL[:, :], rhs=rhsD[i][:, :, :].rearrange("p h l -> p (h l)"),
                                             start=True, stop=True), reads=["rhsD", "SL"], writes=[ptD])
                sop("act", lambda e: e.activation(out=E[i][:, :, :].rearrange("p h l -> p (h l)"), in_=psD[:, :], func=AF.Exp),
                    reads=[ptD], writes=["E"])
                sop("dve", lambda e: e.tensor_tensor(out=Mt[i][:, :, :], in0=E[i][:, :, :],
                                                     in1=cbU[i][:, :].unsqueeze(1).to_broadcast([128, 4, 128]), op=ALU.mult),
                    reads=["E", "cbU"], writes=[("Mt", i)])
                sop("dve", lambda e: e.tensor_tensor(
                    out=v3(xdt[i][:, :]), in0=v3(xB[:, c, 0:256]),
                    in1=dt[:, c, hs].unsqueeze(2).to_broadcast([128, 4, 64]), op=ALU.mult),
                    reads=[("xB", c), "dt"], writes=[("xdt", i)])
                sop("dve", lambda e: e.tensor_tensor(
                    out=v3(xdtd[i][:, :]), in0=v3(xdt[i][:, :]),
                    in1=ed[:, c, hs].unsqueeze(2).to_broadcast([128, 4, 64]), op=ALU.mult),
                    reads=[("xdt", i), "ed"], writes=[("xdtd", i)])
                psy, pty = self.PS.get()
                sop("pe", [lambda e, h=h: e.matmul(psy[:, h * 64:(h + 1) * 64], lhsT=Mt[i][:, h, :], rhs=xdt[i][:, h * 64:(h + 1) * 64],
                                                  start=True, stop=True) for h in range(4)],
                    reads=[("Mt", i), ("xdt", i)], writes=[pty])
                sop("dve", lambda e: e.tensor_tensor(
                    out=v3(t2[i][:, :]), in0=v3(xB[:, c, 0:256]), in1=Dh[:, hs].unsqueeze(2).to_broadcast([128, 4, 64]), op=ALU.mult),
                    reads=[("xB", c), "rows"], writes=[("t2", i)])
                sop("dve", lambda e: e.tensor_tensor(out=t2[i][:, :], in0=psy[:, 0:256], in1=t2[i][:, :], op=ALU.add),
                    reads=[pty, ("t2", i)], writes=[("t2", i)])
                cur["ops"] = tail
                self.PS.lo, self.PS.hi = 4, 8
                pso, pto = self.PS.get()
                sop("pe", lambda e: e.matmul(pso[:, 0:256], lhsT=fm[3][:, ct], rhs=Sbf[:, :], start=True, stop=True),
                    reads=[("fm", 3, hf, q) for hf in range(2) for q in range(2)] + ["Sbf"], writes=[pto])
                pss, pts = self.PS.get()
                sop("pe", lambda e: e.matmul(pss[:, 0:256], lhsT=xB[:, c, 256:384], rhs=xdtd[i][:, :], start=True, stop=True),
                    reads=[("xB", c), ("xdtd", i)], writes=[pts])
                sop("dve", lambda e: e.tensor_tensor(
                    out=v3(t1[i][:, :]), in0=v3(pso[:, 0:256]), in1=ecum[:, c, hs].unsqueeze(2).to_broadcast([128, 4, 64]), op=ALU.mult),
                    reads=[pto, "ecum"], writes=[("t1", i)])
                sop("dve", lambda e: e.tensor_tensor(out=v3(S32[:, :]), in0=v3(S32[:, :]),
                                                     in1=cdrep[:, c, hs].unsqueeze(2).to_broadcast([128, 4, 64]), op=ALU.mult),
                    reads=["S32", "cdrep"], writes=["S32"])
                sop("dve", lambda e: e.tensor_tensor(out=S32[:, :], in0=pss[:, 0:256], in1=S32[:, :], op=ALU.add),
                    reads=[pts, "S32"], writes=["S32"])
                sop("act", lambda e: e.activation(out=Sbf[:, :], in_=S32[:, :], func=AF.Identity), reads=["S32"], writes=["Sbf"])
                sop("dve", lambda e: e.tensor_tensor(out=t1[i][:, :], in0=t1[i][:, :], in1=t2[i][:, :], op=ALU.add),
                    reads=[("t1", i), ("t2", i)], writes=[("t1", i)])
                sop("dve", lambda e: e.tensor_tensor(out=t1[i][:, :], in0=t1[i][:, :], in1=zsall[:, c, :], op=ALU.mult),
                    reads=[("t1", i), "zsall"], writes=[("t1", i)])
                sop("act", lambda e: e.activation(out=yn[i][:, :], in_=t1[i][:, :], func=AF.Square, accum_out=ss[i][:, 0:1]),
                    reads=[("t1", i)], writes=[("yn", i), ("ss", i)])
                sop("act", lambda e: e.activation(out=ss[i][:, :], in_=ss[i][:, :], func=AF.Ln, scale=1.0 / 256, bias=self.eps_col[:, 0:1]),
                    reads=[("ss", i), "eps"], writes=[("ss", i)])
                sop("act", lambda e: e.activation(out=ss[i][:, :], in_=ss[i][:, :], func=AF.Exp, scale=-0.5), reads=[("ss", i)], writes=[("ss", i)])
                sop("dve", lambda e: e.scalar_tensor_tensor(
                    out=yn[i][:, :], in0=t1[i][:, :], scalar=ss[i][:, 0:1], in1=ng[b][:, :], op0=ALU.mult, op1=ALU.mult),
                    reads=[("t1", i), ("ss", i), ("ng", b)], writes=[("yn", i)])
                psT, ptT = self.PS.get()
                sop("pe", [lambda e, j=j: e.matmul(psT[:, j * 128:(j + 1) * 128], lhsT=yn[i][:, j * 128:(j + 1) * 128], rhs=ident[:, :],
                                                  start=True, stop=True) for j in range(2)],
                    reads=[("yn", i), "ident"], writes=[ptT])
                sop("act", lambda e: e.activation(out=yTg[:, :, ct], in_=psT[:, 0:256].rearrange("p (j l) -> p j l", l=128), func=AF.Identity),
                    reads=[ptT], writes=[("yTg", c)])
                return prep, tail

            def zipl(x, y):
                out = []
                for k in range(max(len(x), len(y))):
                    if k < len(x):
                        out.append(x[k])
                    if k < len(y):
                        out.append(y[k])
                return out

            pts_ = [chunk(c) for c in range(16)]
            self.PS.lo, self.PS.hi = 0, 8
            stream = list(pts_[0][0])
            for c in range(16):
                stream += zipl(pts_[c][1], pts_[c + 1][0] if c + 1 < 16 else [])
            for eng, fn, r, w in stream:
                S.op(eng, fn, reads=r, writes=w)
            YT = [("yTg", c) for c in range(16)]
            for nt in range(4):
                ts = slice(nt * 512, (nt + 1) * 512)
                for dc in range(KC):
                    ps, pt = self.PS.get()
                    S.op("pe", [lambda e, j=j, ps=ps, dc=dc, ts=ts: e.matmul(ps[:, :], lhsT=wo[0][:, j, dc * 128:(dc + 1) * 128], rhs=yTg[:, j, ts],
                                                                            start=(j == 0), stop=(j == 1)) for j in range(2)],
                         reads=YT + [("wo", 0)], writes=[pt])
                    S.op("dve", lambda e, ps=ps, dc=dc, ts=ts: e.tensor_tensor(out=hT[:, dc, ts], in0=ps[:, :], in1=hT[:, dc, ts], op=ALU.add),
                         reads=[pt, ("hT", nt, dc)], writes=[("hT", nt, dc)])
        for g in range(8 if stop > 7 else 1):
            group(g)
            if g + 1 < 8 and stop > 7:
                load_group(g + 1)
        self.phase_end()

    def gdn_phase(self, li, d):
        self.phase_begin()
        S, nc, hT, hn, vecs = self.S, self.nc, self.hT, self.hn, self.vecs
        A = self.A
        U, SL, ident, identf, onesf, ones_bf = self.U, self.SL, self.ident_bf, self.ident_f, self.ones_f, self.ones_bf
        jl = li // 3
        w_in = d["gdn_w_in"][jl].rearrange("(kc p) f -> p kc f", p=128)
        w_out = d["gdn_w_out"][jl]
        gcol = lambda kc: vecs[:, li * 8 + kc:li * 8 + kc + 1]
        self.rmsnorm_T(gcol, hn, "hn")
        HN = [("hn", nt, kc) for nt in range(4) for kc in range(KC)]
        CW0 = 240 + jl * 96

        def dma(eng, out, in_, wtok, sem, reads=()):
            S.op(eng, lambda e: e.dma_start(out=out, in_=in_), reads=list(reads), writes=[wtok], dsem=sem)

        rows = A("rows", [128, 144], F32)
        dma("sp", rows[:, :], d["gdn_rows"][jl], "rows", "rows")
        dtb, alog, ong = rows[:, 0:8], rows[:, 8:16], rows[:, 16:144]
        wab = A("wab", [128, KC, 16], BF16)
        dma("pool", wab[:, :, :], w_in[:, :, 4096:4112], "wab", "wab")
        wqk = [A("wqkv%d" % i, [128, KC, 128], BF16) for i in range(2)]
        wgate = [A("wgate0", [128, KC, 128], BF16)] * 2
        wos = [A("wo%d" % i, [128, D], BF16) for i in range(2)]

        def load_head(h, slot):
            b = slot
            wo = wos[slot]
            for q in range(2):
                dma("pool", wqk[q % 2][:, :, :], w_in[:, :, q * 1024 + 128 * h:q * 1024 + 128 * (h + 1)], ("wqkv", q % 2), "wqkv%d" % (q % 2))
            dma("pool", wgate[0][:, :, :], w_in[:, :, 3072 + 128 * h:3072 + 128 * (h + 1)], "wgate", "wgate0")
            dma("pool", wo[:, :], w_out[128 * h:128 * (h + 1), :], (slot, "wo"), "wo%d" % slot)

        F3 = lambda nm: A(nm, [128, 16, 8], F32)
        gg, beta, nbeg, eG, ed, cdrep, G = F3("gg"), F3("beta"), F3("nbeg"), F3("eG"), F3("ed"), F3("cdrep"), F3("G")
        negA = A("negA", [128, 8], F32)
        ps, pt = self.PS.get()
        for c in range(16):
            S.op("pe", [lambda e, kc=kc, c=c, ps=ps: e.matmul(ps[:, c * 16:(c + 1) * 16], lhsT=hn[:, kc, c * 128:(c + 1) * 128],
                                                            rhs=wab[:, kc, :], start=(kc == 0), stop=(kc == KC - 1)) for kc in range(KC)],
                 reads=HN + ["wab"], writes=[pt])
        pab = ps[:, 0:256].rearrange("p (c t) -> p c t", t=16)
        S.op("dve", lambda e: e.tensor_tensor(out=gg[:, :, :], in0=pab[:, :, 0:8], in1=dtb.unsqueeze(1).to_broadcast([128, 16, 8]), op=ALU.add),
             reads=[pt, "rows"], writes=["gg"])
        S.op("act", lambda e: e.activation(out=beta[:, :, :], in_=pab[:, :, 8:16], func=AF.Sigmoid), reads=[pt], writes=["beta"])
        S.op("act", lambda e: e.activation(out=gg[:, :, :], in_=gg[:, :, :], func=AF.Exp), reads=["gg"], writes=["gg"])
        S.op("act", lambda e: e.activation(out=gg[:, :, :], in_=gg[:, :, :], func=AF.Ln, bias=1.0), reads=["gg"], writes=["gg"])
        S.op("act", lambda e: e.activation(out=negA[:, :], in_=alog, func=AF.Exp), reads=["rows"], writes=["negA"])
        S.op("dve", lambda e: e.tensor_scalar(out=negA[:, :], in0=negA[:, :], scalar1=-1.0, scalar2=None, op0=ALU.mult),
             reads=["negA"], writes=["negA"])
        S.op("dve", lambda e: e.tensor_tensor(out=gg[:, :, :], in0=gg[:, :, :], in1=negA[:, :].unsqueeze(1).to_broadcast([128, 16, 8]), op=ALU.mult),
             reads=["gg", "negA"], writes=["gg"])
        ps1, pt1 = self.PS.get()
        ps2, pt2 = self.PS.get()
        S.op("pe", [lambda e, c=c: e.matmul(ps1[:, c * 8:(c + 1) * 8], lhsT=U[:, :], rhs=gg[:, c, :], start=True, stop=True)
                    for c in range(16)], reads=["gg", "U"], writes=[pt1])
        S.op("pe", [lambda e, c=c: e.matmul(ps2[:, c * 8:(c + 1) * 8], lhsT=onesf[:, :], rhs=gg[:, c, :], start=True, stop=True)
                    for c in range(16)], reads=["gg", "onesf"], writes=[pt2])
        pv = lambda p: p[:, 0:128].rearrange("p (c h) -> p c h", h=8)
        S.op("act", lambda e: e.activation(out=G[:, :, :], in_=pv(ps1), func=AF.Identity), reads=[pt1], writes=["G"])
        S.op("act", lambda e: e.activation(out=eG[:, :, :], in_=pv(ps1), func=AF.Exp), reads=[pt1], writes=["eG"])
        S.op("act", lambda e: e.activation(out=cdrep[:, :, :], in_=pv(ps2), func=AF.Exp), reads=[pt2], writes=["cdrep"])
        S.op("dve", lambda e: e.tensor_tensor(out=ed[:, :, :], in0=pv(ps2), in1=G[:, :, :], op=ALU.subtract), reads=[pt2, "G"], writes=["ed"])
        S.op("act", lambda e: e.activation(out=ed[:, :, :], in_=ed[:, :, :], func=AF.Exp), reads=["ed"], writes=["ed"])
        S.op("dve", lambda e: e.tensor_tensor(out=nbeg[:, :, :], in0=beta[:, :, :], in1=eG[:, :, :], op=ALU.mult), reads=["beta", "eG"], writes=["nbeg"])
        S.op("dve", lambda e: e.tensor_scalar(out=nbeg[:, :, :], in0=nbeg[:, :, :], scalar1=-1.0, scalar2=None, op0=ALU.mult),
             reads=["nbeg"], writes=["nbeg"])
        nbeta = F3("nbeta")
        S.op("dve", lambda e: e.tensor_scalar(out=nbeta[:, :, :], in0=beta[:, :, :], scalar1=-1.0, scalar2=None, op0=ALU.mult),
             reads=["beta"], writes=["nbeta"])
        DECR = ["gg", "beta", "nbeg", "eG", "ed", "cdrep", "nbeta"]

        HL = 512
        NP, QN = L // HL, HL // 512
        cbuf = A("cbufb", [128, 3 + L], BF16)
        dg = [A("dg%d" % j, [128, 128], BF16) for j in range(4)]
        accb = A("accb", [128, L], BF16)
        sqb = self.sq[:, 0, :]
        rsn = self.rstd
        f128 = lambda nm: A(nm, [128, 128], F32)
        b128 = lambda nm: A(nm, [128, 128], BF16)
        slots = []
        for si in range(2):
            sl = {}
            sl["fm"] = [A("fmq%d" % si, [128, L], BF16), A("fmk%d" % si, [128, L], BF16), A("fmv%d" % si, [128, L], BF16)]
            sl["kvc"] = [A("kvc%d_%d" % (si, j), [128, 256], BF16) for j in range(2)]
            sl["oT"] = A("oT%d" % si, [128, L], BF16)
            sl["S32"] = A("S32_%d" % si, [128, 128], F32)
            sl["Sbf"] = A("Sbf_%d" % si, [128, 128], BF16)
            for par in range(2):
                for nm in ("rhsD", "E", "ET", "Pm", "X0", "X1"):
                    sl[(nm, par)] = f128("%s_%d_%d" % (nm, si, par))
                for nm in ("YR0", "YR1"):
                    sl[(nm, par)] = A("%s_%d_%d" % (nm, si, par), [128, 256], F32)
                for nm in ("Rtb", "bv", "kbg", "kd", "wTn", "qkT"):
                    sl[(nm, par)] = b128("%s_%d_%d" % (nm, si, par))
            sl["o1"] = f128("o1_%d" % si)
            sl["gsall"] = A("gsall%d" % si, [128, 16, 128], BF16)
            for nm in ("vnew", "onb"):
                sl[nm] = b128("%s_%d" % (nm, si))
            sl["ss"] = A("ss_%d" % si, [128, 1], F32)
            slots.append(sl)
        S.op("dve", lambda e: e.memset(cbuf[:, 0:3], 0.0), writes=[("cbuf", "pad")])
        self.evi = 0

        PER = {"vnew", "onb", "o1", "ss", "S32", "Sbf", "wo", "gsall"}

        def head(h, slot):
            b = slot
            sl = slots[slot]
            fm, oT, S32, Sbf, ss = sl["fm"], sl["oT"], sl["S32"], sl["Sbf"], sl["ss"]
            o1, vnew, onb, gsall = sl["o1"], sl["vnew"], sl["onb"], sl["gsall"]
            wo = wos[slot]
            stage2 = {"ops": None}
            PARTOK = {"rhsD", "E", "ET", "Pm", "Rt", "gs", "Rtb", "bv", "kbg", "kd", "wTn", "qkT"}

            def ns(t):
                if isinstance(t, str):
                    return (slot, t) if t in PER else t
                if t[0] in ("X", "Y", "kv", "oT", "par"):
                    return (slot,) + tuple(t)
                if t[0] == "fm":
                    return (slot,) + tuple(t)
                return t

            def sop(eng, fn, reads=(), writes=()):
                r, w = [ns(t) for t in reads], [ns(t) for t in writes]
                if stage2["ops"] is None:
                    S.op(eng, fn, reads=r, writes=w)
                else:
                    stage2["ops"].append((eng, fn, r, w))

            def evac(ps_ap, out_ap, rd, wr):
                self.evi += 1
                if self.evi % 2:
                    sop("act", lambda e: e.activation(out=out_ap, in_=ps_ap, func=AF.Identity), reads=rd, writes=wr)
                else:
                    sop("dve", lambda e: e.tensor_copy(out=out_ap, in_=ps_ap), reads=rd, writes=wr)

            for ch in range(3):
                jj = ch * 8 + h
                cw = lambda tap, jj=jj: vecs[:, CW0 + jj * 4 + tap:CW0 + jj * 4 + tap + 1]
                if ch == 2:
                    dma("pool", wqk[0][:, :, :], w_in[:, :, 2048 + 128 * h:2048 + 128 * (h + 1)], ("wqkv", 0), "wqkv0")
                for tap in range(4):
                    sop("dve", lambda e, tap=tap, cw=cw: e.tensor_scalar(out=dg[tap][:, :], in0=ident[:, :], scalar1=cw(tap), scalar2=None, op0=ALU.mult),
                        reads=["ident", "vecs"], writes=[("dg", tap)])
                for hf in range(NP):
                    ts = slice(hf * 512, (hf + 1) * 512)
                    ps, pt = self.PS.get()
                    sop("pe", [lambda e, kc=kc, ps=ps, ch=ch, ts=ts: e.matmul(
                        ps[:, :], lhsT=wqk[ch % 2][:, kc, :], rhs=hn[:, kc, ts],
                        start=(kc == 0), stop=(kc == KC - 1)) for kc in range(KC)],
                        reads=HN + [("wqkv", ch % 2)], writes=[pt])
                    sop("act", lambda e, ps=ps, hf=hf: e.activation(out=cbuf[:, 3 + hf * 512:3 + (hf + 1) * 512], in_=ps[:, :], func=AF.Identity),
                        reads=[pt], writes=[("cbuf", hf)])
                pss_ = []
                for hf in range(NP):
                    hsl = slice(hf * HL, (hf + 1) * HL)
                    psc, ptc = self.PS.get()
                    rd = [("cbuf", hf), ("cbuf", "pad")] + ([("cbuf", hf - 1)] if hf > 0 else []) + [("dg", t_) for t_ in range(4)]
                    sop("pe", [lambda e, tap=tap, psc=psc, hf=hf: e.matmul(psc[:, :], lhsT=dg[tap][:, :], rhs=cbuf[:, tap + hf * 512:tap + (hf + 1) * 512],
                                                                        start=(tap == 0), stop=(tap == 3)) for tap in range(4)], reads=rd, writes=[ptc])
                    if ch == 2:
                        sop("act", lambda e, hsl=hsl, psc=psc: e.activation(out=fm[2][:, hsl], in_=psc[:, :], func=AF.Silu),
                            reads=[ptc], writes=[("fm", 2, hf)])
                    else:
                        sop("act", lambda e, psc=psc, hsl=hsl: e.activation(out=accb[:, hsl], in_=psc[:, :], func=AF.Silu), reads=[ptc], writes=[("accb", hf)])
                        sop("act", lambda e, hf=hf, hsl=hsl: e.activation(out=self.sq[:, hf, :], in_=accb[:, hsl], func=AF.Square), reads=[("accb", hf)], writes=[("sqp", hf)])
                        ps, pt = self.PS.get()
                        sop("pe", lambda e, ps=ps, hf=hf: e.matmul(ps[:, :], lhsT=ones_bf[:, :], rhs=self.sq[:, hf, :], start=True, stop=True),
                            reads=[("sqp", hf), "ones"], writes=[pt])
                        pss_.append((ps, pt))
                if ch != 2:
                    sc = (128.0 ** -0.5) if ch == 0 else 1.0
                    for hf in range(NP):
                        hsl = slice(hf * HL, (hf + 1) * HL)
                        ps, pt = pss_[hf]
                        sop("act", lambda e, ps=ps: e.activation(out=rsn[:, :], in_=ps[:, :], func=AF.Ln, bias=self.eps_col[:, 0:1]), reads=[pt, "eps"], writes=["rstd"])
                        sop("act", lambda e: e.activation(out=rsn[:, :], in_=rsn[:, :], func=AF.Exp, scale=-0.5), reads=["rstd"], writes=["rstd"])
                        sop("dve", lambda e, ch=ch, hsl=hsl, sc=sc: e.scalar_tensor_tensor(
                            out=fm[ch][:, hsl], in0=accb[:, hsl], scalar=sc, in1=rsn[:, :], op0=ALU.mult, op1=ALU.mult),
                            reads=[("accb", hf), "rstd"], writes=[("fm", ch, hf)])
            FM = lambda ch: [("fm", ch, hf) for hf in range(NP)]
            sop("dve", lambda e: e.memset(S32[:, :], 0.0), writes=["S32"])
            sop("dve", lambda e: e.memset(Sbf[:, :], 0.0), writes=["Sbf"])
            for cq in range(4):
                psg, ptg = self.PS.get()
                for cc in range(4):
                    c_ = 4 * cq + cc
                    sop("pe", [lambda e, kc=kc, psg=psg, cc=cc, c_=c_: e.matmul(psg[:, cc * 128:(cc + 1) * 128], lhsT=hn[:, kc, c_ * 128:(c_ + 1) * 128],
                                                                              rhs=wgate[b][:, kc, :], start=(kc == 0), stop=(kc == KC - 1)) for kc in range(KC)],
                        reads=HN + ["wgate"], writes=[ptg])
                sop("act", lambda e, psg=psg, cq=cq: e.activation(out=gsall[:, 4 * cq:4 * cq + 4, :], in_=psg[:, :].rearrange("p (c v) -> p c v", v=128), func=AF.Silu),
                    reads=[ptg], writes=["gsall"])
            stage2["ops"] = []
            self.PS.lo, self.PS.hi = 4 * slot, 4 * slot + 4

            def chunk(c):
                ct = slice(c * 128, (c + 1) * 128)
                par = c % 2
                col = lambda t: t[:, c, h:h + 1]
                rhsD, E, ET, Pm = (sl[(n, par)] for n in ("rhsD", "E", "ET", "Pm"))
                YR = [sl[("YR0", par)], sl[("YR1", par)]]
                X = [sl[("X0", par)], sl[("X1", par)]]
                Rtb, bv, kbg, kd, wTn, qkT = (sl[(n, par)] for n in ("Rtb", "bv", "kbg", "kd", "wTn", "qkT"))
                T = lambda n: ("par", n, par)
                prep, tail = [], []
                stage2["ops"] = prep
                self.PS.lo, self.PS.hi = 4 * slot, 4 * slot + 2
                sop("pool", lambda e: e.tensor_scalar(out=rhsD[:, :], in0=SL[:, :], scalar1=col(gg), scalar2=1.0, op0=ALU.mult, op1=ALU.mult),
                     reads=["SL", "gg"], writes=[T("rhsD")])
                psd, ptd = self.PS.get()
                sop("pe", [lambda e: e.matmul(psd[:, 0:128], lhsT=U[:, :], rhs=rhsD[:, :], start=True, stop=True),
                            lambda e: e.matmul(psd[:, 128:256], lhsT=rhsD[:, :], rhs=U[:, :], start=True, stop=True)],
                     reads=["U", T("rhsD")], writes=[ptd])
                sop("act", lambda e: e.activation(out=E[:, :], in_=psd[:, 0:128], func=AF.Exp), reads=[ptd], writes=[T("E")])
                sop("act", lambda e: e.activation(out=ET[:, :], in_=psd[:, 128:256], func=AF.Exp), reads=[ptd], writes=[T("ET")])
                psk, ptk = self.PS.get()
                sop("pe", [lambda e: e.matmul(psk[:, 0:128], lhsT=fm[1][:, ct], rhs=fm[1][:, ct], start=True, stop=True),
                            lambda e: e.matmul(psk[:, 128:256], lhsT=fm[1][:, ct], rhs=fm[0][:, ct], start=True, stop=True)],
                     reads=FM(0) + FM(1), writes=[ptk])
                sop("dve", lambda e: e.tensor_tensor(out=E[:, :], in0=psk[:, 0:128], in1=E[:, :], op=ALU.mult), reads=[ptk, T("E")], writes=[T("E")])
                sop("dve", lambda e: e.scalar_tensor_tensor(out=Pm[:, :], in0=E[:, :], scalar=col(nbeta), in1=SL[:, :], op0=ALU.mult, op1=ALU.mult),
                     reads=[T("E"), "nbeta", "SL"], writes=[T("Pm")])
                sop("dve", lambda e: e.tensor_tensor(out=ET[:, :], in0=psk[:, 128:256], in1=ET[:, :], op=ALU.mult), reads=[ptk, T("ET")], writes=[T("ET")])
                sop("pool", lambda e: e.tensor_tensor(out=qkT[:, :], in0=ET[:, :], in1=U[:, :], op=ALU.mult), reads=[T("ET"), "U"], writes=[T("qkT")])
                pst, ptt = self.PS.get()
                sop("pe", lambda e: e.transpose(out=pst[:, 0:128], in_=Pm[:, :], identity=identf[:, :]), reads=[T("Pm"), "identf"], writes=[ptt])
                evac(pst[:, 0:128], YR[0][:, 0:128], [ptt], [T("Y0")])
                sop("pool", lambda e: e.tensor_copy(out=YR[0][:, 128:256], in_=identf[:, :]), reads=["identf"], writes=[T("R0")])
                Xc, xt = Pm, T("Pm")
                for k in range(6):
                    a_, nb = k % 2, (k + 1) % 2
                    YRa, YRn = YR[a_], YR[nb]
                    Xn = X[nb]
                    psx, ptx = self.PS.get()
                    sop("pe", lambda e, YRa=YRa, Xc=Xc, psx=psx: e.matmul(psx[:, 0:128], lhsT=YRa[:, 0:128], rhs=Xc[:, :], start=True, stop=True),
                         reads=[T("Y%d" % a_), xt], writes=[ptx])
                    evac(psx[:, 0:128], Xn[:, :], [ptx], [T("X%d" % nb)])
                    psb, ptb = self.PS.get()
                    if k < 5:
                        sop("pe", lambda e, YRa=YRa, Xc=Xc, psb=psb: e.matmul(psb[:, 0:256], lhsT=Xc[:, :], rhs=YRa[:, 0:256], start=True, stop=True),
                             reads=[T("Y%d" % a_), T("R%d" % a_), xt], writes=[ptb])
                        evac(psb[:, 0:128], YRn[:, 0:128], [ptb], [T("Y%d" % nb)])
                        sop("dve", lambda e, YRa=YRa, YRn=YRn, psb=psb: e.tensor_tensor(out=YRn[:, 128:256], in0=psb[:, 128:256], in1=YRa[:, 128:256], op=ALU.add),
                             reads=[ptb, T("R%d" % a_)], writes=[T("R%d" % nb)])
                    else:
                        sop("pe", lambda e, YRa=YRa, Xc=Xc, psb=psb: e.matmul(psb[:, 0:128], lhsT=Xc[:, :], rhs=YRa[:, 128:256], start=True, stop=True),
                             reads=[T("R%d" % a_), xt], writes=[ptb])
                        sop("dve", lambda e, YRa=YRa, YRn=YRn, psb=psb: e.tensor_tensor(out=YRn[:, 128:256], in0=psb[:, 0:128], in1=YRa[:, 128:256], op=ALU.add),
                             reads=[ptb, T("R%d" % a_)], writes=[T("R%d" % nb)])
                    Xc, xt = Xn, T("X%d" % nb)
                psf, ptf = self.PS.get()
                sop("pe", lambda e, psf=psf: e.matmul(psf[:, 0:128], lhsT=X[0][:, :], rhs=YR[0][:, 128:256], start=True, stop=True),
                     reads=[T("X0"), T("R0")], writes=[ptf])
                sop("dve", lambda e, psf=psf: e.tensor_tensor(out=Rtb[:, :], in0=psf[:, 0:128], in1=YR[0][:, 128:256], op=ALU.add),
                     reads=[ptf, T("R0")], writes=[T("Rtb")])
                kvc = sl["kvc"][par]
                pskv, ptkv = self.PS.get()
                sop("pe", [lambda e, q=q: e.matmul(pskv[:, q * 128:(q + 1) * 128], lhsT=fm[1 + q][:, ct], rhs=ident[:, :],
                                                    start=True, stop=True) for q in range(2)],
                     reads=FM(1) + FM(2) + ["ident"], writes=[ptkv])
                evac(pskv[:, 0:256], kvc[:, :], [ptkv], [T("kvc")])
                sop("pool", lambda e: e.tensor_scalar(out=bv[:, :], in0=kvc[:, 128:256], scalar1=col(beta), scalar2=1.0, op0=ALU.mult, op1=ALU.mult),
                     reads=[T("kvc"), "beta"], writes=[T("bv")])
                sop("pool", lambda e: e.tensor_scalar(out=kbg[:, :], in0=kvc[:, 0:128], scalar1=col(nbeg), scalar2=1.0, op0=ALU.mult, op1=ALU.mult),
                     reads=[T("kvc"), "nbeg"], writes=[T("kbg")])
                sop("pool", lambda e: e.tensor_scalar(out=kd[:, :], in0=kvc[:, 0:128], scalar1=col(ed), scalar2=1.0, op0=ALU.mult, op1=ALU.mult),
                     reads=[T("kvc"), "ed"], writes=[T("kd")])
                psw, ptw = self.PS.get()
                sop("pe", lambda e: e.matmul(psw[:, 0:128], lhsT=kbg[:, :], rhs=Rtb[:, :], start=True, stop=True), reads=[T("kbg"), T("Rtb")], writes=[ptw])
                evac(psw[:, 0:128], wTn[:, :], [ptw], [T("wTn")])
                stage2["ops"] = tail
                self.PS.lo, self.PS.hi = 4 * slot + 2, 4 * slot + 4
                psv_, ptv = self.PS.get()
                sop("pe", [lambda e: e.matmul(psv_[:, 0:128], lhsT=Rtb[:, :], rhs=bv[:, :], start=True, stop=False),
                            lambda e: e.matmul(psv_[:, 0:128], lhsT=wTn[:, :], rhs=Sbf[:, :], start=False, stop=True)],
                     reads=[T("Rtb"), T("bv"), T("wTn"), "Sbf"], writes=[ptv])
                evac(psv_[:, 0:128], vnew[:, :], [ptv], ["vnew"])
                pso, pto = self.PS.get()
                sop("pe", [lambda e: e.matmul(pso[:, 0:128], lhsT=fm[0][:, ct], rhs=Sbf[:, :], start=True, stop=True),
                            lambda e: e.matmul(pso[:, 128:256], lhsT=qkT[:, :], rhs=vnew[:, :], start=True, stop=True)],
                     reads=FM(0) + ["Sbf", T("qkT"), "vnew"], writes=[pto])
                pss, pts = self.PS.get()
                sop("pe", lambda e: e.matmul(pss[:, 0:128], lhsT=kd[:, :], rhs=vnew[:, :], start=True, stop=True), reads=[T("kd"), "vnew"], writes=[pts])
                sop("dve", lambda e: e.scalar_tensor_tensor(out=S32[:, :], in0=S32[:, :], scalar=col(cdrep), in1=pss[:, 0:128], op0=ALU.mult, op1=ALU.add),
                     reads=[pts, "S32", "cdrep"], writes=["S32"])
                sop("act", lambda e: e.activation(out=Sbf[:, :], in_=S32[:, :], func=AF.Identity), reads=["S32"], writes=["Sbf"])
                sop("act", lambda e: e.activation(out=o1[:, :], in_=pso[:, 0:128], func=AF.Identity, scale=col(eG)), reads=[pto, "eG"], writes=["o1"])
                sop("dve", lambda e: e.tensor_tensor(out=o1[:, :], in0=pso[:, 128:256], in1=o1[:, :], op=ALU.add), reads=[pto, "o1"], writes=["o1"])
                sop("act", lambda e: e.activation(out=onb[:, :], in_=o1[:, :], func=AF.Square, accum_out=ss[:, 0:1]),
                     reads=["o1"], writes=["onb", "ss"])
                sop("act", lambda e: e.activation(out=ss[:, :], in_=ss[:, :], func=AF.Ln, scale=1.0 / 128, bias=self.eps_col[:, 0:1]),
                     reads=["ss", "eps"], writes=["ss"])
                sop("act", lambda e: e.activation(out=ss[:, :], in_=ss[:, :], func=AF.Exp, scale=-0.5), reads=["ss"], writes=["ss"])
                sop("dve", lambda e: e.scalar_tensor_tensor(out=o1[:, :], in0=o1[:, :], scalar=ss[:, 0:1], in1=ong, op0=ALU.mult, op1=ALU.mult),
                     reads=["o1", "ss", "rows"], writes=["o1"])
                sop("dve", lambda e: e.tensor_tensor(out=onb[:, :], in0=o1[:, :], in1=gsall[:, c, :], op=ALU.mult), reads=["o1", "gsall"], writes=["onb"])
                psT, ptT = self.PS.get()
                sop("pe", lambda e: e.matmul(psT[:, 0:128], lhsT=onb[:, :], rhs=ident[:, :], start=True, stop=True), reads=["onb", "ident"], writes=[ptT])
                evac(psT[:, 0:128], oT[:, ct], [ptT], [("oT", c)])
                return prep, tail

            def zipl(x, y):
                out = []
                for i in range(max(len(x), len(y))):
                    if i < len(x):
                        out.append(x[i])
                    if i < len(y):
                        out.append(y[i])
                return out

            pts_ = [chunk(c) for c in range(16)]
            stream = list(pts_[0][0])
            for c in range(16):
                stream += zipl(pts_[c][1], pts_[c + 1][0] if c + 1 < 16 else [])
            stage2["ops"] = stream
            self.PS.lo, self.PS.hi = 4 * slot, 4 * slot + 4
            OT = [("oT", c) for c in range(16)]
            for nt in range(4):
                ts = slice(nt * 512, (nt + 1) * 512)
                for dc in range(KC):
                    ps, pt = self.PS.get()
                    sop("pe", lambda e, ps=ps, dc=dc, ts=ts: e.matmul(ps[:, :], lhsT=wo[:, dc * 128:(dc + 1) * 128], rhs=oT[:, ts], start=True, stop=True),
                         reads=OT + ["wo"], writes=[pt])
                    sop("dve", lambda e, ps=ps, dc=dc, ts=ts: e.tensor_tensor(out=hT[:, dc, ts], in0=ps[:, :], in1=hT[:, dc, ts], op=ALU.add),
                         reads=[pt, ("hT", nt, dc)], writes=[("hT", nt, dc)])
            self.PS.lo, self.PS.hi = 0, 8
            return stage2["ops"]

        for pair in range(4):
            lists = []
            for slot in range(2):
                h = 2 * pair + slot
                load_head(h, slot)
                lists.append(head(h, slot))
            for i in range(max(len(x) for x in lists)):
                for ops in lists:
                    if i < len(ops):
                        eng, fn, r, w = ops[i]
                        S.op(eng, fn, reads=r, writes=w)
        self.phase_end()

    def s5_phase(self, li, d):
        self.phase_begin()
        import os
        stop3 = float(os.environ.get("KDBG3", "99"))
        S, nc, hT, hn, vecs = self.S, self.nc, self.hT, self.hn, self.vecs
        A = self.A
        ident, identf = self.ident_bf, self.ident_f
        PI = float(np.pi)
        w_in = d["s5_w_in"][0].rearrange("(kc p) f -> p kc f", p=128)
        w_out = d["s5_w_out"][0].rearrange("(kc p) f -> p kc f", p=128)
        gcol = lambda kc: vecs[:, li * 8 + kc:li * 8 + kc + 1]
        self.rmsnorm_T(gcol, hn, "hn")
        HN = [("hn", nt, kc) for nt in range(4) for kc in range(KC)]

        def dma(eng, out, in_, wtok, sem, reads=()):
            S.op(eng, lambda e: e.dma_start(out=out, in_=in_), reads=list(reads), writes=[wtok], dsem=sem)

        cur = {"ops": None}

        def op(eng, fn, rd, wr):
            if cur["ops"] is None:
                S.op(eng, fn, reads=rd, writes=wr)
            else:
                cur["ops"].append((eng, fn, list(rd), list(wr)))

        HL = 512
        prm = A("prm", [128, 3 * 64 + 1 + 8], F32)
        dma("sp", prm[:, :], d["s5_prm"], "prm", "prm")
        LR, LIM, LDT, sgn = prm[:, 0:64], prm[:, 64:128], prm[:, 128:192], prm[:, 192:193]
        dcol = lambda mc: prm[:, 193 + mc:194 + mc]
        dtv, ar, th, rdec = (A(n, [128, 64], F32) for n in ("dtv", "ar", "th", "rdec"))
        q0, q1, q2, q3, q4, q5, q6, q7 = (A("q%d" % i, [128, 64], F32) for i in range(8))
        qi = A("qi", [128, 64], mybir.dt.int32)
        cfr, cfi = A("cfr", [128, 64], F32), A("cfi", [128, 64], F32)

        def exp_acc(out, x, xt, ot, k, n):
            sc = 1.0 / (2 ** k)
            op("dve", lambda e: e.tensor_scalar(out=out, in0=x, scalar1=sc / n, scalar2=1.0, op0=ALU.mult, op1=ALU.add), [xt], [ot])
            for m in range(n - 1, 0, -1):
                op("dve", lambda e: e.tensor_tensor(out=out, in0=out, in1=x, op=ALU.mult), [xt, ot], [ot])
                op("dve", lambda e, m=m: e.tensor_scalar(out=out, in0=out, scalar1=sc / m, scalar2=1.0, op0=ALU.mult, op1=ALU.add), [ot], [ot])
            for _ in range(k):
                op("dve", lambda e: e.tensor_tensor(out=out, in0=out, in1=out, op=ALU.mult), [ot], [ot])

        exp_acc(dtv[:, :], LDT, "prm", "dtv", 4, 12)
        op("dve", lambda e: e.tensor_tensor(out=ar[:, :], in0=LR, in1=dtv[:, :], op=ALU.mult), ["prm", "dtv"], ["ar"])
        op("dve", lambda e: e.tensor_tensor(out=th[:, :], in0=LIM, in1=dtv[:, :], op=ALU.mult), ["prm", "dtv"], ["th"])
        exp_acc(rdec[:, :], ar[:, :], "ar", "rdec", 0, 8)
        thoff = A("thoff", [128, 4, 64], F32)
        for hf_ in range(4):
            op("dve", lambda e, hf_=hf_: e.tensor_scalar(out=thoff[:, hf_, :], in0=th[:, :], scalar1=float(hf_ * 512), scalar2=None, op0=ALU.mult), ["th"], ["thoff"])
        NS = 20
        op("dve", lambda e: e.tensor_scalar(out=q0[:, :], in0=ar[:, :], scalar1=1.0 / (NS + 1), scalar2=1.0, op0=ALU.mult, op1=ALU.add), ["ar"], ["q0"])
        op("dve", lambda e: e.tensor_scalar(out=q1[:, :], in0=th[:, :], scalar1=1.0 / (NS + 1), scalar2=None, op0=ALU.mult), ["th"], ["q1"])
        for m in range(NS, 1, -1):
            op("dve", lambda e: e.tensor_tensor(out=q2[:, :], in0=ar[:, :], in1=q0[:, :], op=ALU.mult), ["ar", "q0"], ["q2"])
            op("dve", lambda e: e.tensor_tensor(out=q3[:, :], in0=th[:, :], in1=q1[:, :], op=ALU.mult), ["th", "q1"], ["q3"])
            op("dve", lambda e: e.tensor_tensor(out=q4[:, :], in0=ar[:, :], in1=q1[:, :], op=ALU.mult), ["ar", "q1"], ["q4"])
            op("dve", lambda e: e.tensor_tensor(out=q5[:, :], in0=th[:, :], in1=q0[:, :], op=ALU.mult), ["th", "q0"], ["q5"])
            op("dve", lambda e: e.tensor_tensor(out=q2[:, :], in0=q2[:, :], in1=q3[:, :], op=ALU.subtract), ["q2", "q3"], ["q2"])
            op("dve", lambda e: e.tensor_tensor(out=q4[:, :], in0=q4[:, :], in1=q5[:, :], op=ALU.add), ["q4", "q5"], ["q4"])
            op("dve", lambda e, m=m: e.tensor_scalar(out=q0[:, :], in0=q2[:, :], scalar1=1.0 / m, scalar2=1.0, op0=ALU.mult, op1=ALU.add), ["q2"], ["q0"])
            op("dve", lambda e, m=m: e.tensor_scalar(out=q1[:, :], in0=q4[:, :], scalar1=1.0 / m, scalar2=None, op0=ALU.mult), ["q4"], ["q1"])
        op("dve", lambda e: e.tensor_copy(out=q2[:, :], in_=th[:, :]), ["th"], ["q2"])
        op("dve", lambda e: e.tensor_scalar(out=q3[:, :], in0=th[:, :], scalar1=PI / 2, scalar2=None, op0=ALU.add), ["th"], ["q3"])

        def rr_small(x, xt):
            op("dve", lambda e: e.tensor_scalar(out=qi[:, :], in0=x, scalar1=1.0 / (2 * PI), scalar2=None, op0=ALU.mult), [xt], ["qi"])
            op("dve", lambda e: e.tensor_copy(out=q4[:, :], in_=qi[:, :]), ["qi"], ["q4"])
            op("dve", lambda e: e.scalar_tensor_tensor(out=x, in0=q4[:, :], scalar=-2 * PI, in1=x, op0=ALU.mult, op1=ALU.add), ["q4", xt], [xt])
            op("dve", lambda e: e.tensor_scalar(out=q4[:, :], in0=x, scalar1=PI, scalar2=-2 * PI, op0=ALU.is_gt, op1=ALU.mult), [xt], ["q4"])
            op("dve", lambda e: e.tensor_tensor(out=x, in0=x, in1=q4[:, :], op=ALU.add), [xt, "q4"], [xt])
            op("dve", lambda e: e.tensor_scalar(out=q4[:, :], in0=x, scalar1=-PI, scalar2=2 * PI, op0=ALU.is_lt, op1=ALU.mult), [xt], ["q4"])
            op("dve", lambda e: e.tensor_tensor(out=x, in0=x, in1=q4[:, :], op=ALU.add), [xt, "q4"], [xt])
        rr_small(q2[:, :], "q2")
        rr_small(q3[:, :], "q3")
        op("act", lambda e: e.activation(out=q2[:, :], in_=q2[:, :], func=AF.Sin, scale=0.999995), ["q2"], ["q2"])
        op("act", lambda e: e.activation(out=q3[:, :], in_=q3[:, :], func=AF.Sin, scale=0.999995), ["q3"], ["q3"])
        op("dve", lambda e: e.tensor_tensor(out=q2[:, :], in0=q2[:, :], in1=rdec[:, :], op=ALU.mult), ["q2", "rdec"], ["q2"])
        op("dve", lambda e: e.tensor_tensor(out=q3[:, :], in0=q3[:, :], in1=rdec[:, :], op=ALU.mult), ["q3", "rdec"], ["q3"])
        op("dve", lambda e: e.tensor_scalar(out=q3[:, :], in0=q3[:, :], scalar1=-1.0, scalar2=None, op0=ALU.add), ["q3"], ["q3"])
        op("dve", lambda e: e.tensor_tensor(out=q4[:, :], in0=ar[:, :], in1=ar[:, :], op=ALU.mult), ["ar"], ["q4"])
        op("dve", lambda e: e.tensor_tensor(out=q5[:, :], in0=th[:, :], in1=th[:, :], op=ALU.mult), ["th"], ["q5"])
        op("dve", lambda e: e.tensor_tensor(out=q4[:, :], in0=q4[:, :], in1=q5[:, :], op=ALU.add), ["q4", "q5"], ["q4"])
        op("dve", lambda e: e.reciprocal(out=q5[:, :], in_=q4[:, :]), ["q4"], ["q5"])
        op("dve", lambda e: e.tensor_scalar(out=q4[:, :], in0=q4[:, :], scalar1=4.0, scalar2=None, op0=ALU.is_lt), ["q4"], ["q4"])
        op("dve", lambda e: e.tensor_tensor(out=q6[:, :], in0=q3[:, :], in1=ar[:, :], op=ALU.mult), ["q3", "ar"], ["q6"])
        op("dve", lambda e: e.tensor_tensor(out=q7[:, :], in0=q2[:, :], in1=th[:, :], op=ALU.mult), ["q2", "th"], ["q7"])
        op("dve", lambda e: e.tensor_tensor(out=q6[:, :], in0=q6[:, :], in1=q7[:, :], op=ALU.add), ["q6", "q7"], ["q6"])
        op("dve", lambda e: e.tensor_tensor(out=q6[:, :], in0=q6[:, :], in1=q5[:, :], op=ALU.mult), ["q6", "q5"], ["q6"])
        op("dve", lambda e: e.tensor_tensor(out=q7[:, :], in0=q2[:, :], in1=ar[:, :], op=ALU.mult), ["q2", "ar", "q6"], ["q7"])
        op("dve", lambda e: e.tensor_tensor(out=q2[:, :], in0=q3[:, :], in1=th[:, :], op=ALU.mult), ["q3", "th", "q7"], ["q2"])
        op("dve", lambda e: e.tensor_tensor(out=q7[:, :], in0=q7[:, :], in1=q2[:, :], op=ALU.subtract), ["q7", "q2"], ["q7"])
        op("dve", lambda e: e.tensor_tensor(out=q7[:, :], in0=q7[:, :], in1=q5[:, :], op=ALU.mult), ["q7", "q5"], ["q7"])
        op("dve", lambda e: e.tensor_tensor(out=q0[:, :], in0=q0[:, :], in1=q6[:, :], op=ALU.subtract), ["q0", "q6"], ["q0"])
        op("dve", lambda e: e.tensor_tensor(out=q0[:, :], in0=q0[:, :], in1=q4[:, :], op=ALU.mult), ["q0", "q4"], ["q0"])
        op("dve", lambda e: e.tensor_tensor(out=q0[:, :], in0=q0[:, :], in1=q6[:, :], op=ALU.add), ["q0", "q6"], ["q0"])
        op("dve", lambda e: e.tensor_tensor(out=cfr[:, :], in0=q0[:, :], in1=dtv[:, :], op=ALU.mult), ["q0", "dtv"], ["cfr"])
        op("dve", lambda e: e.tensor_tensor(out=q1[:, :], in0=q1[:, :], in1=q7[:, :], op=ALU.subtract), ["q1", "q7"], ["q1"])
        op("dve", lambda e: e.tensor_tensor(out=q1[:, :], in0=q1[:, :], in1=q4[:, :], op=ALU.mult), ["q1", "q4"], ["q1"])
        op("dve", lambda e: e.tensor_tensor(out=q1[:, :], in0=q1[:, :], in1=q7[:, :], op=ALU.add), ["q1", "q7"], ["q1"])
        op("dve", lambda e: e.tensor_tensor(out=cfi[:, :], in0=q1[:, :], in1=dtv[:, :], op=ALU.mult), ["q1", "dtv"], ["cfi"])
        rx = A("rx", [64, 2, 8, 64], F32)
        self.dbg_names = {}
        ramp = A("ramp", [128, HL], F32)
        op("pool", lambda e: e.iota(ramp[:, :], pattern=[[1, HL]], base=0, channel_multiplier=0, allow_small_or_imprecise_dtypes=True), [], ["ramp"])

        ang, kf, cosT, sinT, wv, Wsc = (A(n, [128, HL], F32) for n in ("ang", "kf", "cosT", "sinT", "wv", "Wsc"))
        cosT2, sinT2, Wsc2 = (A(n, [128, HL], F32) for n in ("cosT2", "sinT2", "Wsc2"))
        ki = A("ki", [128, HL], mybir.dt.int32)
        Ycs = A("Ycs", [128, HL], BF16)
        Ysn = A("Ysn", [128, HL], BF16)
        ub = A("ub", [128, L], BF16)
        yT = A("yT", [128, KC, L], BF16)
        ddiag = A("ddiag", [128, 128], BF16)
        c0, c1, c2, c3, c4, c5 = ang, kf, cosT, sinT, wv, Wsc
        ci = ki
        BT = A("BT", [128, 2, 8, 64], F32)
        Bp = A("Bp", [128, 8, 128], BF16)
        Bs = A("Bs", [128, 8, 128], BF16)
        Cp = A("Cp", [128, 8, 128], BF16)
        Cq = A("Cq", [128, 8, 128], BF16)
        Cf = A("Cf", [128, 2, 8, 128], F32)
        wu = A("wu", [128, KC, 128], BF16)
        wo = [A("wo%d" % i, [128, KC, 256], BF16) for i in range(1)] * 2

        def range_reduce(x, n, tmp_i, tmp_f, rd, tf):
            op("dve", lambda e: e.tensor_scalar(out=tmp_i, in0=x, scalar1=1.0 / (2 * PI), scalar2=None, op0=ALU.mult), rd, ["ki"])
            op("dve", lambda e: e.tensor_copy(out=tmp_f, in_=tmp_i), ["ki"], [tf])
            op("dve", lambda e: e.scalar_tensor_tensor(out=x, in0=tmp_f, scalar=-2 * PI, in1=x, op0=ALU.mult, op1=ALU.add), [tf] + rd, rd)
            op("dve", lambda e: e.tensor_scalar(out=tmp_f, in0=x, scalar1=PI, scalar2=-2 * PI, op0=ALU.is_gt, op1=ALU.mult), rd, [tf])
            op("dve", lambda e: e.tensor_tensor(out=x, in0=x, in1=tmp_f, op=ALU.add), rd + [tf], rd)
            op("dve", lambda e: e.tensor_scalar(out=tmp_f, in0=x, scalar1=-PI, scalar2=2 * PI, op0=ALU.is_lt, op1=ALU.mult), rd, [tf])
            op("dve", lambda e: e.tensor_tensor(out=x, in0=x, in1=tmp_f, op=ALU.add), rd + [tf], rd)

        self.PS.lo, self.PS.hi = 0, 4
        self.PS.i = 0
        yacc = [(self.PS.tiles[4 + q], ("ps", 4 + q)) for q in range(4)]

        def mc_block(mc):
            passes = []
            dma("pool", wu[:, :, :], w_in[:, :, mc * 128:(mc + 1) * 128], "wu", "wu")
            for nt in range(4):
                ts = slice(nt * 512, (nt + 1) * 512)
                ps, pt = self.PS.get()
                S.op("pe", [lambda e, kc=kc, ps=ps, ts=ts: e.matmul(ps[:, :], lhsT=wu[:, kc, :], rhs=hn[:, kc, ts],
                                                                   start=(kc == 0), stop=(kc == KC - 1)) for kc in range(KC)],
                     reads=HN + ["wu"], writes=[pt])
                op("act", lambda e, ps=ps, ts=ts: e.activation(out=ub[:, ts], in_=ps[:, :], func=AF.Identity), [pt], [("ub", nt)])
            UB = [("ub", nt) for nt in range(4)]
            dma("sp", BT[:, :, :, :], d["s5_bT"][:, :, mc * 8:(mc + 1) * 8, :], "BT", "BT")
            dma("sp", Cf[:, :, :, :], d["s5_cpad"][:, :, mc * 8:(mc + 1) * 8, :], "Cf", "Cf")
            for ri, cf, ctile, ctok in ((0, cfr, c3, "sinT"), (1, cfi, c4, "wv")):
                op("dve", lambda e, ri=ri, cf=cf: e.tensor_tensor(
                    out=rx[:, ri, :, :], in0=identf[0:64, 0:64].unsqueeze(1).to_broadcast([64, 8, 64]),
                    in1=cf[0:64, mc * 8:(mc + 1) * 8].unsqueeze(2).to_broadcast([64, 8, 64]), op=ALU.mult),
                    ["identf", "cfr", "cfi"], [("rx", ri)])
                psx, ptx = self.PS.get()
                S.op("pe", lambda e, psx=psx, ri=ri: e.matmul(psx[:, :], lhsT=self.ones_f[0:64, :], rhs=rx[:, ri, :, :].rearrange("p g q -> p (g q)"),
                                                             start=True, stop=True), reads=[("rx", ri), "onesf"], writes=[ptx])
                op("act", lambda e, psx=psx, ctile=ctile: e.activation(out=ctile[:, :], in_=psx[:, :], func=AF.Identity), [ptx], [ctok])
            cr3 = c3[:, :].rearrange("p (g q) -> p g q", q=64)
            ci3 = c4[:, :].rearrange("p (g q) -> p g q", q=64)
            t5 = c5[:, :].rearrange("p (g q) -> p g q", q=64)
            t0 = c0[:, :].rearrange("p (g q) -> p g q", q=64)
            bre, bim = BT[:, 0, :, :], BT[:, 1, :, :]
            op("dve", lambda e: e.tensor_tensor(out=t5, in0=cr3, in1=bre, op=ALU.mult), ["sinT", "BT", "Wsc"], ["Wsc"])
            op("dve", lambda e: e.tensor_tensor(out=t0, in0=ci3, in1=bim, op=ALU.mult), ["wv", "BT", "ang"], ["ang"])
            op("dve", lambda e: e.tensor_tensor(out=Bp[:, :, 0:64], in0=t5, in1=t0, op=ALU.subtract), ["Wsc", "ang"], [("Bp", 0)])
            op("dve", lambda e: e.tensor_scalar(out=Bs[:, :, 64:128], in0=Bp[:, :, 0:64], scalar1=-1.0, scalar2=None, op0=ALU.mult), [("Bp", 0)], [("Bs", 1)])
            op("dve", lambda e: e.tensor_tensor(out=t5, in0=cr3, in1=bim, op=ALU.mult), ["sinT", "BT", "Wsc", ("Bp", 0)], ["Wsc"])
            op("dve", lambda e: e.tensor_tensor(out=t0, in0=ci3, in1=bre, op=ALU.mult), ["wv", "BT", "ang", ("Bp", 0)], ["ang"])
            op("dve", lambda e: e.tensor_tensor(out=Bp[:, :, 64:128], in0=t5, in1=t0, op=ALU.add), ["Wsc", "ang"], [("Bp", 1)])
            op("dve", lambda e: e.tensor_copy(out=Bs[:, :, 0:64], in_=Bp[:, :, 64:128]), [("Bp", 1)], [("Bs", 0)])
            BP = [("Bp", 0), ("Bp", 1), ("Bs", 0), ("Bs", 1)]
            op("dve", lambda e: e.tensor_scalar(out=Cp[:, :, :], in0=Cf[:, 0, :, :], scalar1=sgn, scalar2=None, op0=ALU.mult), ["Cf", "prm"], ["Cp"])
            op("dve", lambda e: e.tensor_scalar(out=Cq[:, :, :], in0=Cf[:, 1, :, :], scalar1=-1.0, scalar2=None, op0=ALU.mult), ["Cf"], ["Cq"])
            op("dve", lambda e: e.tensor_scalar(out=ddiag[:, :], in0=identf[:, :], scalar1=dcol(mc), scalar2=None, op0=ALU.mult), ["identf", "prm"], ["ddiag"])
            for q in range(4):
                ya, yt = yacc[q]
                S.op("pe", lambda e, ya=ya, q=q: e.matmul(ya[:, :], lhsT=ddiag[:, :], rhs=ub[:, q * 512:(q + 1) * 512], start=True, stop=False),
                     reads=["ddiag"] + UB, writes=[yt])

            if stop3 <= 1:
                return

            def group(gl):
                g = 8 * mc + gl
                def one_pass(hf):
                    par = hf % 2
                    cosT_, sinT_, Wsc_, Ycs_, Ysn_ = (cosT2, sinT2, Wsc2, Ycs, Ysn) if par else (cosT, sinT, Wsc, Ycs, Ysn)
                    WscP, twP = (Wsc, 'Wsc') if par else (Wsc2, 'Wsc2')
                    tc, tsn, tw, tyc, tys = ('cosT2', 'sinT2', 'Wsc2', 'Ycs', 'Ysn') if par else ('cosT', 'sinT', 'Wsc', 'Ycs', 'Ysn')
                    Tops, Dops = [], []
                    cur["ops"] = Tops
                    nt = hf
                    ts = slice(nt * 512, (nt + 1) * 512)
                    qs = slice(0, 512)
                    psz, ptz = self.PS.get()
                    pss, pts = self.PS.get()
                    op("pe", lambda e: e.matmul(psz[:, :], lhsT=Bp[:, gl, :], rhs=ub[:, ts], start=True, stop=True), BP + UB, [ptz])
                    op("pe", lambda e: e.matmul(pss[:, :], lhsT=Bs[:, gl, :], rhs=ub[:, ts], start=True, stop=True), BP + UB, [pts])
                    op("act", lambda e, hf=hf: e.activation(out=ang[:, :], in_=ramp[:, :], func=AF.Identity, scale=th[:, g:g + 1],
                                                            bias=thoff[:, hf, g:g + 1]), ["ramp", "th", "thoff"], ["ang"])
                    op("dve", lambda e: e.tensor_scalar(out=ki[:, :], in0=ang[:, :], scalar1=1.0 / (2 * PI), scalar2=None, op0=ALU.mult), ["ang"], ["ki"])
                    op("dve", lambda e: e.scalar_tensor_tensor(out=ang[:, :], in0=ki[:, :], scalar=-2 * PI, in1=ang[:, :], op0=ALU.mult, op1=ALU.add),
                       ["ki", "ang"], ["ang"])
                    op("pool", lambda e: e.tensor_scalar(out=ang[:, :], in0=ang[:, :], scalar1=PI, scalar2=-PI, op0=ALU.min, op1=ALU.max), ["ang"], ["ang"])
                    op("act", lambda e: e.activation(out=sinT_[:, :], in_=ang[:, :], func=AF.Sin, scale=0.999995), ["ang"], [tsn])
                    op("act", lambda e: e.activation(out=cosT_[:, :], in_=ang[:, :], func=AF.Sin, scale=0.5), ["ang"], [tc])
                    op("act", lambda e: e.activation(out=cosT_[:, :], in_=cosT_[:, :], func=AF.Square), [tc], [tc])
                    op("act", lambda e: e.activation(out=cosT_[:, :], in_=cosT_[:, :], func=AF.Identity, scale=-2.0, bias=1.0), [tc], [tc])
                    cur["ops"] = Dops
                    for q in range(1):
                        op("dve", lambda e, pss=pss, qs=qs: e.tensor_tensor(out=kf[:, qs], in0=pss[:, :], in1=sinT_[:, qs], op=ALU.mult), [pts, tsn], ["kf"])
                        op("dve", lambda e, psz=psz, qs=qs: e.tensor_tensor(out=wv[:, qs], in0=psz[:, :], in1=cosT_[:, qs], op=ALU.mult), [ptz, tc], ["wv"])
                        op("dve", lambda e, qs=qs: e.tensor_tensor(out=wv[:, qs], in0=wv[:, qs], in1=kf[:, qs], op=ALU.add), ["kf", "wv"], ["wv"])
                    WV = ["wv"]
                    if hf > 0:
                        op("dve", lambda e: e.scalar_tensor_tensor(out=wv[:, 0:1], in0=WscP[:, HL - 1:HL], scalar=rdec[:, g:g + 1], in1=wv[:, 0:1],
                                                                   op0=ALU.mult, op1=ALU.add), WV + ["rdec", twP], WV)
                    op("dve", lambda e: e.tensor_tensor_scan(out=Wsc_[:, :], data0=rdec[:, g:g + 1].to_broadcast([128, HL]), data1=wv[:, :],
                                                             initial=0.0, op0=ALU.mult, op1=ALU.add), WV + ["rdec"], [tw])
                    op("pool", lambda e: e.tensor_tensor(out=Ycs_[:, :], in0=Wsc_[:, :], in1=cosT_[:, :], op=ALU.mult), [tw, tc], [tyc])
                    op("pool", lambda e: e.tensor_tensor(out=Ysn_[:, :], in0=Wsc_[:, :], in1=sinT_[:, :], op=ALU.mult), [tw, tsn], [tys])
                    for q in range(1):
                        nt = hf
                        qs = slice(0, 512)
                        ya, yt = yacc[nt]
                        last = (gl == 7)
                        op("pe", [lambda e, ya=ya, qs=qs: e.matmul(ya[:, :], lhsT=Cp[:, gl, :], rhs=Ycs_[:, qs], start=False, stop=False),
                                  lambda e, ya=ya, qs=qs: e.matmul(ya[:, :], lhsT=Cq[:, gl, :], rhs=Ysn_[:, qs], start=False, stop=last)],
                           ["Cp", "Cq", tyc, tys, yt], [yt])
                    cur["ops"] = None
                    return Tops, Dops

                for hf in range(4):
                    passes.append(one_pass(hf))

            for gl in range(8):
                group(gl)
            def emit(ops):
                for eng, fn, r, w in ops:
                    S.op(eng, fn, reads=r, writes=w)
            emit(passes[0][0])
            for n in range(len(passes)):
                if n + 1 < len(passes):
                    emit(passes[n + 1][0])
                emit(passes[n][1])
            for nt in range(4):
                ts = slice(nt * 512, (nt + 1) * 512)
                ya, yt = yacc[nt]
                x = cosT[:, 0:512]
                x2 = sinT[:, 0:512]
                op("act", lambda e, ya=ya: e.activation(out=x, in_=ya[:, :], func=AF.Identity), [yt], ["cosT"])
                op("dve", lambda e: e.tensor_tensor(out=x2, in0=x, in1=x, op=ALU.mult), ["cosT"], ["sinT"])
                op("dve", lambda e: e.tensor_scalar(out=x2, in0=x2, scalar1=0.044715, scalar2=1.0, op0=ALU.mult, op1=ALU.add), ["sinT"], ["sinT"])
                op("dve", lambda e: e.tensor_tensor(out=x2, in0=x2, in1=x, op=ALU.mult), ["sinT", "cosT"], ["sinT"])
                op("act", lambda e: e.activation(out=x2, in_=x2, func=AF.Sigmoid, scale=1.5957691216057308), ["sinT"], ["sinT"])
                op("dve", lambda e, ts=ts: e.tensor_tensor(out=yT[:, mc, ts], in0=x2, in1=x, op=ALU.mult), ["sinT", "cosT"], [("yT", mc, nt)])

        for mc in range(8 if stop3 > 4 else 1):
            mc_block(mc)
        if stop3 <= 4:
            self.PS.lo, self.PS.hi = 0, 8
            self.phase_end()
            return
        self.PS.lo, self.PS.hi = 0, 8
        YT = [("yT", mc, nt) for mc in range(8) for nt in range(4)]

        def outp(dc):
            b = dc % 2
            dma("pool", wo[b][:, :, 0:128], w_out[:, :, dc * 128:(dc + 1) * 128], ("wo", 0, 0), "wo0a")
            dma("pool", wo[b][:, :, 128:256], w_out[:, :, 1024 + dc * 128:1024 + (dc + 1) * 128], ("wo", 0, 1), "wo0b")
            for nt in range(4):
                ts = slice(nt * 512, (nt + 1) * 512)
                psv, ptv = self.PS.get()
                psg, ptg = self.PS.get()
                S.op("pe", [lambda e, kc=kc, psv=psv, ts=ts: e.matmul(psv[:, :], lhsT=wo[b][:, kc, 0:128], rhs=yT[:, kc, ts],
                                                                     start=(kc == 0), stop=(kc == KC - 1)) for kc in range(KC)],
                     reads=YT + [("wo", 0, 0)], writes=[ptv])
                S.op("pe", [lambda e, kc=kc, psg=psg, ts=ts: e.matmul(psg[:, :], lhsT=wo[b][:, kc, 128:256], rhs=yT[:, kc, ts],
                                                                     start=(kc == 0), stop=(kc == KC - 1)) for kc in range(KC)],
                     reads=YT + [("wo", 0, 1)], writes=[ptg])
                sg = cosT[:, 0:512]
                op("act", lambda e, psg=psg: e.activation(out=sg, in_=psg[:, :], func=AF.Sigmoid), [ptg], ["cosT"])
                op("dve", lambda e, psv=psv: e.tensor_tensor(out=sg, in0=psv[:, :], in1=sg, op=ALU.mult), [ptv, "cosT"], ["cosT"])
                op("dve", lambda e, ts=ts: e.tensor_tensor(out=hT[:, dc, ts], in0=hT[:, dc, ts], in1=sg, op=ALU.add),
                   ["cosT", ("hT", nt, dc)], [("hT", nt, dc)])

        for dc in range(KC):
            outp(dc)
        self.phase_end()

    def build(self):
        nc, S, stack = self.nc, self.S, self.stack
        d = {}
        xT = self.din("xT", [D, L])
        vecs_d = self.din("vecs", [128, NV])
        for name, shape in DRAM_INPUTS:
            d[name] = self.din(name, shape)
        yT = nc.dram_tensor("yT", [D, L], F32, kind="ExternalOutput").ap()

        self.PS = PsumPool(nc, stack)
        self.hT = hT = self.sb("hT", [128, KC, L], F32)
        self.hn = hn = self.sb("hn", [128, KC, L], BF16)
        self.sq = self.sb("sq", [128, KC, 512], BF16)
        self.rstd = self.sb("rstd", [128, 512], F32)
        self.ones_bf = self.sb("ones_bf", [128, 128], BF16)
        self.ones_f = self.sb("ones_f", [128, 128], F32)
        self.ident_bf = self.sb("ident_bf", [128, 128], BF16)
        self.ident_f = self.sb("ident_f", [128, 128], F32)
        self.U = self.sb("U", [128, 128], F32)
        self.SL = self.sb("SL", [128, 128], F32)
        self.eps_col = self.sb("eps_col", [128, 1], F32)
        self.vecs = vecs = self.sb("vecs", [128, NV], F32)

        S.op("dve", lambda e: e.memset(self.ones_bf[:, :], 1.0), writes=["ones"])
        S.op("dve", lambda e: e.memset(self.ones_f[:, :], 1.0), writes=["onesf"])
        S.op("dve", lambda e: e.memset(self.eps_col[:, :], EPS), writes=["eps"])
        S.op("pool", lambda e: e.affine_select(out=self.U[:, :], in_=self.ones_f[:, :], pattern=[[1, 128]], compare_op=ALU.is_ge,
                                               fill=0.0, base=0, channel_multiplier=-1), reads=["onesf"], writes=["U"])
        S.op("pool", lambda e: e.affine_select(out=self.SL[:, :], in_=self.ones_f[:, :], pattern=[[-1, 128]], compare_op=ALU.is_gt,
                                               fill=0.0, base=0, channel_multiplier=1), reads=["onesf"], writes=["SL"])
        S.op("pool", lambda e: e.affine_select(out=self.ident_f[:, :], in_=self.ones_f[:, :], pattern=[[1, 128]], compare_op=ALU.is_equal,
                                               fill=0.0, base=0, channel_multiplier=-1), reads=["onesf"], writes=["identf"])
        S.op("dve", lambda e: e.tensor_copy(out=self.ident_bf[:, :], in_=self.ident_f[:, :]), reads=["identf"], writes=["ident"])
        S.op("sp", lambda e: e.dma_start(out=vecs[:, :], in_=vecs_d[:, :]), writes=["vecs"], dsem="vecs")
        xv = xT.rearrange("(kc p) t -> p kc t", p=128)
        for nt in range(4):
            ts = slice(nt * 512, (nt + 1) * 512)
            S.op("sp", lambda e, ts=ts: e.dma_start(out=hT[:, :, ts], in_=xv[:, :, ts]),
                 writes=[("hT", nt, kc) for kc in range(KC)], dsem="x%d" % nt)
        S.flush()

        for st in self.stages:
            if st[0] == "mlp":
                self.mlp_phase(st[1], d)
            elif st[0] == "final":
                self.final_phase()
            elif st[0] == "mix":
                li = st[1]
                if li % 3 == 2:
                    self.mamba2_phase(li, d)
                elif li % 3 == 0:
                    self.gdn_phase(li, d)
                else:
                    self.s5_phase(li, d)

        S.barrier()
        yv = yT.rearrange("(kc p) t -> p kc t", p=128)
        outtok = []
        for nt in range(4):
            ts = slice(nt * 512, (nt + 1) * 512)
            S.op("sp", lambda e, ts=ts: e.dma_start(out=yv[:, :, ts], in_=hT[:, :, ts]),
                 writes=[("y", nt)], dsem="y%d" % nt)
            outtok.append(("y", nt))
        S.wait_tokens("sp", outtok)
        S.flush()
        self.stack.close()
        return nc


NV = 80 + 160 + 192
DRAM_INPUTS = [
    ("mlp_w1", [4, D, DFF]), ("mlp_w2", [4, DFF, D]),
    ("m2_w_in", [1, D, 6176]), ("m2_w_out", [1, 2048, D]), ("m2_rows", [128, 96 + 2048]),
    ("gdn_w_in", [2, D, 4112]), ("gdn_w_out", [2, D, D]), ("gdn_rows", [2, 128, 144]),
    ("s5_w_in", [1, D, D]), ("s5_w_out", [1, D, 2 * D]), ("s5_prm", [128, 201]),
    ("s5_bT", [128, 2, 64, 64]), ("s5_cpad", [128, 2, 64, 128]),
]


def make_host_inputs(inp):
    v = np.zeros((128, NV), np.float32)

    def colmajor(a):
        return np.ascontiguousarray(a.reshape(-1, 128).T)
    for l in range(4):
        v[:, l * 8:(l + 1) * 8] = colmajor(inp["norm_mix_g"][l])
        v[:, 32 + l * 8:32 + (l + 1) * 8] = colmajor(inp["norm_mlp_g"][l])
    v[:, 64:72] = colmajor(inp["final_norm_g"])
    cw = inp["m2_conv_w"][0]
    cb = inp["m2_conv_b"][0]
    for j in range(32):
        for tap in range(4):
            v[:, 80 + j * 5 + tap] = cw[tap, j * 128:(j + 1) * 128]
        v[:, 80 + j * 5 + 4] = cb[j * 128:(j + 1) * 128]
    m2_rows = np.concatenate([inp["m2_dt_bias"][0], inp["m2_a_log"][0], inp["m2_d"][0], inp["m2_norm_g"][0]])[None, :]
    m2_rows = np.ascontiguousarray(np.broadcast_to(m2_rows, (128, m2_rows.shape[1]))).astype(np.float32)
    gr = []
    for jl in range(2):
        gcw = inp["gdn_conv_w"][jl]
        for jj in range(24):
            for tap in range(4):
                v[:, 240 + jl * 96 + jj * 4 + tap] = gcw[tap, jj * 128:(jj + 1) * 128]
        r = np.concatenate([inp["gdn_dt_bias"][jl], inp["gdn_a_log"][jl], inp["gdn_o_norm_g"][jl]])[None, :]
        gr.append(np.broadcast_to(r, (128, 144)))
    gdn_rows = np.ascontiguousarray(np.stack(gr, 0)).astype(np.float32)
    lre, lim, ldt = inp["s5_lam_re"][0], inp["s5_lam_im"][0], inp["s5_log_dt"][0]
    prm = np.zeros((128, 201), np.float32)
    prm[:, 0:64] = np.concatenate([lre.T, lre.T], 0)
    prm[:, 64:128] = np.concatenate([lim.T, lim.T], 0)
    prm[:, 128:192] = np.broadcast_to(ldt[None, :], (128, 64))
    prm[0:64, 192] = 1.0
    prm[64:128, 192] = -1.0
    prm[:, 193:201] = inp["s5_d"][0].reshape(8, 128).T
    bre, bim = inp["s5_b_re"][0], inp["s5_b_im"][0]
    cre, cim = inp["s5_c_re"][0], inp["s5_c_im"][0]
    bT = np.zeros((128, 2, 64, 64), np.float32)
    cpad = np.zeros((128, 2, 64, 128), np.float32)
    for g in range(64):
        gl = g % 8
        bT[16 * gl:16 * gl + 16, 0, g, :] = bre[g].T
        bT[16 * gl:16 * gl + 16, 1, g, :] = bim[g].T
        cpad[0:64, 0, g, 16 * gl:16 * gl + 16] = cre[g].T
        cpad[64:128, 0, g, 16 * gl:16 * gl + 16] = cim[g].T
        cpad[0:64, 1, g, 16 * gl:16 * gl + 16] = cim[g].T
        cpad[64:128, 1, g, 16 * gl:16 * gl + 16] = cre[g].T
    out = {"vecs": v, "m2_rows": m2_rows, "gdn_rows": gdn_rows, "s5_prm": prm, "s5_bT": bT, "s5_cpad": cpad}
    for k in ("s5_w_in", "s5_w_out"):
        out[k] = np.ascontiguousarray(inp[k], dtype=np.float32)
    for k in ("mlp_w1", "mlp_w2", "m2_w_in", "m2_w_out", "gdn_w_in", "gdn_w_out"):
        out[k] = np.ascontiguousarray(inp[k], dtype=np.float32)
    return out


ALL_STAGES = [("mix", 0), ("mlp", 0), ("mix", 1), ("mlp", 1), ("mix", 2), ("mlp", 2), ("mix", 3), ("mlp", 3), ("final",)]


def run_stages(stages, xT_list, inp, trace=False):
    prog = Prog(stages)
    nc = prog.build()
    host = make_host_inputs(inp)
    n = len(xT_list)
    in_maps = []
    for c in range(n):
        m = dict(host)
        m["xT"] = np.ascontiguousarray(xT_list[c], dtype=np.float32)
        in_maps.append(m)
    res = run_bass_kernel_spmd(nc, in_maps, core_ids=list(range(n)), trace=trace)
    return [r["yT"] for r in res.results], res


def kernel(**inputs):
    x = inputs["x"]
    xT_list = [np.ascontiguousarray(x[b].T) for b in range(x.shape[0])]
    import os
    stages = ALL_STAGES
    if os.environ.get("KSTAGES"):
        stages = [tuple(t) for t in json.loads(os.environ["KSTAGES"])]
    outs, _ = run_stages(stages, xT_list, inputs)
    return np.stack([np.ascontiguousarray(o.T) for o in outs], axis=0).astype(np.float32)
```
